# Optimizing a Trainium2 kernel written in Bass

```python
import math
import jax, jax.numpy as jnp
from jax import lax
import numpy as np

D_MODEL = 1024
BATCH = 16
SEQ = 2048
DEPTH = 1
DEC_BATCH = 16
DEC_SEQ = 4096
PAST_LEN = 128

N_MEM = 256
S5_WIDTH = D_MODEL
S5_GROUP = 16
S5_GROUPS = S5_WIDTH // S5_GROUP
S5_STATE = 64
S5_CHUNK = 128
DT_MIN = 1e-3
DT_MAX = 1e-1
DIFF_HEADS = 8
DIFF_DH = D_MODEL // DIFF_HEADS // 2
DIFF_WIDTH = DIFF_HEADS * 2 * DIFF_DH
Q_BLOCK = 128
ROPE_THETA = 10000.0
MEM_HEADS = 4
MEM_DH = D_MODEL // MEM_HEADS
MEM_WIDTH = MEM_HEADS * MEM_DH
N_BRANCH = 3
D_FF = -(-8 * D_MODEL // (3 * 256)) * 256
IN_COLS = S5_WIDTH + 3 * DIFF_WIDTH + MEM_WIDTH + N_BRANCH * D_MODEL
SPLITS = [S5_WIDTH, S5_WIDTH + DIFF_WIDTH, S5_WIDTH + 2 * DIFF_WIDTH, S5_WIDTH + 3 * DIFF_WIDTH, S5_WIDTH + 3 * DIFF_WIDTH + MEM_WIDTH]
EPS = 1e-6

kernel_name = "hybrid_s5_diffattn_memxattn_encoder"

F32 = jnp.float32


def rmsnorm(x, g):
    xf = x.astype(F32)
    y = xf * lax.rsqrt(jnp.mean(xf * xf, axis=-1, keepdims=True) + EPS)
    return (y * g.astype(F32)).astype(x.dtype)


def rope(x):
    L, dh = x.shape[1], x.shape[-1]
    half = dh // 2
    inv = ROPE_THETA ** (-jnp.arange(half, dtype=F32) / half)
    ang = jnp.arange(L, dtype=F32)[:, None] * inv[None, :]
    cos = jnp.cos(ang)[None, :, None, :]
    sin = jnp.sin(ang)[None, :, None, :]
    xf = x.astype(F32)
    x1, x2 = xf[..., :half], xf[..., half:]
    return jnp.concatenate([x1 * cos - x2 * sin, x1 * sin + x2 * cos], axis=-1).astype(x.dtype)


def _lin_op(e1, e2):
    a1, b1 = e1
    a2, b2 = e2
    return a1 * a2, a2 * b1 + b2


def s5_direction(u, lam_re, lam_im, log_dt, b_re, b_im, c_re, c_im):
    Bsz, L, G, H = u.shape
    lam = lax.complex(lam_re.astype(F32), lam_im.astype(F32))
    dt = jnp.exp(log_dt.astype(F32))[:, None]
    lam_bar = jnp.exp(lam * dt)
    b_bar = ((lam_bar - 1.0) / lam)[..., None] * lax.complex(b_re.astype(F32), b_im.astype(F32))
    c = lax.complex(c_re.astype(F32), c_im.astype(F32))
    n_chunks = L // S5_CHUNK
    uc = u.reshape(Bsz, n_chunks, S5_CHUNK, G, H).transpose(1, 0, 2, 3, 4)

    def chunk_step(state, u_chunk):
        bu = jnp.einsum('bcgh,gph->bcgp', u_chunk.astype(jnp.complex64), b_bar)
        a = jnp.broadcast_to(lam_bar, bu.shape)
        a_cum, s = lax.associative_scan(_lin_op, (a, bu), axis=1)
        s = s + a_cum * state[:, None]
        y = jnp.einsum('bcgp,ghp->bcgh', s, c).real
        return s[:, -1], y

    state0 = jnp.zeros((Bsz, G, S5_STATE), jnp.complex64)
    _, ys = lax.scan(chunk_step, state0, uc)
    return ys.transpose(1, 0, 2, 3, 4).reshape(Bsz, L, G, H)


def diff_attention(q, k, v, q_g, k_g, lq1, lk1, lq2, lk2, sub_g, lambda_init):
    Bsz, L = q.shape[0], q.shape[1]
    q = rope(rmsnorm(q.reshape(Bsz, L, 2 * DIFF_HEADS, DIFF_DH), q_g))
    k = rope(rmsnorm(k.reshape(Bsz, L, 2 * DIFF_HEADS, DIFF_DH), k_g))
    v = v.reshape(Bsz, L, DIFF_HEADS, 2 * DIFF_DH)
    lam = (jnp.exp(jnp.sum(lq1.astype(F32) * lk1.astype(F32)))
           - jnp.exp(jnp.sum(lq2.astype(F32) * lk2.astype(F32))) + lambda_init)
    scale = DIFF_DH ** -0.5
    qb = q.reshape(Bsz, L // Q_BLOCK, Q_BLOCK, 2 * DIFF_HEADS, DIFF_DH).transpose(1, 0, 2, 3, 4)

    def block(q_blk):
        s = jnp.einsum('bqhd,bkhd->bhqk', q_blk, k, preferred_element_type=F32) * scale
        p = jax.nn.softmax(s, axis=-1).reshape(Bsz, DIFF_HEADS, 2, Q_BLOCK, L)
        a = p[:, :, 0] - lam * p[:, :, 1]
        return jnp.einsum('bhqk,bkhe->bqhe', a.astype(v.dtype), v)

    o = lax.map(block, qb)
    o = o.transpose(1, 0, 2, 3, 4).reshape(Bsz, L, DIFF_HEADS, 2 * DIFF_DH)
    o = rmsnorm(o, sub_g) * (1.0 - lambda_init)
    return o.reshape(Bsz, L, DIFF_WIDTH)


def memory_attention(q, mem_n, w_kv, q_g, k_g):
    Bsz, L = q.shape[0], q.shape[1]
    M = mem_n.shape[1]
    kv = mem_n @ w_kv
    k, v = jnp.split(kv, 2, axis=-1)
    q = rmsnorm(q.reshape(Bsz, L, MEM_HEADS, MEM_DH), q_g)
    k = rmsnorm(k.reshape(Bsz, M, MEM_HEADS, MEM_DH), k_g)
    v = v.reshape(Bsz, M, MEM_HEADS, MEM_DH)
    s = jnp.einsum('bqhd,bmhd->bhqm', q, k, preferred_element_type=F32) * (MEM_DH ** -0.5)
    p = jax.nn.softmax(s, axis=-1)
    o = jnp.einsum('bhqm,bmhe->bqhe', p.astype(v.dtype), v)
    return o.reshape(Bsz, L, MEM_WIDTH)


def layer(x, mem, li, norm_mix_g, w_in, b_gate,
          s5_lam_re, s5_lam_im, s5_log_dt, s5_b_re, s5_b_im, s5_c_re, s5_c_im, s5_d, s5_w_glu,
          diff_q_g, diff_k_g, diff_lq1, diff_lk1, diff_lq2, diff_lk2, diff_sub_g,
          mem_norm_g, w_mem_kv, mem_q_g, mem_k_g,
          w_branch, w_out, ffn_norm_g, w_gate_up, w_down):
    Bsz, L, _ = x.shape
    h = rmsnorm(x, norm_mix_g[li])
    proj = h @ w_in[li]
    u, q, k, v, qm, gl = jnp.split(proj, SPLITS, axis=-1)

    uf = u.astype(F32)
    ug = uf.reshape(Bsz, L, S5_GROUPS, S5_GROUP)
    y_f = s5_direction(ug, s5_lam_re[li, 0], s5_lam_im[li, 0], s5_log_dt[li, 0],
                       s5_b_re[li, 0], s5_b_im[li, 0], s5_c_re[li, 0], s5_c_im[li, 0])
    y_b = jnp.flip(s5_direction(jnp.flip(ug, axis=1), s5_lam_re[li, 1], s5_lam_im[li, 1], s5_log_dt[li, 1],
                                s5_b_re[li, 1], s5_b_im[li, 1], s5_c_re[li, 1], s5_c_im[li, 1]), axis=1)
    y = (y_f + y_b).reshape(Bsz, L, S5_WIDTH) + s5_d[li].astype(F32) * uf
    z = jax.nn.gelu(y)
    s5_out = (z * jax.nn.sigmoid(z @ s5_w_glu[li].astype(F32))).astype(x.dtype)

    lambda_init = 0.8 - 0.6 * math.exp(-0.3 * li)
    diff_out = diff_attention(q, k, v, diff_q_g[li], diff_k_g[li], diff_lq1[li], diff_lk1[li],
                              diff_lq2[li], diff_lk2[li], diff_sub_g[li], lambda_init)

    mem_out = memory_attention(qm, rmsnorm(mem, mem_norm_g[li]), w_mem_kv[li], mem_q_g[li], mem_k_g[li])

    branches = jnp.stack([s5_out, diff_out.astype(x.dtype), mem_out.astype(x.dtype)], axis=2)
    br = jnp.einsum('blnc,ncd->blnd', branches, w_branch[li])
    gates = jax.nn.sigmoid((gl + b_gate[li]).astype(F32)).reshape(Bsz, L, N_BRANCH, D_MODEL)
    merged = jnp.sum(gates * br.astype(F32), axis=2).astype(x.dtype)
    x = x + merged @ w_out[li]

    h2 = rmsnorm(x, ffn_norm_g[li])
    g, up = jnp.split(h2 @ w_gate_up[li], 2, axis=-1)
    x = x + (jax.nn.silu(g) * up) @ w_down[li]
    return x


def setup_inputs(seed: int = 0) -> dict:
    key = jax.random.key(seed)
    ks = jax.random.split(key, 40)
    nrm = lambda k, shape, s: jax.random.normal(k, shape, F32) * s
    gain = lambda k, shape: 1.0 + 0.02 * jax.random.normal(k, shape, F32)
    G, P, H = S5_GROUPS, S5_STATE, S5_GROUP
    n_idx = jnp.arange(P, dtype=F32)
    lam_re = -0.5 + 0.01 * jax.random.normal(ks[5], (DEPTH, 2, G, P), F32)
    lam_im = math.pi * n_idx + 0.01 * jax.random.normal(ks[6], (DEPTH, 2, G, P), F32)
    log_dt = jax.random.uniform(ks[7], (DEPTH, 2, G), F32, math.log(DT_MIN), math.log(DT_MAX))
    return {
        "x_prompt": nrm(ks[0], (BATCH, SEQ, D_MODEL), 1.0),
        "x_sample": nrm(ks[1], (DEC_BATCH, DEC_SEQ, D_MODEL), 1.0),
        "mem_prompt": nrm(ks[2], (BATCH, N_MEM, D_MODEL), 1.0),
        "mem_sample": nrm(ks[3], (DEC_BATCH, N_MEM, D_MODEL), 1.0),
        "norm_mix_g": gain(ks[4], (DEPTH, D_MODEL)),
        "w_in": nrm(ks[8], (DEPTH, D_MODEL, IN_COLS), D_MODEL ** -0.5),
        "b_gate": nrm(ks[9], (DEPTH, N_BRANCH * D_MODEL), 0.02),
        "s5_lam_re": lam_re,
        "s5_lam_im": lam_im,
        "s5_log_dt": log_dt,
        "s5_b_re": nrm(ks[10], (DEPTH, 2, G, P, H), (2 * H) ** -0.5),
        "s5_b_im": nrm(ks[11], (DEPTH, 2, G, P, H), (2 * H) ** -0.5),
        "s5_c_re": nrm(ks[12], (DEPTH, 2, G, H, P), (2 * P) ** -0.5),
        "s5_c_im": nrm(ks[13], (DEPTH, 2, G, H, P), (2 * P) ** -0.5),
        "s5_d": nrm(ks[14], (DEPTH, S5_WIDTH), 1.0),
        "s5_w_glu": nrm(ks[15], (DEPTH, S5_WIDTH, S5_WIDTH), S5_WIDTH ** -0.5),
        "diff_q_g": gain(ks[16], (DEPTH, DIFF_DH)),
        "diff_k_g": gain(ks[17], (DEPTH, DIFF_DH)),
        "diff_lq1": nrm(ks[18], (DEPTH, DIFF_DH), 0.1),
        "diff_lk1": nrm(ks[19], (DEPTH, DIFF_DH), 0.1),
        "diff_lq2": nrm(ks[20], (DEPTH, DIFF_DH), 0.1),
        "diff_lk2": nrm(ks[21], (DEPTH, DIFF_DH), 0.1),
        "diff_sub_g": gain(ks[22], (DEPTH, 2 * DIFF_DH)),
        "mem_norm_g": gain(ks[23], (DEPTH, D_MODEL)),
        "w_mem_kv": nrm(ks[24], (DEPTH, D_MODEL, 2 * MEM_WIDTH), D_MODEL ** -0.5),
        "mem_q_g": gain(ks[25], (DEPTH, MEM_DH)),
        "mem_k_g": gain(ks[26], (DEPTH, MEM_DH)),
        "w_branch": nrm(ks[27], (DEPTH, N_BRANCH, D_MODEL, D_MODEL), D_MODEL ** -0.5),
        "w_out": nrm(ks[28], (DEPTH, D_MODEL, D_MODEL), D_MODEL ** -0.5),
        "ffn_norm_g": gain(ks[29], (DEPTH, D_MODEL)),
        "w_gate_up": nrm(ks[30], (DEPTH, D_MODEL, 2 * D_FF), D_MODEL ** -0.5),
        "w_down": nrm(ks[31], (DEPTH, D_FF, D_MODEL), D_FF ** -0.5),
    }


def reference(x_prompt, x_sample, mem_prompt, mem_sample, norm_mix_g, w_in, b_gate,
              s5_lam_re, s5_lam_im, s5_log_dt, s5_b_re, s5_b_im, s5_c_re, s5_c_im, s5_d, s5_w_glu,
              diff_q_g, diff_k_g, diff_lq1, diff_lk1, diff_lq2, diff_lk2, diff_sub_g,
              mem_norm_g, w_mem_kv, mem_q_g, mem_k_g,
              w_branch, w_out, ffn_norm_g, w_gate_up, w_down):
    y_prompt = x_prompt
    y_sample = x_sample
    for li in range(DEPTH):
        y_prompt = layer(y_prompt, mem_prompt, li, norm_mix_g, w_in, b_gate,
                         s5_lam_re, s5_lam_im, s5_log_dt, s5_b_re, s5_b_im, s5_c_re, s5_c_im, s5_d, s5_w_glu,
                         diff_q_g, diff_k_g, diff_lq1, diff_lk1, diff_lq2, diff_lk2, diff_sub_g,
                         mem_norm_g, w_mem_kv, mem_q_g, mem_k_g,
                         w_branch, w_out, ffn_norm_g, w_gate_up, w_down)
        y_sample = layer(y_sample, mem_sample, li, norm_mix_g, w_in, b_gate,
                         s5_lam_re, s5_lam_im, s5_log_dt, s5_b_re, s5_b_im, s5_c_re, s5_c_im, s5_d, s5_w_glu,
                         diff_q_g, diff_k_g, diff_lq1, diff_lk1, diff_lq2, diff_lk2, diff_sub_g,
                         mem_norm_g, w_mem_kv, mem_q_g, mem_k_g,
                         w_branch, w_out, ffn_norm_g, w_gate_up, w_down)
    return (y_prompt, y_sample)
```

```python
import math
import numpy as np
import ml_dtypes
import concourse.bass as bass
import concourse.mybir as mybir
from concourse.bass_utils import run_bass_kernel_spmd

F32 = mybir.dt.float32
BF = mybir.dt.bfloat16
I32 = mybir.dt.int32
AF = mybir.ActivationFunctionType
OP = mybir.AluOpType

D = 1024
NMEM = 256
EPS = 1e-6
LAMBDA_INIT = 0.8 - 0.6 * math.exp(-0.3 * 0)
TWO_PI = 2.0 * math.pi
KDMA = 8


class Tok:
    __slots__ = ("w", "r", "dr")

    def __init__(self):
        self.w = None
        self.r = {}
        self.dr = []


class Prog:
    def __init__(self):
        self.ops = []
        self.last = {}
        self.dma_since = []

    def op(self, eng, fn, reads=(), writes=(), dma=False):
        deps = set()
        for b in reads:
            if b.w is not None:
                deps.add(b.w)
        for b in writes:
            if b.w is not None:
                deps.add(b.w)
            deps.update(b.r.values())
            deps.update(b.dr)
        i = len(self.ops)
        self.ops.append((eng, fn, deps, dma))
        for b in reads:
            if dma:
                b.dr.append(i)
            else:
                b.r[eng] = i
        for b in writes:
            b.w = i
            b.r = {}
            b.dr = []
        self.last[eng] = i
        if dma:
            self.dma_since.append(i)
        return i

    def barrier(self):
        deps = set(self.last.values()) | set(self.dma_since)
        for eng in ("pe", "act", "dve", "pool", "sp"):
            self.ops.append((eng, None, set(deps), False))
        self.dma_since = []

    def prepare(self, nc):
        self.csem = {e: nc.semaphore("cs_" + e).__enter__() for e in ("pe", "act", "dve", "pool")}
        self.dsem = [nc.semaphore("ds%d" % i).__enter__() for i in range(KDMA)]

    def emit(self, nc, block):
        ops = self.ops
        n = len(ops)
        needed = [False] * n
        for j in range(n):
            ej = ops[j][0]
            for k in ops[j][2]:
                if ops[k][0] == "pe" and ej == "pe" and not ops[k][3]:
                    continue
                needed[k] = True
        csem, dsem = self.csem, self.dsem
        semof = [None] * n
        cnt = {e: 0 for e in csem}
        dcount = 0
        dma_prev = {}
        for j in range(n):
            eng, fn, deps, dma = ops[j]
            if fn is None:
                continue
            if dma:
                s = dsem[dcount % KDMA]
                semof[j] = (s, 16 * (dcount // KDMA + 1), dcount)
                dcount += 1
            elif needed[j]:
                cnt[eng] += 1
                semof[j] = (csem[eng], cnt[eng], None)
        streams = {e: [] for e in ("pe", "act", "dve", "pool", "sp")}
        for j in range(n):
            streams[ops[j][0]].append(j)
        final_dma = {}
        for j in range(n):
            if ops[j][3] and ops[j][1] is not None:
                s, v, _ = semof[j]
                final_dma[id(s)] = (s, v)

        def run(engname, e):
            waited = {}
            for j in streams[engname]:
                eng, fn, deps, dma = ops[j]
                need = {}
                for k in deps:
                    if ops[k][1] is None:
                        continue
                    if ops[k][0] == "pe" and eng == "pe" and not ops[k][3]:
                        continue
                    sk = semof[k]
                    if sk is None:
                        continue
                    s, v, _ = sk
                    if need.get(id(s), (None, 0))[1] < v:
                        need[id(s)] = (s, v)
                if dma and fn is not None:
                    s, v, idx = semof[j]
                    if idx >= KDMA:
                        pv = v - 16
                        if need.get(id(s), (None, 0))[1] < pv:
                            need[id(s)] = (s, pv)
                for sid, (s, v) in need.items():
                    if waited.get(sid, 0) < v:
                        e.wait_ge(s, v)
                        waited[sid] = v
                if fn is None:
                    continue
                inst = fn(e)
                if semof[j] is not None:
                    inst.then_inc(semof[j][0], 16 if dma else 1)
            if engname == "sp":
                for sid, (s, v) in final_dma.items():
                    if waited.get(sid, 0) < v:
                        e.wait_ge(s, v)

        @block.tensor
        def _(e):
            run("pe", e)

        @block.scalar
        def _(e):
            run("act", e)

        @block.vector
        def _(e):
            run("dve", e)

        @block.gpsimd
        def _(e):
            run("pool", e)

        @block.sync
        def _(e):
            run("sp", e)


class Buf:
    def __init__(self, ap):
        self.ap = ap
        self.t = Tok()


def build(seqLs, STOP_AFTER=99, DEBUG=False):
    NS = len(seqLs)
    NTOK = sum(seqLs)
    LMAX = max(seqLs)
    nc = bass.Bass("TRN2", target_bir_lowering=False)

    def dram(name, shape, dtype, kind):
        return nc.dram_tensor(name, shape, dtype, kind=kind).ap()

    x_d = dram("x", [NTOK, D], F32, "ExternalInput")
    mem_d = dram("mem", [NS * NMEM, D], F32, "ExternalInput")
    y_d = dram("y", [NTOK, D], F32, "ExternalOutput")
    w_in_d = dram("w_in", [D, 8192], F32, "ExternalInput")
    w_glu_d = dram("w_glu", [D, D], F32, "ExternalInput")
    w_br_d = dram("w_br", [3 * D, D], F32, "ExternalInput")
    w_out_d = dram("w_out", [D, D], F32, "ExternalInput")
    w_gu_d = dram("w_gu", [D, 5632], F32, "ExternalInput")
    w_down_d = dram("w_down", [2816, D], F32, "ExternalInput")
    w_kv_d = dram("w_kv", [D, 2048], F32, "ExternalInput")
    smalls_d = dram("smalls", [128, 320], F32, "ExternalInput")
    s5p_d = dram("s5p", [128, 64 * 67], F32, "ExternalInput")
    cbf_d = dram("cbf", [128, 512], BF, "ExternalInput")
    ropec_d = dram("ropec", [128, 4096], F32, "ExternalInput")
    ropes_d = dram("ropes", [128, 4096], F32, "ExternalInput")

    WAin = dram("WAin", [64, 128, 1024], BF, "Internal")
    WAglu = dram("WAglu", [8, 128, 1024], BF, "Internal")
    WAbr = dram("WAbr", [24, 128, 1024], BF, "Internal")
    WAgu = dram("WAgu", [44, 128, 1024], BF, "Internal")
    WAkv = dram("WAkv", [8, 128, 1024], BF, "Internal")
    WBv = dram("WBv", [D, D], BF, "Internal")
    WBout = dram("WBout", [D, D], BF, "Internal")
    WBdown = dram("WBdown", [2816, D], BF, "Internal")
    WBkv = dram("WBkv", [D, D], BF, "Internal")
    hT_s = dram("hT_s", [D, LMAX], BF, "ExternalOutput" if DEBUG else "Internal")
    uT_s = dram("uT_s", [D, LMAX], BF, "ExternalOutput" if DEBUG else "Internal")
    zT_s = dram("zT_s", [D, LMAX], BF, "ExternalOutput" if DEBUG else "Internal")
    s5T_s = dram("s5T_s", [D, LMAX], BF, "ExternalOutput" if DEBUG else "Internal")
    KT_s = dram("KT_s", [D, LMAX], BF, "ExternalOutput" if DEBUG else "Internal")
    V_s = dram("V_s", [LMAX, D], BF, "ExternalOutput" if DEBUG else "Internal")
    s5w_s = dram("s5w_s", [128, 7936], F32, "ExternalOutput" if DEBUG else "Internal")

    P = Prog()
    ARENA = 45000
    arena_cm = nc.sbuf_tensor("arena", [128, ARENA], F32)
    arena = arena_cm.__enter__()
    ps_cms = [nc.psum_tensor("ps%d" % i, [128, 512], F32) for i in range(8)]
    psb = [Buf(c.__enter__()[:, :]) for c in ps_cms]
    astate = {"off": 0}

    def alloc(words, dtype=F32, shape=None):
        o = astate["off"]
        assert o + words <= ARENA, ("arena overflow", o, words)
        astate["off"] = o + words
        ap = arena[:, o:o + words]
        if dtype != F32:
            ap = ap.bitcast(dtype)
        return Buf(ap)

    def abf(cols):
        return alloc((cols + 1) // 2, BF)

    smalls = alloc(320)
    cbf = abf(512)
    P.op("sp", lambda e: e.dma_start(out=smalls.ap, in_=smalls_d[:, :]), [], [smalls.t], dma=True)
    P.op("sp", lambda e: e.dma_start(out=cbf.ap, in_=cbf_d[:, :]), [], [cbf.t], dma=True)
    ident = cbf.ap[:, 0:128]
    rotm = cbf.ap[:, 128:256]
    ones64 = cbf.ap[:, 256:384]
    ones128 = cbf.ap[:, 384:512]
    G_MIX, G_FFN, G_MEM, B_GATE, S5D = 0, 8, 16, 24, 48
    QG, KG, SUBG, MQG, MKG, LQ = 56, 57, 58, 59, 61, 63
    sm = smalls.ap
    epsb = alloc(2)
    P.op("dve", lambda e: e.memset(epsb.ap[:, 0:1], EPS), [], [epsb.t])
    lamb = alloc(8)

    def emit_lambda():
        a = lamb.ap
        P.op("dve", lambda e: e.memset(a[:, 0:8], 0.0), [], [lamb.t])
        tmp = alloc(64)
        for i in range(2):
            P.op("dve", lambda e, i=i: e.tensor_tensor(out=tmp.ap, in0=sm[:, LQ + 128 * i:LQ + 128 * i + 64],
                                                    in1=sm[:, LQ + 128 * i + 64:LQ + 128 * i + 128], op=OP.mult),
                 [smalls.t, lamb.t], [tmp.t])
            P.op("dve", lambda e, i=i: e.tensor_reduce(out=a[:, 1 + i:2 + i], in_=tmp.ap, axis=mybir.AxisListType.X, op=OP.add),
                 [tmp.t], [lamb.t])
        P.op("act", lambda e: e.activation(out=a[:, 3:5], in_=a[:, 1:3], func=AF.Exp), [lamb.t], [lamb.t])
        P.op("dve", lambda e: e.tensor_tensor(out=a[:, 5:6], in0=a[:, 4:5], in1=a[:, 3:4], op=OP.subtract), [lamb.t], [lamb.t])
        P.op("dve", lambda e: e.tensor_scalar(a[:, 0:1], a[:, 5:6], -LAMBDA_INIT, None, OP.add), [lamb.t], [lamb.t])

    emit_lambda()
    NEGLAM = lamb.ap[:, 0:1]

    persist_mark = astate["off"]

    stg = [alloc(2048) for _ in range(2)]
    stb = [abf(2048) for _ in range(2)]
    cast_i = [0]

    def cast_weight(src, nk, ncols, dstA=None, dstB=None, gcol=None, colsA=None, colsB=None):
        for kc in range(nk):
            for c0 in range(0, ncols, 2048):
                cw = min(2048, ncols - c0)
                i = cast_i[0]
                cast_i[0] += 1
                sg, sb = stg[i % 2], stb[i % 2]
                P.op("sp", lambda e, sg=sg, kc=kc, c0=c0, cw=cw: e.dma_start(out=sg.ap[:, 0:cw], in_=src[kc * 128:(kc + 1) * 128, c0:c0 + cw]),
                     [], [sg.t], dma=True)
                if gcol is not None:
                    gap = sm[:, gcol + kc:gcol + kc + 1]
                    if i % 2 == 0:
                        P.op("act", lambda e, sg=sg, sb=sb, cw=cw, gap=gap: e.activation(out=sb.ap[:, 0:cw], in_=sg.ap[:, 0:cw], func=AF.Copy, scale=gap),
                             [sg.t, smalls.t], [sb.t])
                    else:
                        P.op("dve", lambda e, sg=sg, sb=sb, cw=cw, gap=gap: e.tensor_scalar(sb.ap[:, 0:cw], sg.ap[:, 0:cw], gap, None, OP.mult),
                             [sg.t, smalls.t], [sb.t])
                else:
                    if i % 2 == 0:
                        P.op("act", lambda e, sg=sg, sb=sb, cw=cw: e.copy(out=sb.ap[:, 0:cw], in_=sg.ap[:, 0:cw]), [sg.t], [sb.t])
                    else:
                        P.op("dve", lambda e, sg=sg, sb=sb, cw=cw: e.tensor_copy(out=sb.ap[:, 0:cw], in_=sg.ap[:, 0:cw]), [sg.t], [sb.t])
                for (lo, hi, dA, mb0) in (colsA or []):
                    a0, a1 = max(lo, c0), min(hi, c0 + cw)
                    if a0 >= a1:
                        continue
                    dview = dA.rearrange("mb p (kc m) -> p mb kc m", kc=8)[:, mb0 + (a0 - lo) // 128:mb0 + (a1 - lo) // 128, kc, :]
                    sview = sb.ap[:, a0 - c0:a1 - c0].rearrange("p (mb m) -> p mb m", m=128)
                    P.op("sp", lambda e, dview=dview, sview=sview: e.dma_start(out=dview, in_=sview), [sb.t], [], dma=True)
                for (lo, hi, dB) in (colsB or []):
                    a0, a1 = max(lo, c0), min(hi, c0 + cw)
                    if a0 >= a1:
                        continue
                    P.op("sp", lambda e, dB=dB, kc=kc, a0=a0, a1=a1, lo=lo, sb=sb, c0=c0: e.dma_start(
                        out=dB[kc * 128:(kc + 1) * 128, a0 - lo:a1 - lo], in_=sb.ap[:, a0 - c0:a1 - c0]), [sb.t], [], dma=True)

    cast_weight(w_in_d, 8, 8192, gcol=G_MIX, colsA=[(0, 8192, WAin, 0)], colsB=[(3072, 4096, WBv)])
    cast_weight(w_glu_d, 8, 1024, colsA=[(0, 1024, WAglu, 0)])
    for nb in range(3):
        cast_weight(w_br_d[nb * 1024:(nb + 1) * 1024, :], 8, 1024, colsA=[(0, 1024, WAbr, 8 * nb)])
    cast_weight(w_out_d, 8, 1024, colsB=[(0, 1024, WBout)])
    cast_weight(w_gu_d, 8, 5632, gcol=G_FFN, colsA=[(0, 5632, WAgu, 0)])
    cast_weight(w_down_d, 22, 1024, colsB=[(0, 1024, WBdown)])
    cast_weight(w_kv_d, 8, 2048, gcol=G_MEM, colsA=[(0, 1024, WAkv, 0)], colsB=[(1024, 2048, WBkv)])

    s5raw = alloc(64 * 67)
    P.op("sp", lambda e: e.dma_start(out=s5raw.ap, in_=s5p_d[:, :]), [], [s5raw.t], dma=True)
    raw = s5raw.ap
    LRE = raw[:, 0:64]
    LIM = raw[:, 64:128]
    LDT = raw[:, 128:192]
    Braw = raw[:, 192:192 + 2048].rearrange("p (q r h) -> p q r h", q=64, r=2)
    Craw = raw[:, 2240:2240 + 2048].rearrange("p (q r h) -> p q r h", q=64, r=2)
    S5WORDS = 2304 + 2048 + 2048 + 1536

    def s5_views(buf):
        ap = buf.ap
        pwv = ap[:, 0:2304].rearrange("p (q k c) -> p q k c", q=64, k=12)
        cpv = ap[:, 2304:4352].bitcast(BF).rearrange("p (q r c) -> p q r c", q=64, r=2)
        w1v = ap[:, 4352:6400].bitcast(BF).rearrange("p (d j r m) -> p d j r m", d=2, j=8, r=2)
        pav = ap[:, 6400:7936].rearrange("p (q k c) -> p q k c", q=64, k=8)
        return pwv, cpv, w1v, pav

    S5W = alloc(S5WORDS)
    PWv, CPv, W1v, PAv = s5_views(S5W)
    CPflat = S5W.ap[:, 2304:4352].bitcast(BF)
    BPb = abf(64 * 2 * 32)
    BPv = BPb.ap.rearrange("p (q r c) -> p q r c", q=64, r=2)
    s5_mark = astate["off"]
    tl = [alloc(64) for _ in range(16)]
    tint = alloc(64, I32)
    T = [t.ap for t in tl]
    s5t = Tok()

    def dv(fn):
        P.op("dve", fn, [s5raw.t, s5t], [s5t])

    def ac(fn):
        P.op("act", fn, [s5raw.t, s5t], [s5t])

    ac(lambda e: e.activation(out=T[0], in_=LDT, func=AF.Exp))
    dv(lambda e: e.tensor_tensor(out=T[1], in0=LRE, in1=T[0], op=OP.mult))
    ac(lambda e: e.activation(out=T[1], in_=T[1], func=AF.Exp))
    dv(lambda e: e.tensor_tensor(out=T[2], in0=LIM, in1=T[0], op=OP.mult))

    def sin_of(dst, src, shift):
        dv(lambda e: e.tensor_scalar(T[10], src, shift, 1.0 / TWO_PI, OP.add, OP.mult))
        dv(lambda e: e.tensor_copy(out=tint.ap, in_=T[10]))
        dv(lambda e: e.tensor_copy(out=T[10], in_=tint.ap))
        dv(lambda e: e.tensor_scalar(T[11], src, shift, None, OP.add))
        dv(lambda e: e.scalar_tensor_tensor(out=T[11], in0=T[10], scalar=-TWO_PI, in1=T[11], op0=OP.mult, op1=OP.add))
        dv(lambda e: e.tensor_scalar(T[12], T[11], math.pi, -TWO_PI, OP.is_gt, OP.mult))
        dv(lambda e: e.tensor_tensor(out=T[11], in0=T[11], in1=T[12], op=OP.add))
        dv(lambda e: e.tensor_scalar(T[12], T[11], -math.pi, TWO_PI, OP.is_lt, OP.mult))
        dv(lambda e: e.tensor_tensor(out=T[11], in0=T[11], in1=T[12], op=OP.add))
        dv(lambda e: e.tensor_scalar(T[11], T[11], math.pi, -math.pi, OP.min, OP.max))
        ac(lambda e: e.activation(out=dst, in_=T[11], func=AF.Sin))

    sin_of(T[3], T[2], 0.0)
    sin_of(T[4], T[2], math.pi / 2)
    dv(lambda e: e.tensor_tensor(out=T[5], in0=T[1], in1=T[4], op=OP.mult))
    dv(lambda e: e.tensor_tensor(out=T[6], in0=T[1], in1=T[3], op=OP.mult))
    dv(lambda e: e.tensor_copy(out=PWv[:, :, 0, 0], in_=T[5]))
    dv(lambda e: e.tensor_copy(out=PWv[:, :, 0, 1], in_=T[6]))
    for k in range(11):
        dv(lambda e, k=k: e.tensor_tensor(out=T[10], in0=PWv[:, :, k, 0], in1=PWv[:, :, k, 0], op=OP.mult))
        dv(lambda e, k=k: e.tensor_tensor(out=T[11], in0=PWv[:, :, k, 1], in1=PWv[:, :, k, 1], op=OP.mult))
        dv(lambda e, k=k: e.tensor_tensor(out=PWv[:, :, k + 1, 0], in0=T[10], in1=T[11], op=OP.subtract))
        dv(lambda e, k=k: e.tensor_tensor(out=T[10], in0=PWv[:, :, k, 0], in1=PWv[:, :, k, 1], op=OP.mult))
        dv(lambda e, k=k: e.tensor_scalar(PWv[:, :, k + 1, 1], T[10], 2.0, None, OP.mult))
    dv(lambda e: e.tensor_scalar(PWv[:, :, :, 2], PWv[:, :, :, 1], -1.0, None, OP.mult))
    dv(lambda e: e.tensor_copy(out=PAv[:, :, 0, 0], in_=T[5]))
    dv(lambda e: e.tensor_copy(out=PAv[:, :, 0, 1], in_=T[6]))
    for m in range(1, 8):
        dv(lambda e, m=m: e.tensor_tensor(out=T[10], in0=PAv[:, :, m - 1, 0], in1=T[5], op=OP.mult))
        dv(lambda e, m=m: e.tensor_tensor(out=T[11], in0=PAv[:, :, m - 1, 1], in1=T[6], op=OP.mult))
        dv(lambda e, m=m: e.tensor_tensor(out=PAv[:, :, m, 0], in0=T[10], in1=T[11], op=OP.subtract))
        dv(lambda e, m=m: e.tensor_tensor(out=T[10], in0=PAv[:, :, m - 1, 0], in1=T[6], op=OP.mult))
        dv(lambda e, m=m: e.tensor_tensor(out=T[11], in0=PAv[:, :, m - 1, 1], in1=T[5], op=OP.mult))
        dv(lambda e, m=m: e.tensor_tensor(out=PAv[:, :, m, 1], in0=T[10], in1=T[11], op=OP.add))
    dv(lambda e: e.tensor_scalar(PAv[:, :, :, 2], PAv[:, :, :, 1], -1.0, None, OP.mult))
    dv(lambda e: e.tensor_scalar(T[7], T[5], -1.0, None, OP.add))
    dv(lambda e: e.tensor_tensor(out=T[8], in0=LRE, in1=LRE, op=OP.mult))
    dv(lambda e: e.tensor_tensor(out=T[9], in0=LIM, in1=LIM, op=OP.mult))
    dv(lambda e: e.tensor_tensor(out=T[8], in0=T[8], in1=T[9], op=OP.add))
    dv(lambda e: e.reciprocal(out=T[8], in_=T[8]))
    dv(lambda e: e.tensor_tensor(out=T[9], in0=T[7], in1=LRE, op=OP.mult))
    dv(lambda e: e.tensor_tensor(out=T[13], in0=T[6], in1=LIM, op=OP.mult))
    dv(lambda e: e.tensor_tensor(out=T[9], in0=T[9], in1=T[13], op=OP.add))
    dv(lambda e: e.tensor_tensor(out=T[9], in0=T[9], in1=T[8], op=OP.mult))
    dv(lambda e: e.tensor_tensor(out=T[13], in0=T[6], in1=LRE, op=OP.mult))
    dv(lambda e: e.tensor_tensor(out=T[14], in0=T[7], in1=LIM, op=OP.mult))
    dv(lambda e: e.tensor_tensor(out=T[13], in0=T[13], in1=T[14], op=OP.subtract))
    dv(lambda e: e.tensor_tensor(out=T[13], in0=T[13], in1=T[8], op=OP.mult))
    bb = alloc(64 * 16 * 3)
    bbv = bb.ap.rearrange("p (c q h) -> p c q h", c=3, q=64)
    crb = T[9].unsqueeze(2).to_broadcast([128, 64, 16])
    cib = T[13].unsqueeze(2).to_broadcast([128, 64, 16])
    P.op("dve", lambda e: e.memset(BPb.ap, 0.0), [], [s5t])
    P.op("dve", lambda e: e.memset(CPflat, 0.0), [], [s5t])
    dv(lambda e: e.tensor_tensor(out=bbv[:, 0], in0=Braw[:, :, 0, :], in1=crb, op=OP.mult))
    dv(lambda e: e.tensor_tensor(out=bbv[:, 2], in0=Braw[:, :, 1, :], in1=cib, op=OP.mult))
    dv(lambda e: e.tensor_tensor(out=bbv[:, 0], in0=bbv[:, 0], in1=bbv[:, 2], op=OP.subtract))
    dv(lambda e: e.tensor_tensor(out=bbv[:, 1], in0=Braw[:, :, 1, :], in1=crb, op=OP.mult))
    dv(lambda e: e.tensor_tensor(out=bbv[:, 2], in0=Braw[:, :, 0, :], in1=cib, op=OP.mult))
    dv(lambda e: e.tensor_tensor(out=bbv[:, 1], in0=bbv[:, 1], in1=bbv[:, 2], op=OP.add))
    for ri in range(2):
        dv(lambda e, ri=ri: e.tensor_copy(out=BPv[0:64, :, ri, 0:16], in_=bbv[0:64, ri]))
        dv(lambda e, ri=ri: e.tensor_copy(out=BPv[64:128, :, ri, 16:32], in_=bbv[64:128, ri]))
    dv(lambda e: e.tensor_copy(out=CPv[0:64, :, 0, 0:16], in_=Craw[0:64, :, 0, :]))
    dv(lambda e: e.tensor_copy(out=CPv[64:128, :, 0, 16:32], in_=Craw[64:128, :, 0, :]))
    dv(lambda e: e.tensor_scalar(CPv[0:64, :, 1, 0:16], Craw[0:64, :, 1, :], -1.0, None, OP.mult))
    dv(lambda e: e.tensor_scalar(CPv[64:128, :, 1, 16:32], Craw[64:128, :, 1, :], -1.0, None, OP.mult))
    for d in range(2):
        for j in range(8):
            for ri in range(2):
                pb = psb[(d * 16 + j * 2 + ri) % 4]
                for a in range(4):
                    dq = d * 32 + j * 4 + a
                    P.op("pe", lambda e, pb=pb, a=a, dq=dq, ri=ri: e.matmul(pb.ap[32 * a:32 * a + 32, 0:128], BPv[:, dq, ri, :], ident,
                                                                              start=True, stop=True, tile_position=(0, 32 * a)),
                         [s5t, cbf.t], [pb.t])
                P.op("dve", lambda e, pb=pb, d=d, j=j, ri=ri: e.tensor_copy(out=W1v[:, d, j, ri, :], in_=pb.ap[:, 0:128]), [pb.t], [s5t])
    P.op("sp", lambda e: e.dma_start(out=s5w_s[:, :], in_=S5W.ap), [s5t], [], dma=True)
    astate["off"] = persist_mark
    P.barrier()
    work_mark = astate["off"]

    bank_rr = [0]

    def nextbank(lo=0, hi=4):
        b = psb[lo + bank_rr[0] % (hi - lo)]
        bank_rr[0] += 1
        return b

    def load_WA(dst, WA, mb0, nmb):
        P.op("sp", lambda e: e.dma_start(out=dst.ap.rearrange("p (mb c) -> p mb c", mb=nmb),
                                         in_=WA[mb0:mb0 + nmb].rearrange("mb p c -> p mb c")),
             [], [dst.t], dma=True)

    def load_WB(dst, WB, nk, c0, cw):
        P.op("sp", lambda e: e.dma_start(out=dst.ap.rearrange("p (kc c) -> p kc c", kc=nk),
                                         in_=WB[:, c0:c0 + cw].rearrange("(kc p) c -> p kc c", p=128)),
             [], [dst.t], dma=True)

    def rms_rstd(ss_ap, n, rstd_buf, reads):
        P.op("act", lambda e: e.activation(out=rstd_buf.ap, in_=ss_ap, func=AF.Sqrt, bias=epsb.ap[:, 0:1], scale=1.0 / n),
             reads + [epsb.t], [rstd_buf.t])
        P.op("dve", lambda e: e.reciprocal(out=rstd_buf.ap, in_=rstd_buf.ap), [rstd_buf.t], [rstd_buf.t])

    def act_recip(dst, src_ap, reads):
        P.op("dve", lambda e: e.reciprocal(out=dst.ap, in_=src_ap), reads, [dst.t])

    def norm_transpose(xt, hT, s, xn, ssb, junk):
        P.op("act", lambda e: e.activation(out=junk.ap, in_=xt.ap, func=AF.Square, accum_out=ssb.ap[:, 0:1]), [xt.t], [junk.t, ssb.t])
        P.op("act", lambda e: e.activation(out=ssb.ap[:, 1:2], in_=ssb.ap[:, 0:1], func=AF.Sqrt, bias=epsb.ap[:, 0:1], scale=1.0 / D),
             [ssb.t, epsb.t], [ssb.t])
        P.op("dve", lambda e: e.reciprocal(out=ssb.ap[:, 2:3], in_=ssb.ap[:, 1:2]), [ssb.t], [ssb.t])
        P.op("dve", lambda e: e.tensor_scalar(xn.ap, xt.ap, ssb.ap[:, 2:3], None, OP.mult), [xt.t, ssb.t], [xn.t])
        pb = nextbank()
        pv = pb.ap.bitcast(BF)
        for kc in range(8):
            P.op("pe", lambda e, kc=kc: e.transpose(pv[:, kc * 128:(kc + 1) * 128], xn.ap[:, kc * 128:(kc + 1) * 128], ident),
                 [xn.t, cbf.t], [pb.t])
        hv = hT.ap.rearrange("p (kc t) -> p kc t", kc=8)
        P.op("act", lambda e: e.copy(out=hv[:, :, s * 128:(s + 1) * 128], in_=pv.rearrange("p (kc t) -> p kc t", kc=8)), [pb.t], [hT.t])

    def proj_fm(W, mbi, hT, nk=8):
        pb = nextbank()
        hv = hT.ap.rearrange("p (kc t) -> p kc t", kc=nk)
        for kc in range(nk):
            P.op("pe", lambda e, kc=kc: e.matmul(pb.ap, W.ap[:, mbi * 1024 + kc * 128:mbi * 1024 + (kc + 1) * 128], hv[:, kc, :],
                                                start=(kc == 0), stop=(kc == nk - 1)),
                 [W.t, hT.t], [pb.t])
        return pb

    def qknorm_rope(pb, gcol, dst_ap, dst_tok, cs, sn, wk, ones_blk, rope=True, n=64):
        sq, rstd, ybf, t1, t2 = wk
        P.op("act", lambda e: e.activation(out=sq.ap, in_=pb.ap, func=AF.Square), [pb.t], [sq.t])
        p2 = nextbank()
        P.op("pe", lambda e: e.matmul(p2.ap, ones_blk, sq.ap, start=True, stop=True), [sq.t, cbf.t], [p2.t])
        rms_rstd(p2.ap, n, rstd, [p2.t])
        P.op("dve", lambda e: e.tensor_scalar(ybf.ap, pb.ap, sm[:, gcol:gcol + 1], None, OP.mult), [pb.t, smalls.t], [ybf.t])
        if not rope:
            P.op("dve", lambda e: e.tensor_tensor(out=dst_ap, in0=ybf.ap, in1=rstd.ap, op=OP.mult), [ybf.t, rstd.t], [dst_tok])
            return
        p3 = nextbank()
        P.op("pe", lambda e: e.matmul(p3.ap, rotm, ybf.ap, start=True, stop=True), [ybf.t, cbf.t], [p3.t])
        P.op("pool", lambda e: e.tensor_tensor(out=t1.ap, in0=ybf.ap, in1=cs.ap, op=OP.mult), [ybf.t, cs.t], [t1.t])
        P.op("dve", lambda e: e.tensor_tensor(out=t2.ap, in0=p3.ap, in1=sn.ap, op=OP.mult), [p3.t, sn.t], [t2.t])
        P.op("dve", lambda e: e.tensor_tensor(out=t1.ap, in0=t1.ap, in1=t2.ap, op=OP.add), [t2.t, t1.t], [t1.t])
        P.op("dve", lambda e: e.tensor_tensor(out=dst_ap, in0=t1.ap, in1=rstd.ap, op=OP.mult), [t1.t, rstd.t], [dst_tok])

    def headnorm2(pbs, gcols, dsts, wk2, n=256):
        sqA, sqB, rstd = wk2
        N = dsts[0][0].shape[1]
        P.op("act", lambda e: e.activation(out=sqA.ap[:, 0:N], in_=pbs[0].ap[:, 0:N], func=AF.Square), [pbs[0].t], [sqA.t])
        P.op("act", lambda e: e.activation(out=sqB.ap[:, 0:N], in_=pbs[1].ap[:, 0:N], func=AF.Square), [pbs[1].t], [sqB.t])
        p2 = nextbank()
        P.op("pe", lambda e: e.matmul(p2.ap[:, 0:N], ones128, sqA.ap[:, 0:N], start=True, stop=False), [sqA.t, cbf.t], [p2.t])
        P.op("pe", lambda e: e.matmul(p2.ap[:, 0:N], ones128, sqB.ap[:, 0:N], start=False, stop=True), [sqB.t, cbf.t], [p2.t])
        P.op("act", lambda e: e.activation(out=rstd.ap[:, 0:N], in_=p2.ap[:, 0:N], func=AF.Sqrt, bias=epsb.ap[:, 0:1], scale=1.0 / n),
             [p2.t, epsb.t], [rstd.t])
        P.op("dve", lambda e: e.reciprocal(out=rstd.ap[:, 0:N], in_=rstd.ap[:, 0:N]), [rstd.t], [rstd.t])
        for c in range(2):
            P.op("dve", lambda e, c=c: e.scalar_tensor_tensor(out=dsts[c][0], in0=pbs[c].ap[:, 0:N], scalar=sm[:, gcols[c]:gcols[c] + 1],
                                                             in1=rstd.ap[:, 0:N], op0=OP.mult, op1=OP.mult),
                 [pbs[c].t, rstd.t, smalls.t], [dsts[c][1]])

    def do_seq(si, L, tok0):
        NT = L // 512
        NKT = L // 128
        astate["off"] = work_mark
        Wuk = abf(16 * 1024)
        P.op("sp", lambda e: e.dma_start(out=Wuk.ap[:, 0:8192].rearrange("p (mb c) -> p mb c", mb=8),
                                         in_=WAin[0:8].rearrange("mb p c -> p mb c")), [], [Wuk.t], dma=True)
        P.op("sp", lambda e: e.dma_start(out=Wuk.ap[:, 8192:16384].rearrange("p (mb c) -> p mb c", mb=8),
                                         in_=WAin[16:24].rearrange("mb p c -> p mb c")), [], [Wuk.t], dma=True)
        Wvh = [abf(8 * 512) for _ in range(2)]
        for hf in range(2):
            load_WB(Wvh[hf], WBv, 8, hf * 512, 512)
        xts = [alloc(1024) for _ in range(2)]
        xn = abf(1024)
        ssb = alloc(4)
        hTs = [abf(8 * 512) for _ in range(2)]
        css = [alloc(512) for _ in range(2)]
        sns = [alloc(512) for _ in range(2)]
        wk = (abf(512), alloc(512), abf(512), alloc(512), alloc(512))
        outb = [abf(512) for _ in range(4)]
        vtb = [abf(1024) for _ in range(2)]
        oi = 0
        for it in range(NT):
            hT = hTs[it % 2]
            cs, sn = css[it % 2], sns[it % 2]
            P.op("sp", lambda e, cs=cs, it=it: e.dma_start(out=cs.ap, in_=ropec_d[:, it * 512:(it + 1) * 512]), [], [cs.t], dma=True)
            P.op("sp", lambda e, sn=sn, it=it: e.dma_start(out=sn.ap, in_=ropes_d[:, it * 512:(it + 1) * 512]), [], [sn.t], dma=True)
            for s in range(4):
                xt = xts[s % 2]
                r0 = tok0 + it * 512 + s * 128
                P.op("sp", lambda e, xt=xt, r0=r0: e.dma_start(out=xt.ap, in_=x_d[r0:r0 + 128, :]), [], [xt.t], dma=True)
                norm_transpose(xt, hT, s, xn, ssb, xn)
            P.op("sp", lambda e, hT=hT, it=it: e.dma_start(out=hT_s[:, it * 512:(it + 1) * 512].rearrange("(kc p) t -> p kc t", p=128),
                                                          in_=hT.ap.rearrange("p (kc t) -> p kc t", kc=8)), [hT.t], [], dma=True)
            for m in range(8):
                pb = proj_fm(Wuk, m, hT)
                ob = outb[oi % 4]; oi += 1
                P.op("act", lambda e, ob=ob, pb=pb: e.copy(out=ob.ap, in_=pb.ap), [pb.t], [ob.t])
                P.op("sp", lambda e, ob=ob, m=m, it=it: e.dma_start(out=uT_s[m * 128:(m + 1) * 128, it * 512:(it + 1) * 512], in_=ob.ap),
                     [ob.t], [], dma=True)
            for m in range(8):
                pb = proj_fm(Wuk, 8 + m, hT)
                ob = outb[oi % 4]; oi += 1
                qknorm_rope(pb, KG, ob.ap, ob.t, cs, sn, wk, ones64)
                P.op("sp", lambda e, ob=ob, m=m, it=it: e.dma_start(out=KT_s[m * 128:(m + 1) * 128, it * 512:(it + 1) * 512], in_=ob.ap),
                     [ob.t], [], dma=True)
            hv = hT.ap.rearrange("p (kc t) -> p kc t", kc=8)
            for s in range(4):
                vt = vtb[s % 2]
                for hf in range(2):
                    pb = nextbank()
                    wv = Wvh[hf].ap.rearrange("p (kc c) -> p kc c", kc=8)
                    for kc in range(8):
                        P.op("pe", lambda e, pb=pb, kc=kc, s=s, wv=wv, hv=hv: e.matmul(pb.ap, hv[:, kc, s * 128:(s + 1) * 128], wv[:, kc, :],
                                                                                     start=(kc == 0), stop=(kc == 7)),
                             [hT.t, Wvh[hf].t], [pb.t])
                    if hf == 0:
                        P.op("act", lambda e, vt=vt, pb=pb: e.copy(out=vt.ap[:, 0:512], in_=pb.ap), [pb.t], [vt.t])
                    else:
                        P.op("dve", lambda e, vt=vt, pb=pb: e.tensor_copy(out=vt.ap[:, 512:1024], in_=pb.ap), [pb.t], [vt.t])
                P.op("sp", lambda e, vt=vt, it=it, s=s: e.dma_start(out=V_s[it * 512 + s * 128:it * 512 + (s + 1) * 128, :], in_=vt.ap),
                     [vt.t], [], dma=True)
        P.barrier()
        if STOP_AFTER == 1:
            return

        astate["off"] = work_mark
        S5Wl = alloc(S5WORDS)
        P.op("sp", lambda e, S5Wl=S5Wl: e.dma_start(out=S5Wl.ap, in_=s5w_s[:, :]), [], [S5Wl.t], dma=True)
        PWl, CPl, W1l, PAl = s5_views(S5Wl)
        wt = S5Wl.t
        T8 = 8
        Lc = L // T8
        NLV = int(math.log2(Lc))
        SAd = [[[abf(L) for _ in range(2)] for _ in range(2)] for _ in range(2)]
        Kb = [[[[alloc(Lc + 2) for _ in range(2)] for _ in range(2)] for _ in range(2)] for _ in range(2)]
        for st_ in range(2):
            for d in range(2):
                for pp in range(2):
                    for ri in range(2):
                        P.op("dve", lambda e, kb=Kb[st_][d][pp][ri]: e.memset(kb.ap, 0.0), [], [Kb[st_][d][pp][ri].t])
        uTj = [abf(L) for _ in range(2)]
        zj = abf(L)
        yj = abf(L)
        tz = [alloc(512) for _ in range(3)]
        ctmp = [alloc(Lc) for _ in range(4)]
        cti = [0]
        GC = 2.0 * math.sqrt(2.0 / math.pi)
        tpn = 512 // Lc if Lc < 512 else 1

        def stt(out, in0, scal, in1, reads, writes):
            P.op("dve", lambda e: e.scalar_tensor_tensor(out=out, in0=in0, scalar=scal, in1=in1, op0=OP.mult, op1=OP.add), reads, writes)

        def Xv(q, d):
            bufs = SAd[q % 2][d]
            return [bufs[ri].ap.rearrange("p (t c) -> p t c", t=T8) for ri in range(2)], [bufs[0].t, bufs[1].t]

        def s_evac(q):
            j, a = q // 4, q % 4
            uj = uTj[j % 2]
            if a == 0:
                P.op("sp", lambda e: e.dma_start(out=uj.ap, in_=uT_s[j * 128:(j + 1) * 128, 0:L]), [], [uj.t], dma=True)
            for d in range(2):
                for ri in range(2):
                    buf = SAd[q % 2][d][ri]
                    sav = buf.ap.rearrange("p (t c) -> p t c", t=T8)
                    for n in range(NT):
                        pb = nextbank()
                        P.op("pe", lambda e, pb=pb, d=d, ri=ri, n=n: e.matmul(
                            pb.ap, W1l[32 * a:32 * a + 32, d, j, ri, :], uj.ap[32 * a:32 * a + 32, n * 512:(n + 1) * 512],
                            start=True, stop=True, tile_position=(32 * a, 0)), [uj.t, wt], [pb.t])
                        P.op("act", lambda e, pb=pb, sav=sav, n=n: e.copy(out=sav[:, :, n * 64:(n + 1) * 64],
                                                                         in_=pb.ap.rearrange("p (c t) -> p t c", t=T8)), [pb.t], [buf.t])

        def scan_thunks(q, d):
            dq = d * 32 + q
            th = []
            X, tX = Xv(q, d)
            kb = Kb[q % 2][d]
            a1 = PAl[:, dq, 0, :]
            order = list(range(1, T8)) if d == 0 else list(range(T8 - 2, -1, -1))
            for tau in order:
                pv = tau - 1 if d == 0 else tau + 1
                th.append(lambda tau=tau, pv=pv: stt(X[0][:, tau, :], X[0][:, pv, :], a1[:, 0:1], X[0][:, tau, :], [tX[0], wt], [tX[0]]))
                th.append(lambda tau=tau, pv=pv: stt(X[1][:, tau, :], X[0][:, pv, :], a1[:, 1:2], X[1][:, tau, :], [tX[0], tX[1], wt], [tX[1]]))
                th.append(lambda tau=tau, pv=pv: stt(X[0][:, tau, :], X[1][:, pv, :], a1[:, 2:3], X[0][:, tau, :], [tX[0], tX[1], wt], [tX[0]]))
                th.append(lambda tau=tau, pv=pv: stt(X[1][:, tau, :], X[1][:, pv, :], a1[:, 0:1], X[1][:, tau, :], [tX[1], wt], [tX[1]]))
            e_ = T8 - 1 if d == 0 else 0
            cur = [X[0][:, e_, :], X[1][:, e_, :]]
            ctk = list(tX)
            for k in range(NLV):
                s = 1 << k
                nb_ = kb[k % 2]
                nd = [nb_[0].ap[:, 1:Lc + 1], nb_[1].ap[:, 1:Lc + 1]]
                ntk = [nb_[0].t, nb_[1].t]
                pw = PWl[:, dq, k + 3, :]
                if d == 0:
                    hi, lo, keep = slice(s, Lc), slice(0, Lc - s), slice(0, s)
                else:
                    hi, lo, keep = slice(0, Lc - s), slice(s, Lc), slice(Lc - s, Lc)
                for ri in range(2):
                    th.append(lambda ri=ri, nd=nd, cur=cur, keep=keep, ctk=ctk, ntk=ntk: P.op(
                        "dve", lambda e: e.tensor_copy(out=nd[ri][:, keep], in_=cur[ri][:, keep]), [ctk[ri]], [ntk[ri]]))
                th.append(lambda nd=nd, cur=cur, hi=hi, lo=lo, pw=pw, ctk=ctk, ntk=ntk: stt(nd[0][:, hi], cur[0][:, lo], pw[:, 0:1], cur[0][:, hi], [ctk[0], wt], [ntk[0]]))
                th.append(lambda nd=nd, cur=cur, hi=hi, lo=lo, pw=pw, ctk=ctk, ntk=ntk: stt(nd[1][:, hi], cur[0][:, lo], pw[:, 1:2], cur[1][:, hi], [ctk[0], ctk[1], wt], [ntk[1]]))
                th.append(lambda nd=nd, cur=cur, hi=hi, lo=lo, pw=pw, ctk=ctk, ntk=ntk: stt(nd[0][:, hi], cur[1][:, lo], pw[:, 2:3], nd[0][:, hi], [ctk[1], ntk[0], wt], [ntk[0]]))
                th.append(lambda nd=nd, cur=cur, hi=hi, lo=lo, pw=pw, ctk=ctk, ntk=ntk: stt(nd[1][:, hi], cur[1][:, lo], pw[:, 0:1], nd[1][:, hi], [ctk[1], ntk[1], wt], [ntk[1]]))
                cur, ctk = nd, ntk
            return th

        def s_scan(q):
            th0 = scan_thunks(q, 0)
            th1 = scan_thunks(q, 1)
            for i in range(max(len(th0), len(th1))):
                if i < len(th0):
                    th0[i]()
                if i < len(th1):
                    th1[i]()

        def s_carry(q):
            for d in range(2):
                dq = d * 32 + q
                X, tX = Xv(q, d)
                fin = Kb[q % 2][d][(NLV - 1) % 2]
                sh = [fin[ri].ap[:, 0:Lc] if d == 0 else fin[ri].ap[:, 2:Lc + 2] for ri in range(2)]
                ftk = [fin[0].t, fin[1].t]
                e_ = T8 - 1 if d == 0 else 0
                for ri in range(2):
                    P.op("act", lambda e, ri=ri, X=X, fin=fin, e_=e_: e.copy(out=X[ri][:, e_, :], in_=fin[ri].ap[:, 1:Lc + 1]), [ftk[ri]], [tX[ri]])
                taus = list(range(0, T8 - 1)) if d == 0 else list(range(1, T8))
                for tau in taus:
                    m = tau if d == 0 else (T8 - 1 - tau)
                    pw = PAl[:, dq, m, :]
                    for (ri, ca, cb) in ((0, 0, 2), (1, 1, 0)):
                        c1 = ctmp[cti[0] % 4]; c2 = ctmp[(cti[0] + 1) % 4]; cti[0] += 2
                        P.op("act", lambda e, c1=c1, sh=sh, pw=pw, ca=ca: e.activation(out=c1.ap, in_=sh[0], func=AF.Copy, scale=pw[:, ca:ca + 1]), [ftk[0], wt], [c1.t])
                        P.op("act", lambda e, c2=c2, sh=sh, pw=pw, cb=cb: e.activation(out=c2.ap, in_=sh[1], func=AF.Copy, scale=pw[:, cb:cb + 1]), [ftk[1], wt], [c2.t])
                        P.op("pool", lambda e, c1=c1, X=X, ri=ri, tau=tau: e.tensor_tensor(out=c1.ap, in0=c1.ap, in1=X[ri][:, tau, :], op=OP.add), [c1.t, tX[ri]], [c1.t])
                        P.op("pool", lambda e, c1=c1, c2=c2, X=X, ri=ri, tau=tau: e.tensor_tensor(out=X[ri][:, tau, :], in0=c1.ap, in1=c2.ap, op=OP.add), [c1.t, c2.t], [tX[ri]])

        def s_outmm(q):
            a = q % 4
            qs = slice(32 * a, 32 * a + 32)
            for n in range(NT):
                pb = nextbank()
                cnt = 0
                for d in range(2):
                    for ri in range(2):
                        buf = SAd[q % 2][d][ri]
                        P.op("pe", lambda e, pb=pb, d=d, ri=ri, n=n, cnt=cnt, buf=buf: e.matmul(
                            pb.ap[qs, :], CPl[:, d * 32 + q, ri, :], buf.ap[:, n * 512:(n + 1) * 512],
                            start=(cnt == 0), stop=(cnt == 3), tile_position=(0, 32 * a)), [buf.t, wt], [pb.t])
                        cnt += 1
                P.op("act", lambda e, pb=pb, n=n: e.copy(out=yj.ap[qs, n * 512:(n + 1) * 512], in_=pb.ap[qs, :]), [pb.t], [yj.t])

        def z_chunk(j):
            uj = uTj[j % 2]
            t0, t1_, t2_ = tz
            v3 = lambda ap_: ap_.rearrange("p (t c) -> p t c", t=tpn)
            for n in range(NT):
                upv = uj.ap.rearrange("p (c t) -> p t c", t=T8)[:, n * tpn:(n + 1) * tpn, :]
                zpv = zj.ap.rearrange("p (c t) -> p t c", t=T8)[:, n * tpn:(n + 1) * tpn, :]
                P.op("dve", lambda e, n=n, upv=upv: e.scalar_tensor_tensor(
                    out=v3(t0.ap), in0=upv, scalar=sm[:, S5D + j:S5D + j + 1], in1=v3(yj.ap[:, n * 512:(n + 1) * 512]), op0=OP.mult, op1=OP.add),
                    [yj.t, uj.t, smalls.t], [t0.t])
                P.op("act", lambda e: e.activation(out=t1_.ap, in_=t0.ap, func=AF.Square), [t0.t], [t1_.t])
                P.op("dve", lambda e: e.tensor_scalar(t1_.ap, t1_.ap, 0.044715, 1.0, OP.mult, OP.add), [t1_.t], [t1_.t])
                P.op("dve", lambda e: e.tensor_tensor(out=t1_.ap, in0=t1_.ap, in1=t0.ap, op=OP.mult), [t1_.t, t0.t], [t1_.t])
                P.op("act", lambda e: e.activation(out=t2_.ap, in_=t1_.ap, func=AF.Sigmoid, scale=GC), [t1_.t], [t2_.t])
                P.op("dve", lambda e, zpv=zpv: e.tensor_tensor(out=zpv, in0=v3(t0.ap), in1=v3(t2_.ap), op=OP.mult), [t0.t, t2_.t], [zj.t])
            P.op("sp", lambda e: e.dma_start(out=zT_s[j * 128:(j + 1) * 128, 0:L], in_=zj.ap), [zj.t], [], dma=True)

        s_evac(0)
        for i in range(32):
            if i >= 1:
                s_outmm(i - 1)
            if i + 1 < 32:
                s_evac(i + 1)
            s_scan(i)
            if i >= 1 and (i - 1) % 4 == 3:
                z_chunk((i - 1) // 4)
            s_carry(i)
        s_outmm(31)
        z_chunk(7)
        P.barrier()
        astate["off"] = work_mark
        Wg = abf(8 * 1024)
        load_WA(Wg, WAglu, 0, 8)
        zts = [abf(8 * 512) for _ in range(2)]
        sig = alloc(512)
        outb = [abf(512) for _ in range(4)]
        for it in range(NT):
            zt = zts[it % 2]
            P.op("sp", lambda e, zt=zt, it=it: e.dma_start(out=zt.ap.rearrange("p (kc t) -> p kc t", kc=8),
                                                          in_=zT_s[:, it * 512:(it + 1) * 512].rearrange("(kc p) t -> p kc t", p=128)),
                 [], [zt.t], dma=True)
            zv = zt.ap.rearrange("p (kc t) -> p kc t", kc=8)
            for m in range(8):
                pb = proj_fm(Wg, m, zt)
                P.op("act", lambda e, pb=pb: e.activation(out=sig.ap, in_=pb.ap, func=AF.Sigmoid), [pb.t], [sig.t])
                ob = outb[oi % 4]; oi += 1
                P.op("dve", lambda e, ob=ob, m=m, zv=zv: e.tensor_tensor(out=ob.ap, in0=zv[:, m, :], in1=sig.ap, op=OP.mult), [sig.t, zt.t], [ob.t])
                P.op("sp", lambda e, ob=ob, m=m, it=it: e.dma_start(out=s5T_s[m * 128:(m + 1) * 128, it * 512:(it + 1) * 512], in_=ob.ap),
                     [ob.t], [], dma=True)
        P.barrier()
        if STOP_AFTER == 2:
            return

        astate["off"] = work_mark
        hTs = [abf(8 * 512) for _ in range(2)]
        css = alloc(512); sns = alloc(512)
        wk = (abf(512), alloc(512), abf(512), alloc(512), alloc(512))
        wk2 = (abf(512), abf(512), alloc(512))
        NWB = 6
        wblk = [abf(1024) for _ in range(NWB)]
        wbi = [0]

        def load_block(WA, idx):
            b = wblk[wbi[0] % NWB]
            wbi[0] += 1
            P.op("sp", lambda e, b=b: e.dma_start(out=b.ap, in_=WA[idx]), [], [b.t], dma=True)
            return b

        qTb = [abf(512) for _ in range(2)]
        KTh = [abf(L) for _ in range(2)]
        Vh = [abf(L) for _ in range(2)]
        PT = [abf(512) for _ in range(4)]
        pts2 = [abf(512) for _ in range(2)]
        fo = wk2[2]
        brT = [abf(8 * 512) for _ in range(3)]
        MkT = abf(8 * 256)
        MV = abf(2 * 1024)
        merged = abf(8 * 512)
        gate = wk[1]; macc = wk2[2]; mtmp = wk[4]
        WBp = [abf(8 * 512) for _ in range(2)]
        x1s = [alloc(1024) for _ in range(2)]
        xn = abf(1024)
        ssb = alloc(4)
        actT = abf(22 * 512)
        ybuf = [wk[3], wk[4]]
        memx = x1s
        brv = [b.ap.rearrange("p (kc t) -> p kc t", kc=8) for b in brT]
        mgv = merged.ap.rearrange("p (kc t) -> p kc t", kc=8)
        MkTv = MkT.ap.rearrange("p (c m) -> p c m", c=8)
        MVv = MV.ap.rearrange("p (s c) -> p s c", s=2)
        actv = actT.ap.rearrange("p (kc t) -> p kc t", kc=22)

        mhT = hTs[0]
        mhv = mhT.ap.rearrange("p (kc t) -> p kc t", kc=8)
        for s in range(2):
            xt = memx[s]
            r0 = si * NMEM + s * 128
            P.op("sp", lambda e, xt=xt, r0=r0: e.dma_start(out=xt.ap, in_=mem_d[r0:r0 + 128, :]), [], [xt.t], dma=True)
            norm_transpose(xt, mhT, s, xn, ssb, xn)
        for hm in range(4):
            pbs = []
            for c in range(2):
                wb = load_block(WAkv, 2 * hm + c)
                pb = nextbank()
                for kc in range(8):
                    P.op("pe", lambda e, pb=pb, wb=wb, kc=kc: e.matmul(pb.ap[:, 0:256], wb.ap[:, kc * 128:(kc + 1) * 128], mhv[:, kc, 0:256],
                                                                      start=(kc == 0), stop=(kc == 7)), [wb.t, mhT.t], [pb.t])
                pbs.append(pb)
            headnorm2(pbs, (MKG, MKG + 1), [(MkTv[:, 2 * hm + c, :], MkT.t) for c in range(2)], wk2)
        for hf in range(2):
            wp = WBp[hf]
            load_WB(wp, WBkv, 8, hf * 512, 512)
            wpv = wp.ap.rearrange("p (kc c) -> p kc c", kc=8)
            for s in range(2):
                pb = nextbank()
                for kc in range(8):
                    P.op("pe", lambda e, pb=pb, kc=kc, s=s, wpv=wpv: e.matmul(pb.ap, mhv[:, kc, s * 128:(s + 1) * 128], wpv[:, kc, :],
                                                                             start=(kc == 0), stop=(kc == 7)), [wp.t, mhT.t], [pb.t])
                P.op("act", lambda e, pb=pb, s=s, hf=hf: e.copy(out=MVv[:, s, hf * 512:(hf + 1) * 512], in_=pb.ap), [pb.t], [MV.t])

        def load_kv(h):
            kth, vh = KTh[h % 2], Vh[h % 2]
            P.op("sp", lambda e: e.dma_start(out=kth.ap, in_=KT_s[h * 128:(h + 1) * 128, 0:L]), [], [kth.t], dma=True)
            P.op("sp", lambda e: e.dma_start(out=vh.ap.rearrange("p (kt c) -> p kt c", kt=NKT),
                                             in_=V_s[0:L, h * 128:(h + 1) * 128].rearrange("(kt p) c -> p kt c", p=128)),
                 [], [vh.t], dma=True)

        def tile_loads(it):
            ts = slice(it * 512, (it + 1) * 512)
            hT = hTs[it % 2]
            P.op("sp", lambda e: e.dma_start(out=hT.ap.rearrange("p (kc t) -> p kc t", kc=8),
                                             in_=hT_s[:, ts].rearrange("(kc p) t -> p kc t", p=128)), [], [hT.t], dma=True)
            P.op("sp", lambda e: e.dma_start(out=css.ap, in_=ropec_d[:, ts]), [], [css.t], dma=True)
            P.op("sp", lambda e: e.dma_start(out=sns.ap, in_=ropes_d[:, ts]), [], [sns.t], dma=True)
            P.op("sp", lambda e: e.dma_start(out=brv[0], in_=s5T_s[:, ts].rearrange("(kc p) t -> p kc t", p=128)), [], [brT[0].t], dma=True)
            load_kv(0)

        tile_loads(0)
        for it in range(NT):
            ts = slice(it * 512, (it + 1) * 512)
            hT = hTs[it % 2]
            h2T = hTs[(it + 1) % 2]
            def q_stage(h):
                wbq = load_block(WAin, 8 + h)
                pbq = proj_fm(wbq, 0, hT)
                qknorm_rope(pbq, QG, qTb[h % 2].ap, qTb[h % 2].t, css, sns, wk, ones64)

            q_stage(0)
            for h in range(8):
                kth, vh, qT = KTh[h % 2], Vh[h % 2], qTb[h % 2]
                if h > 0:
                    load_kv(h)
                vhv = vh.ap.rearrange("p (kt c) -> p kt c", kt=NKT)
                O1, O2, Z1, Z2 = psb[4], psb[5], psb[6], psb[7]
                def emit_scores(kt):
                    ks = slice(kt * 128, (kt + 1) * 128)
                    pS1 = nextbank(); pS2 = nextbank()
                    P.op("pe", lambda e, pS1=pS1, ks=ks, kth=kth, qT=qT: e.matmul(pS1.ap, kth.ap[0:64, ks], qT.ap[0:64, :], start=True, stop=True,
                                                                               tile_position=(0, 0)), [kth.t, qT.t], [pS1.t])
                    P.op("pe", lambda e, pS2=pS2, ks=ks, kth=kth, qT=qT: e.matmul(pS2.ap, kth.ap[64:128, ks], qT.ap[64:128, :], start=True, stop=True,
                                                                               tile_position=(64, 0)), [kth.t, qT.t], [pS2.t])
                    return pS1, pS2

                nxt_sc = emit_scores(0)
                for kt in range(NKT):
                    pS1, pS2 = nxt_sc
                    qst_here = (kt == min(1, NKT - 1) and h + 1 < 8)
                    if kt + 1 < NKT and not qst_here:
                        nxt_sc = emit_scores(kt + 1)
                    p1, p2 = PT[(2 * kt) % 4], PT[(2 * kt + 1) % 4]
                    P.op("act", lambda e, pS1=pS1, p1=p1: e.activation(out=p1.ap, in_=pS1.ap, func=AF.Exp, scale=0.125), [pS1.t], [p1.t])
                    P.op("act", lambda e, pS2=pS2, p2=p2: e.activation(out=p2.ap, in_=pS2.ap, func=AF.Exp, scale=0.125), [pS2.t], [p2.t])
                    st, sp_ = (kt == 0), (kt == NKT - 1)
                    P.op("pe", lambda e, p1=p1, kt=kt, st=st, sp_=sp_, vhv=vhv: e.matmul(O1.ap, vhv[:, kt, :], p1.ap, start=st, stop=sp_), [vh.t, p1.t], [O1.t])
                    P.op("pe", lambda e, p1=p1, st=st, sp_=sp_: e.matmul(Z1.ap, ones128, p1.ap, start=st, stop=sp_), [cbf.t, p1.t], [Z1.t])
                    P.op("pe", lambda e, p2=p2, kt=kt, st=st, sp_=sp_, vhv=vhv: e.matmul(O2.ap, vhv[:, kt, :], p2.ap, start=st, stop=sp_), [vh.t, p2.t], [O2.t])
                    P.op("pe", lambda e, p2=p2, st=st, sp_=sp_: e.matmul(Z2.ap, ones128, p2.ap, start=st, stop=sp_), [cbf.t, p2.t], [Z2.t])
                    if qst_here:
                        q_stage(h + 1)
                        if kt + 1 < NKT:
                            nxt_sc = emit_scores(kt + 1)
                sq, rstd, ybf, t1, t2 = wk
                P.op("act", lambda e: e.copy(out=t1.ap, in_=Z1.ap), [Z1.t], [t1.t])
                P.op("act", lambda e: e.copy(out=t2.ap, in_=Z2.ap), [Z2.t], [t2.t])
                P.op("act", lambda e: e.copy(out=fo.ap, in_=O1.ap), [O1.t], [fo.t])
                P.op("act", lambda e: e.copy(out=rstd.ap, in_=O2.ap), [O2.t], [rstd.t])
                P.op("dve", lambda e: e.reciprocal(out=t1.ap, in_=t1.ap), [t1.t], [t1.t])
                P.op("dve", lambda e: e.tensor_tensor(out=t1.ap, in0=fo.ap, in1=t1.ap, op=OP.mult), [fo.t, t1.t], [t1.t])
                P.op("dve", lambda e: e.reciprocal(out=t2.ap, in_=t2.ap), [t2.t], [t2.t])
                P.op("dve", lambda e: e.tensor_tensor(out=t2.ap, in0=rstd.ap, in1=t2.ap, op=OP.mult), [rstd.t, t2.t], [t2.t])
                P.op("dve", lambda e: e.scalar_tensor_tensor(out=fo.ap, in0=t2.ap, scalar=NEGLAM, in1=t1.ap, op0=OP.mult, op1=OP.add),
                     [t1.t, t2.t, lamb.t], [fo.t])
                P.op("act", lambda e: e.activation(out=sq.ap, in_=fo.ap, func=AF.Square), [fo.t], [sq.t])
                p2b = nextbank()
                P.op("pe", lambda e, p2b=p2b: e.matmul(p2b.ap, ones128, sq.ap, start=True, stop=True), [sq.t, cbf.t], [p2b.t])
                rms_rstd(p2b.ap, 128, rstd, [p2b.t])
                P.op("dve", lambda e: e.scalar_tensor_tensor(out=t1.ap, in0=fo.ap, scalar=sm[:, SUBG:SUBG + 1], in1=rstd.ap, op0=OP.mult, op1=OP.mult),
                     [fo.t, rstd.t, smalls.t], [t1.t])
                P.op("act", lambda e, h=h: e.activation(out=brv[1][:, h, :], in_=t1.ap, func=AF.Copy, scale=1.0 - LAMBDA_INIT), [t1.t], [brT[1].t])
            def mem_A(hm):
                pbs = []
                for c in range(2):
                    wb = load_block(WAin, 32 + 2 * hm + c)
                    pbs.append(proj_fm(wb, 0, hT))
                qa, qb = PT[2 * (hm % 2)], PT[2 * (hm % 2) + 1]
                headnorm2(pbs, (MQG, MQG + 1), [(qa.ap, qa.t), (qb.ap, qb.t)], wk2)

            def mem_B(hm):
                qa, qb = PT[2 * (hm % 2)], PT[2 * (hm % 2) + 1]
                pts = pts2
                for mt in range(2):
                    pS = nextbank()
                    ms = slice(mt * 128, (mt + 1) * 128)
                    P.op("pe", lambda e, pS=pS, ms=ms: e.matmul(pS.ap, MkTv[:, 2 * hm, ms], qa.ap, start=True, stop=False), [MkT.t, qa.t], [pS.t])
                    P.op("pe", lambda e, pS=pS, ms=ms: e.matmul(pS.ap, MkTv[:, 2 * hm + 1, ms], qb.ap, start=False, stop=True), [MkT.t, qb.t], [pS.t])
                    P.op("act", lambda e, pS=pS, mt=mt: e.activation(out=pts[mt].ap, in_=pS.ap, func=AF.Exp, scale=1.0 / 16.0), [pS.t], [pts[mt].t])
                pZ = nextbank()
                P.op("pe", lambda e: e.matmul(pZ.ap, ones128, pts[0].ap, start=True, stop=False), [cbf.t, pts[0].t], [pZ.t])
                P.op("pe", lambda e: e.matmul(pZ.ap, ones128, pts[1].ap, start=False, stop=True), [cbf.t, pts[1].t], [pZ.t])
                rz = wk[3]
                act_recip(rz, pZ.ap, [pZ.t])
                for ec in range(2):
                    pO = nextbank()
                    es = slice((2 * hm + ec) * 128, (2 * hm + ec + 1) * 128)
                    P.op("pe", lambda e, pO=pO, es=es: e.matmul(pO.ap, MVv[:, 0, es], pts[0].ap, start=True, stop=False), [MV.t, pts[0].t], [pO.t])
                    P.op("pe", lambda e, pO=pO, es=es: e.matmul(pO.ap, MVv[:, 1, es], pts[1].ap, start=False, stop=True), [MV.t, pts[1].t], [pO.t])
                    P.op("dve", lambda e, pO=pO, ec=ec: e.tensor_tensor(out=brv[2][:, 2 * hm + ec, :], in0=pO.ap, in1=rz.ap, op=OP.mult),
                         [pO.t, rz.t], [brT[2].t])

            mem_A(0)
            for hm in range(4):
                if hm + 1 < 4:
                    mem_A(hm + 1)
                mem_B(hm)
            for m in range(8):
                for nb in range(3):
                    wg = load_block(WAin, 40 + 8 * nb + m)
                    pG = proj_fm(wg, 0, hT)
                    P.op("act", lambda e, pG=pG, nb=nb, m=m: e.activation(out=gate.ap, in_=pG.ap, func=AF.Sigmoid,
                                                                        bias=sm[:, B_GATE + 8 * nb + m:B_GATE + 8 * nb + m + 1]),
                         [pG.t, smalls.t], [gate.t])
                    wbb = load_block(WAbr, 8 * nb + m)
                    pB = proj_fm(wbb, 0, brT[nb])
                    if nb == 0:
                        P.op("dve", lambda e, pB=pB: e.tensor_tensor(out=macc.ap, in0=pB.ap, in1=gate.ap, op=OP.mult), [pB.t, gate.t], [macc.t])
                    else:
                        P.op("dve", lambda e, pB=pB: e.tensor_tensor(out=mtmp.ap, in0=pB.ap, in1=gate.ap, op=OP.mult), [pB.t, gate.t], [mtmp.t])
                        if nb == 1:
                            P.op("pool", lambda e: e.tensor_tensor(out=macc.ap, in0=macc.ap, in1=mtmp.ap, op=OP.add), [macc.t, mtmp.t], [macc.t])
                        else:
                            P.op("pool", lambda e, m=m: e.tensor_tensor(out=mgv[:, m, :], in0=macc.ap, in1=mtmp.ap, op=OP.add), [macc.t, mtmp.t], [merged.t])
            for hf in range(2):
                load_WB(WBp[hf], WBout, 8, hf * 512, 512)
            ytoks = [Tok() for _ in range(4)]

            def op_mm(s):
                x1 = x1s[s % 2]
                r0 = tok0 + it * 512 + s * 128
                P.op("sp", lambda e: e.dma_start(out=x1.ap, in_=x_d[r0:r0 + 128, :]), [], [x1.t], dma=True)
                for hf in range(2):
                    pb = nextbank()
                    wpv = WBp[hf].ap.rearrange("p (kc c) -> p kc c", kc=8)
                    for kc in range(8):
                        P.op("pe", lambda e, pb=pb, kc=kc, wpv=wpv: e.matmul(pb.ap, mgv[:, kc, s * 128:(s + 1) * 128], wpv[:, kc, :],
                                                                          start=(kc == 0), stop=(kc == 7)), [merged.t, WBp[hf].t], [pb.t])
                    P.op("dve", lambda e, pb=pb, hf=hf: e.tensor_tensor(out=x1.ap[:, hf * 512:(hf + 1) * 512], in0=pb.ap,
                                                                      in1=x1.ap[:, hf * 512:(hf + 1) * 512], op=OP.add), [pb.t, x1.t], [x1.t])

            def op_post(s):
                x1 = x1s[s % 2]
                r0 = tok0 + it * 512 + s * 128
                norm_transpose(x1, h2T, s, xn, ssb, xn)
                P.op("sp", lambda e: e.dma_start(out=y_d[r0:r0 + 128, :], in_=x1.ap), [x1.t], [ytoks[s]], dma=True)

            op_mm(0)
            for s in range(4):
                if s + 1 < 4:
                    op_mm(s + 1)
                op_post(s)
            for m in range(22):
                wg = load_block(WAgu, m)
                pg = proj_fm(wg, 0, h2T)
                wu = load_block(WAgu, 22 + m)
                pu = proj_fm(wu, 0, h2T)
                P.op("act", lambda e, pg=pg: e.activation(out=gate.ap, in_=pg.ap, func=AF.Silu), [pg.t], [gate.t])
                P.op("dve", lambda e, pu=pu, m=m: e.tensor_tensor(out=actv[:, m, :], in0=pu.ap, in1=gate.ap, op=OP.mult), [pu.t, gate.t], [actT.t])
            if it + 1 < NT:
                tile_loads(it + 1)
            pi = 0
            for hf in range(2):
                groups = [(0, 8), (8, 16), (16, 22)]
                for (g0, g1) in groups:
                    wp = WBp[pi % 2]; pi += 1
                    nk = g1 - g0
                    P.op("sp", lambda e, wp=wp, g0=g0, g1=g1, nk=nk, hf=hf: e.dma_start(
                        out=wp.ap[:, 0:nk * 512].rearrange("p (kc c) -> p kc c", kc=nk),
                        in_=WBdown[g0 * 128:g1 * 128, hf * 512:(hf + 1) * 512].rearrange("(kc p) c -> p kc c", p=128)), [], [wp.t], dma=True)
                    wpv = wp.ap[:, 0:nk * 512].rearrange("p (kc c) -> p kc c", kc=nk)
                    for s in range(4):
                        acc = psb[4 + s] if hf == 0 else psb[s]
                        for kc in range(g0, g1):
                            P.op("pe", lambda e, acc=acc, kc=kc, g0=g0, s=s, wpv=wpv: e.matmul(acc.ap, actv[:, kc, s * 128:(s + 1) * 128], wpv[:, kc - g0, :],
                                                                                             start=(kc == 0), stop=(kc == 21)), [actT.t, wp.t], [acc.t])
                for s in range(4):
                    yb = ybuf[s % 2]
                    r0 = tok0 + it * 512 + s * 128
                    cs_ = slice(hf * 512, (hf + 1) * 512)
                    acc = psb[4 + s] if hf == 0 else psb[s]
                    P.op("sp", lambda e, yb=yb, r0=r0, cs_=cs_: e.dma_start(out=yb.ap, in_=y_d[r0:r0 + 128, cs_]), [ytoks[s]], [yb.t], dma=True)
                    P.op("dve", lambda e, yb=yb, acc=acc: e.tensor_tensor(out=yb.ap, in0=acc.ap, in1=yb.ap, op=OP.add), [acc.t, yb.t], [yb.t])
                    P.op("sp", lambda e, yb=yb, r0=r0, cs_=cs_: e.dma_start(out=y_d[r0:r0 + 128, cs_], in_=yb.ap), [yb.t], [ytoks[s]], dma=True)
        P.barrier()


    tok0 = 0
    for si, L in enumerate(seqLs):
        do_seq(si, L, tok0)
        tok0 += L

    P.prepare(nc)
    with nc.Block() as block:
        P.emit(nc, block)
    return nc


def _host_consts():
    bf = ml_dtypes.bfloat16
    cb = np.zeros((128, 512), np.float32)
    cb[:, 0:128] = np.eye(128)
    rot = np.zeros((128, 128), np.float32)
    for b in (0, 64):
        for d in range(32):
            rot[b + d + 32, b + d] = -1.0
            rot[b + d, b + d + 32] = 1.0
    cb[:, 128:256] = rot
    o64 = np.zeros((128, 128), np.float32)
    o64[0:64, 0:64] = 1.0
    o64[64:128, 64:128] = 1.0
    cb[:, 256:384] = o64
    cb[:, 384:512] = 1.0
    half = 32
    inv = (np.float32(10000.0) ** (-(np.arange(half, dtype=np.float32) / np.float32(half)))).astype(np.float32)
    ang = (np.arange(4096, dtype=np.float32)[:, None] * inv[None, :]).astype(np.float32)
    cos = np.cos(ang).astype(np.float32)
    sin = np.sin(ang).astype(np.float32)
    idx = np.arange(128) % 32
    ropec = np.ascontiguousarray(cos[:, idx].T)
    ropes = np.ascontiguousarray(sin[:, idx].T)
    return cb.astype(bf), ropec, ropes


def _host_params(inp):
    f = lambda k: np.asarray(inp[k], np.float32)
    sm = np.zeros((128, 320), np.float32)
    col = lambda v, n: np.ascontiguousarray(v.reshape(n, 128).T)
    sm[:, 0:8] = col(f("norm_mix_g")[0], 8)
    sm[:, 8:16] = col(f("ffn_norm_g")[0], 8)
    sm[:, 16:24] = col(f("mem_norm_g")[0], 8)
    sm[:, 24:48] = col(f("b_gate")[0], 24)
    sm[:, 48:56] = col(f("s5_d")[0], 8)
    p = np.arange(128)
    sm[:, 56] = f("diff_q_g")[0][p % 64]
    sm[:, 57] = f("diff_k_g")[0][p % 64]
    sm[:, 58] = f("diff_sub_g")[0]
    sm[:, 59:61] = col(f("mem_q_g")[0], 2)
    sm[:, 61:63] = col(f("mem_k_g")[0], 2)
    for i, k in enumerate(("diff_lq1", "diff_lk1", "diff_lq2", "diff_lk2")):
        sm[:, 63 + 64 * i:63 + 64 * (i + 1)] = f(k)[0][None, :]
    def ps(a):
        a = a.reshape((2, 32, 2, 64) + a.shape[3:])
        a = np.moveaxis(a, (2, 3), (0, 1))
        return np.ascontiguousarray(a.reshape((128, 64) + a.shape[4:]))
    lre = ps(f("s5_lam_re")[0])
    lim = ps(f("s5_lam_im")[0])
    ldt = ps(np.broadcast_to(f("s5_log_dt")[0][:, :, None], (2, 64, 64)).copy())
    bre = ps(f("s5_b_re")[0])
    bim = ps(f("s5_b_im")[0])
    cre = ps(np.swapaxes(f("s5_c_re")[0], 2, 3).copy())
    cim = ps(np.swapaxes(f("s5_c_im")[0], 2, 3).copy())
    Bst = np.stack([bre, bim], axis=2).reshape(128, 2048)
    Cst = np.stack([cre, cim], axis=2).reshape(128, 2048)
    s5p = np.concatenate([lre, lim, ldt, Bst, Cst], axis=1).astype(np.float32)
    return sm, np.ascontiguousarray(s5p)


_NC_CACHE = {}


def _get_nc(seqLs, **kw):
    key = (tuple(seqLs), tuple(sorted(kw.items())))
    if key not in _NC_CACHE:
        _NC_CACHE[key] = build(list(seqLs), **kw)
    return _NC_CACHE[key]


def _weights_map(inp):
    f = lambda k: np.ascontiguousarray(np.asarray(inp[k], np.float32))
    sm, s5p = _host_params(inp)
    cb, ropec, ropes = _host_consts()
    return {
        "w_in": f("w_in")[0], "w_glu": f("s5_w_glu")[0], "w_br": f("w_branch")[0].reshape(3072, 1024),
        "w_out": f("w_out")[0], "w_gu": f("w_gate_up")[0], "w_down": f("w_down")[0], "w_kv": f("w_mem_kv")[0],
        "smalls": sm, "s5p": s5p, "cbf": cb, "ropec": ropec, "ropes": ropes,
    }


def kernel(**inp):
    xp = np.asarray(inp["x_prompt"], np.float32)
    xs = np.asarray(inp["x_sample"], np.float32)
    mp = np.asarray(inp["mem_prompt"], np.float32)
    ms = np.asarray(inp["mem_sample"], np.float32)
    seqLs = [xp.shape[1]] * 2 + [xs.shape[1]] * 2
    nc = _get_nc(seqLs)
    wm = _weights_map(inp)
    in_maps = []
    for c in range(8):
        x = np.concatenate([xp[2 * c], xp[2 * c + 1], xs[2 * c], xs[2 * c + 1]], axis=0)
        mem = np.concatenate([mp[2 * c], mp[2 * c + 1], ms[2 * c], ms[2 * c + 1]], axis=0)
        m = dict(wm)
        m["x"] = np.ascontiguousarray(x)
        m["mem"] = np.ascontiguousarray(mem)
        in_maps.append(m)
    res = run_bass_kernel_spmd(nc, in_maps, core_ids=list(range(8)))
    yp = np.empty_like(xp)
    ys = np.empty_like(xs)
    Lp, Ls = xp.shape[1], xs.shape[1]
    for c in range(8):
        y = np.asarray(res.results[c]["y"], np.float32)
        yp[2 * c] = y[0:Lp]
        yp[2 * c + 1] = y[Lp:2 * Lp]
        ys[2 * c] = y[2 * Lp:2 * Lp + Ls]
        ys[2 * c + 1] = y[2 * Lp + Ls:2 * Lp + 2 * Ls]
    return (yp, ys)
```

```python
import math
import numpy as np
import ml_dtypes
import concourse.bass as bass
import concourse.mybir as mybir
from concourse.bass_utils import run_bass_kernel_spmd

F32 = mybir.dt.float32
BF = mybir.dt.bfloat16
I32 = mybir.dt.int32
AF = mybir.ActivationFunctionType
OP = mybir.AluOpType

D = 1024
NMEM = 256
EPS = 1e-6
LAMBDA_INIT = 0.8 - 0.6 * math.exp(-0.3 * 0)
TWO_PI = 2.0 * math.pi
KDMA = 8

SPLIT_Q = True
SPLIT_F = True


class Tok:
    __slots__ = ("w", "r", "dr")

    def __init__(self):
        self.w = None
        self.r = {}
        self.dr = []


class Prog:
    def __init__(self):
        self.ops = []
        self.last = {}
        self.dma_since = []

    def op(self, eng, fn, reads=(), writes=(), dma=False):
        deps = set()
        for b in reads:
            if b.w is not None:
                deps.add(b.w)
        for b in writes:
            if b.w is not None:
                deps.add(b.w)
            deps.update(b.r.values())
            deps.update(b.dr)
        i = len(self.ops)
        self.ops.append((eng, fn, deps, dma))
        for b in reads:
            if dma:
                b.dr.append(i)
            else:
                b.r[eng] = i
        for b in writes:
            b.w = i
            b.r = {}
            b.dr = []
        self.last[eng] = i
        if dma:
            self.dma_since.append(i)
        return i

    def barrier(self):
        deps = set(self.last.values()) | set(self.dma_since)
        for eng in ("pe", "act", "dve", "pool", "sp"):
            self.ops.append((eng, None, set(deps), False))
        self.dma_since = []

    def prepare(self, nc):
        self.csem = {e: nc.semaphore("cs_" + e).__enter__() for e in ("pe", "act", "dve", "pool")}
        self.dsem = [nc.semaphore("ds%d" % i).__enter__() for i in range(KDMA)]

    def emit(self, nc, block):
        ops = self.ops
        n = len(ops)
        needed = [False] * n
        for j in range(n):
            ej = ops[j][0]
            for k in ops[j][2]:
                if ops[k][0] == "pe" and ej == "pe" and not ops[k][3]:
                    continue
                needed[k] = True
        csem, dsem = self.csem, self.dsem
        semof = [None] * n
        cnt = {e: 0 for e in csem}
        dcount = 0
        dma_prev = {}
        for j in range(n):
            eng, fn, deps, dma = ops[j]
            if fn is None:
                continue
            if dma:
                s = dsem[dcount % KDMA]
                semof[j] = (s, 16 * (dcount // KDMA + 1), dcount)
                dcount += 1
            elif needed[j]:
                cnt[eng] += 1
                semof[j] = (csem[eng], cnt[eng], None)
        streams = {e: [] for e in ("pe", "act", "dve", "pool", "sp")}
        for j in range(n):
            streams[ops[j][0]].append(j)
        final_dma = {}
        for j in range(n):
            if ops[j][3] and ops[j][1] is not None:
                s, v, _ = semof[j]
                final_dma[id(s)] = (s, v)

        def run(engname, e):
            waited = {}
            for j in streams[engname]:
                eng, fn, deps, dma = ops[j]
                need = {}
                for k in deps:
                    if ops[k][1] is None:
                        continue
                    if ops[k][0] == "pe" and eng == "pe" and not ops[k][3]:
                        continue
                    sk = semof[k]
                    if sk is None:
                        continue
                    s, v, _ = sk
                    if need.get(id(s), (None, 0))[1] < v:
                        need[id(s)] = (s, v)
                if dma and fn is not None:
                    s, v, idx = semof[j]
                    if idx >= KDMA:
                        pv = v - 16
                        if need.get(id(s), (None, 0))[1] < pv:
                            need[id(s)] = (s, pv)
                for sid, (s, v) in need.items():
                    if waited.get(sid, 0) < v:
                        e.wait_ge(s, v)
                        waited[sid] = v
                if fn is None:
                    continue
                inst = fn(e)
                if semof[j] is not None:
                    inst.then_inc(semof[j][0], 16 if dma else 1)
            if engname == "sp":
                for sid, (s, v) in final_dma.items():
                    if waited.get(sid, 0) < v:
                        e.wait_ge(s, v)

        @block.tensor
        def _(e):
            run("pe", e)

        @block.scalar
        def _(e):
            run("act", e)

        @block.vector
        def _(e):
            run("dve", e)

        @block.gpsimd
        def _(e):
            run("pool", e)

        @block.sync
        def _(e):
            run("sp", e)


class Buf:
    def __init__(self, ap):
        self.ap = ap
        self.t = Tok()


def build(seqLs, STOP_AFTER=99, DEBUG=False):
    NS = len(seqLs)
    NTOK = sum(seqLs)
    LMAX = max(seqLs)
    nc = bass.Bass("TRN2", target_bir_lowering=False)

    def dram(name, shape, dtype, kind):
        return nc.dram_tensor(name, shape, dtype, kind=kind).ap()

    x_d = dram("x", [NTOK, D], F32, "ExternalInput")
    mem_d = dram("mem", [NS * NMEM, D], F32, "ExternalInput")
    y_d = dram("y", [NTOK, D], F32, "ExternalOutput")
    w_in_d = dram("w_in", [D, 8192], F32, "ExternalInput")
    w_glu_d = dram("w_glu", [D, D], F32, "ExternalInput")
    w_br_d = dram("w_br", [3 * D, D], F32, "ExternalInput")
    w_out_d = dram("w_out", [D, D], F32, "ExternalInput")
    w_gu_d = dram("w_gu", [D, 5632], F32, "ExternalInput")
    w_down_d = dram("w_down", [2816, D], F32, "ExternalInput")
    w_kv_d = dram("w_kv", [D, 2048], F32, "ExternalInput")
    smalls_d = dram("smalls", [128, 320], F32, "ExternalInput")
    s5p_d = dram("s5p", [128, 64 * 67], F32, "ExternalInput")
    cbf_d = dram("cbf", [128, 512], BF, "ExternalInput")
    ropec_d = dram("ropec", [128, 4096], F32, "ExternalInput")
    ropes_d = dram("ropes", [128, 4096], F32, "ExternalInput")

    WAin = dram("WAin", [64, 128, 1024], BF, "Internal")
    WAglu = dram("WAglu", [8, 128, 1024], BF, "Internal")
    WAbr = dram("WAbr", [24, 128, 1024], BF, "Internal")
    WAgu = dram("WAgu", [44, 128, 1024], BF, "Internal")
    WAkv = dram("WAkv", [8, 128, 1024], BF, "Internal")
    WBv = dram("WBv", [D, D], BF, "Internal")
    WBout = dram("WBout", [D, D], BF, "Internal")
    WBdown = dram("WBdown", [2816, D], BF, "Internal")
    WBkv = dram("WBkv", [D, D], BF, "Internal")
    hT_s = dram("hT_s", [D, LMAX], BF, "ExternalOutput" if DEBUG else "Internal")
    uT_s = dram("uT_s", [D, LMAX], BF, "ExternalOutput" if DEBUG else "Internal")
    zT_s = dram("zT_s", [D, LMAX], BF, "ExternalOutput" if DEBUG else "Internal")
    s5T_s = dram("s5T_s", [D, LMAX], BF, "ExternalOutput" if DEBUG else "Internal")
    KT_s = dram("KT_s", [D, LMAX], BF, "ExternalOutput" if DEBUG else "Internal")
    V_s = dram("V_s", [LMAX, D], BF, "ExternalOutput" if DEBUG else "Internal")
    s5w_s = dram("s5w_s", [128, 7936], F32, "ExternalOutput" if DEBUG else "Internal")

    P = Prog()
    ARENA = 45000
    arena_cm = nc.sbuf_tensor("arena", [128, ARENA], F32)
    arena = arena_cm.__enter__()
    ps_cms = [nc.psum_tensor("ps%d" % i, [128, 512], F32) for i in range(8)]
    psb = [Buf(c.__enter__()[:, :]) for c in ps_cms]
    astate = {"off": 0}

    def alloc(words, dtype=F32, shape=None):
        o = astate["off"]
        assert o + words <= ARENA, ("arena overflow", o, words)
        astate["off"] = o + words
        ap = arena[:, o:o + words]
        if dtype != F32:
            ap = ap.bitcast(dtype)
        return Buf(ap)

    def abf(cols):
        return alloc((cols + 1) // 2, BF)

    smalls = alloc(320)
    cbf = abf(512)
    P.op("sp", lambda e: e.dma_start(out=smalls.ap, in_=smalls_d[:, :]), [], [smalls.t], dma=True)
    P.op("sp", lambda e: e.dma_start(out=cbf.ap, in_=cbf_d[:, :]), [], [cbf.t], dma=True)
    ident = cbf.ap[:, 0:128]
    rotm = cbf.ap[:, 128:256]
    ones64 = cbf.ap[:, 256:384]
    ones128 = cbf.ap[:, 384:512]
    G_MIX, G_FFN, G_MEM, B_GATE, S5D = 0, 8, 16, 24, 48
    QG, KG, SUBG, MQG, MKG, LQ = 56, 57, 58, 59, 61, 63
    sm = smalls.ap
    epsb = alloc(2)
    P.op("dve", lambda e: e.memset(epsb.ap[:, 0:1], EPS), [], [epsb.t])
    lamb = alloc(8)

    def emit_lambda():
        a = lamb.ap
        P.op("dve", lambda e: e.memset(a[:, 0:8], 0.0), [], [lamb.t])
        tmp = alloc(64)
        for i in range(2):
            P.op("dve", lambda e, i=i: e.tensor_tensor(out=tmp.ap, in0=sm[:, LQ + 128 * i:LQ + 128 * i + 64],
                                                    in1=sm[:, LQ + 128 * i + 64:LQ + 128 * i + 128], op=OP.mult),
                 [smalls.t, lamb.t], [tmp.t])
            P.op("dve", lambda e, i=i: e.tensor_reduce(out=a[:, 1 + i:2 + i], in_=tmp.ap, axis=mybir.AxisListType.X, op=OP.add),
                 [tmp.t], [lamb.t])
        P.op("act", lambda e: e.activation(out=a[:, 3:5], in_=a[:, 1:3], func=AF.Exp), [lamb.t], [lamb.t])
        P.op("dve", lambda e: e.tensor_tensor(out=a[:, 5:6], in0=a[:, 4:5], in1=a[:, 3:4], op=OP.subtract), [lamb.t], [lamb.t])
        P.op("dve", lambda e: e.tensor_scalar(a[:, 0:1], a[:, 5:6], -LAMBDA_INIT, None, OP.add), [lamb.t], [lamb.t])

    emit_lambda()
    NEGLAM = lamb.ap[:, 0:1]

    persist_mark = astate["off"]

    stg = [alloc(2048) for _ in range(2)]
    stb = [abf(2048) for _ in range(2)]
    cast_i = [0]

    def cast_weight(src, nk, ncols, dstA=None, dstB=None, gcol=None, colsA=None, colsB=None):
        for kc in range(nk):
            for c0 in range(0, ncols, 2048):
                cw = min(2048, ncols - c0)
                i = cast_i[0]
                cast_i[0] += 1
                sg, sb = stg[i % 2], stb[i % 2]
                P.op("sp", lambda e, sg=sg, kc=kc, c0=c0, cw=cw: e.dma_start(out=sg.ap[:, 0:cw], in_=src[kc * 128:(kc + 1) * 128, c0:c0 + cw]),
                     [], [sg.t], dma=True)
                if gcol is not None:
                    gap = sm[:, gcol + kc:gcol + kc + 1]
                    if i % 2 == 0:
                        P.op("act", lambda e, sg=sg, sb=sb, cw=cw, gap=gap: e.activation(out=sb.ap[:, 0:cw], in_=sg.ap[:, 0:cw], func=AF.Copy, scale=gap),
                             [sg.t, smalls.t], [sb.t])
                    else:
                        P.op("dve", lambda e, sg=sg, sb=sb, cw=cw, gap=gap: e.tensor_scalar(sb.ap[:, 0:cw], sg.ap[:, 0:cw], gap, None, OP.mult),
                             [sg.t, smalls.t], [sb.t])
                else:
                    if i % 2 == 0:
                        P.op("act", lambda e, sg=sg, sb=sb, cw=cw: e.copy(out=sb.ap[:, 0:cw], in_=sg.ap[:, 0:cw]), [sg.t], [sb.t])
                    else:
                        P.op("dve", lambda e, sg=sg, sb=sb, cw=cw: e.tensor_copy(out=sb.ap[:, 0:cw], in_=sg.ap[:, 0:cw]), [sg.t], [sb.t])
                for (lo, hi, dA, mb0) in (colsA or []):
                    a0, a1 = max(lo, c0), min(hi, c0 + cw)
                    if a0 >= a1:
                        continue
                    dview = dA.rearrange("mb p (kc m) -> p mb kc m", kc=8)[:, mb0 + (a0 - lo) // 128:mb0 + (a1 - lo) // 128, kc, :]
                    sview = sb.ap[:, a0 - c0:a1 - c0].rearrange("p (mb m) -> p mb m", m=128)
                    P.op("sp", lambda e, dview=dview, sview=sview: e.dma_start(out=dview, in_=sview), [sb.t], [], dma=True)
                for (lo, hi, dB) in (colsB or []):
                    a0, a1 = max(lo, c0), min(hi, c0 + cw)
                    if a0 >= a1:
                        continue
                    P.op("sp", lambda e, dB=dB, kc=kc, a0=a0, a1=a1, lo=lo, sb=sb, c0=c0: e.dma_start(
                        out=dB[kc * 128:(kc + 1) * 128, a0 - lo:a1 - lo], in_=sb.ap[:, a0 - c0:a1 - c0]), [sb.t], [], dma=True)

    cast_weight(w_in_d, 8, 8192, gcol=G_MIX, colsA=[(0, 8192, WAin, 0)], colsB=[(3072, 4096, WBv)])
    cast_weight(w_glu_d, 8, 1024, colsA=[(0, 1024, WAglu, 0)])
    for nb in range(3):
        cast_weight(w_br_d[nb * 1024:(nb + 1) * 1024, :], 8, 1024, colsA=[(0, 1024, WAbr, 8 * nb)])
    cast_weight(w_out_d, 8, 1024, colsB=[(0, 1024, WBout)])
    cast_weight(w_gu_d, 8, 5632, gcol=G_FFN, colsA=[(0, 5632, WAgu, 0)])
    cast_weight(w_down_d, 22, 1024, colsB=[(0, 1024, WBdown)])
    cast_weight(w_kv_d, 8, 2048, gcol=G_MEM, colsA=[(0, 1024, WAkv, 0)], colsB=[(1024, 2048, WBkv)])

    s5raw = alloc(64 * 67)
    P.op("sp", lambda e: e.dma_start(out=s5raw.ap, in_=s5p_d[:, :]), [], [s5raw.t], dma=True)
    raw = s5raw.ap
    LRE = raw[:, 0:64]
    LIM = raw[:, 64:128]
    LDT = raw[:, 128:192]
    Braw = raw[:, 192:192 + 2048].rearrange("p (q r h) -> p q r h", q=64, r=2)
    Craw = raw[:, 2240:2240 + 2048].rearrange("p (q r h) -> p q r h", q=64, r=2)
    S5WORDS = 2304 + 2048 + 2048 + 1536

    def s5_views(buf):
        ap = buf.ap
        pwv = ap[:, 0:2304].rearrange("p (q k c) -> p q k c", q=64, k=12)
        cpv = ap[:, 2304:4352].bitcast(BF).rearrange("p (q r c) -> p q r c", q=64, r=2)
        w1v = ap[:, 4352:6400].bitcast(BF).rearrange("p (d j r m) -> p d j r m", d=2, j=8, r=2)
        pav = ap[:, 6400:7936].rearrange("p (q k c) -> p q k c", q=64, k=8)
        return pwv, cpv, w1v, pav

    S5W = alloc(S5WORDS)
    PWv, CPv, W1v, PAv = s5_views(S5W)
    CPflat = S5W.ap[:, 2304:4352].bitcast(BF)
    BPb = abf(64 * 2 * 32)
    BPv = BPb.ap.rearrange("p (q r c) -> p q r c", q=64, r=2)
    s5_mark = astate["off"]
    tl = [alloc(64) for _ in range(16)]
    tint = alloc(64, I32)
    T = [t.ap for t in tl]
    s5t = Tok()

    def dv(fn):
        P.op("dve", fn, [s5raw.t, s5t], [s5t])

    def ac(fn):
        P.op("act", fn, [s5raw.t, s5t], [s5t])

    ac(lambda e: e.activation(out=T[0], in_=LDT, func=AF.Exp))
    dv(lambda e: e.tensor_tensor(out=T[1], in0=LRE, in1=T[0], op=OP.mult))
    ac(lambda e: e.activation(out=T[1], in_=T[1], func=AF.Exp))
    dv(lambda e: e.tensor_tensor(out=T[2], in0=LIM, in1=T[0], op=OP.mult))

    def sin_of(dst, src, shift):
        dv(lambda e: e.tensor_scalar(T[10], src, shift, 1.0 / TWO_PI, OP.add, OP.mult))
        dv(lambda e: e.tensor_copy(out=tint.ap, in_=T[10]))
        dv(lambda e: e.tensor_copy(out=T[10], in_=tint.ap))
        dv(lambda e: e.tensor_scalar(T[11], src, shift, None, OP.add))
        dv(lambda e: e.scalar_tensor_tensor(out=T[11], in0=T[10], scalar=-TWO_PI, in1=T[11], op0=OP.mult, op1=OP.add))
        dv(lambda e: e.tensor_scalar(T[12], T[11], math.pi, -TWO_PI, OP.is_gt, OP.mult))
        dv(lambda e: e.tensor_tensor(out=T[11], in0=T[11], in1=T[12], op=OP.add))
        dv(lambda e: e.tensor_scalar(T[12], T[11], -math.pi, TWO_PI, OP.is_lt, OP.mult))
        dv(lambda e: e.tensor_tensor(out=T[11], in0=T[11], in1=T[12], op=OP.add))
        dv(lambda e: e.tensor_scalar(T[11], T[11], math.pi, -math.pi, OP.min, OP.max))
        ac(lambda e: e.activation(out=dst, in_=T[11], func=AF.Sin))

    sin_of(T[3], T[2], 0.0)
    sin_of(T[4], T[2], math.pi / 2)
    dv(lambda e: e.tensor_tensor(out=T[5], in0=T[1], in1=T[4], op=OP.mult))
    dv(lambda e: e.tensor_tensor(out=T[6], in0=T[1], in1=T[3], op=OP.mult))
    dv(lambda e: e.tensor_copy(out=PWv[:, :, 0, 0], in_=T[5]))
    dv(lambda e: e.tensor_copy(out=PWv[:, :, 0, 1], in_=T[6]))
    for k in range(11):
        dv(lambda e, k=k: e.tensor_tensor(out=T[10], in0=PWv[:, :, k, 0], in1=PWv[:, :, k, 0], op=OP.mult))
        dv(lambda e, k=k: e.tensor_tensor(out=T[11], in0=PWv[:, :, k, 1], in1=PWv[:, :, k, 1], op=OP.mult))
        dv(lambda e, k=k: e.tensor_tensor(out=PWv[:, :, k + 1, 0], in0=T[10], in1=T[11], op=OP.subtract))
        dv(lambda e, k=k: e.tensor_tensor(out=T[10], in0=PWv[:, :, k, 0], in1=PWv[:, :, k, 1], op=OP.mult))
        dv(lambda e, k=k: e.tensor_scalar(PWv[:, :, k + 1, 1], T[10], 2.0, None, OP.mult))
    dv(lambda e: e.tensor_scalar(PWv[:, :, :, 2], PWv[:, :, :, 1], -1.0, None, OP.mult))
    dv(lambda e: e.tensor_copy(out=PAv[:, :, 0, 0], in_=T[5]))
    dv(lambda e: e.tensor_copy(out=PAv[:, :, 0, 1], in_=T[6]))
    for m in range(1, 8):
        dv(lambda e, m=m: e.tensor_tensor(out=T[10], in0=PAv[:, :, m - 1, 0], in1=T[5], op=OP.mult))
        dv(lambda e, m=m: e.tensor_tensor(out=T[11], in0=PAv[:, :, m - 1, 1], in1=T[6], op=OP.mult))
        dv(lambda e, m=m: e.tensor_tensor(out=PAv[:, :, m, 0], in0=T[10], in1=T[11], op=OP.subtract))
        dv(lambda e, m=m: e.tensor_tensor(out=T[10], in0=PAv[:, :, m - 1, 0], in1=T[6], op=OP.mult))
        dv(lambda e, m=m: e.tensor_tensor(out=T[11], in0=PAv[:, :, m - 1, 1], in1=T[5], op=OP.mult))
        dv(lambda e, m=m: e.tensor_tensor(out=PAv[:, :, m, 1], in0=T[10], in1=T[11], op=OP.add))
    dv(lambda e: e.tensor_scalar(PAv[:, :, :, 2], PAv[:, :, :, 1], -1.0, None, OP.mult))
    dv(lambda e: e.tensor_scalar(T[7], T[5], -1.0, None, OP.add))
    dv(lambda e: e.tensor_tensor(out=T[8], in0=LRE, in1=LRE, op=OP.mult))
    dv(lambda e: e.tensor_tensor(out=T[9], in0=LIM, in1=LIM, op=OP.mult))
    dv(lambda e: e.tensor_tensor(out=T[8], in0=T[8], in1=T[9], op=OP.add))
    dv(lambda e: e.reciprocal(out=T[8], in_=T[8]))
    dv(lambda e: e.tensor_tensor(out=T[9], in0=T[7], in1=LRE, op=OP.mult))
    dv(lambda e: e.tensor_tensor(out=T[13], in0=T[6], in1=LIM, op=OP.mult))
    dv(lambda e: e.tensor_tensor(out=T[9], in0=T[9], in1=T[13], op=OP.add))
    dv(lambda e: e.tensor_tensor(out=T[9], in0=T[9], in1=T[8], op=OP.mult))
    dv(lambda e: e.tensor_tensor(out=T[13], in0=T[6], in1=LRE, op=OP.mult))
    dv(lambda e: e.tensor_tensor(out=T[14], in0=T[7], in1=LIM, op=OP.mult))
    dv(lambda e: e.tensor_tensor(out=T[13], in0=T[13], in1=T[14], op=OP.subtract))
    dv(lambda e: e.tensor_tensor(out=T[13], in0=T[13], in1=T[8], op=OP.mult))
    bb = alloc(64 * 16 * 3)
    bbv = bb.ap.rearrange("p (c q h) -> p c q h", c=3, q=64)
    crb = T[9].unsqueeze(2).to_broadcast([128, 64, 16])
    cib = T[13].unsqueeze(2).to_broadcast([128, 64, 16])
    P.op("dve", lambda e: e.memset(BPb.ap, 0.0), [], [s5t])
    P.op("dve", lambda e: e.memset(CPflat, 0.0), [], [s5t])
    dv(lambda e: e.tensor_tensor(out=bbv[:, 0], in0=Braw[:, :, 0, :], in1=crb, op=OP.mult))
    dv(lambda e: e.tensor_tensor(out=bbv[:, 2], in0=Braw[:, :, 1, :], in1=cib, op=OP.mult))
    dv(lambda e: e.tensor_tensor(out=bbv[:, 0], in0=bbv[:, 0], in1=bbv[:, 2], op=OP.subtract))
    dv(lambda e: e.tensor_tensor(out=bbv[:, 1], in0=Braw[:, :, 1, :], in1=crb, op=OP.mult))
    dv(lambda e: e.tensor_tensor(out=bbv[:, 2], in0=Braw[:, :, 0, :], in1=cib, op=OP.mult))
    dv(lambda e: e.tensor_tensor(out=bbv[:, 1], in0=bbv[:, 1], in1=bbv[:, 2], op=OP.add))
    for ri in range(2):
        dv(lambda e, ri=ri: e.tensor_copy(out=BPv[0:64, :, ri, 0:16], in_=bbv[0:64, ri]))
        dv(lambda e, ri=ri: e.tensor_copy(out=BPv[64:128, :, ri, 16:32], in_=bbv[64:128, ri]))
    dv(lambda e: e.tensor_copy(out=CPv[0:64, :, 0, 0:16], in_=Craw[0:64, :, 0, :]))
    dv(lambda e: e.tensor_copy(out=CPv[64:128, :, 0, 16:32], in_=Craw[64:128, :, 0, :]))
    dv(lambda e: e.tensor_scalar(CPv[0:64, :, 1, 0:16], Craw[0:64, :, 1, :], -1.0, None, OP.mult))
    dv(lambda e: e.tensor_scalar(CPv[64:128, :, 1, 16:32], Craw[64:128, :, 1, :], -1.0, None, OP.mult))
    for d in range(2):
        for j in range(8):
            for ri in range(2):
                pb = psb[(d * 16 + j * 2 + ri) % 4]
                for a in range(4):
                    dq = d * 32 + j * 4 + a
                    P.op("pe", lambda e, pb=pb, a=a, dq=dq, ri=ri: e.matmul(pb.ap[32 * a:32 * a + 32, 0:128], BPv[:, dq, ri, :], ident,
                                                                              start=True, stop=True, tile_position=(0, 32 * a)),
                         [s5t, cbf.t], [pb.t])
                P.op("dve", lambda e, pb=pb, d=d, j=j, ri=ri: e.tensor_copy(out=W1v[:, d, j, ri, :], in_=pb.ap[:, 0:128]), [pb.t], [s5t])
    P.op("sp", lambda e: e.dma_start(out=s5w_s[:, :], in_=S5W.ap), [s5t], [], dma=True)
    astate["off"] = persist_mark
    P.barrier()
    work_mark = astate["off"]

    bank_rr = [0]

    def nextbank(lo=0, hi=4):
        b = psb[lo + bank_rr[0] % (hi - lo)]
        bank_rr[0] += 1
        return b

    def load_WA(dst, WA, mb0, nmb):
        P.op("sp", lambda e: e.dma_start(out=dst.ap.rearrange("p (mb c) -> p mb c", mb=nmb),
                                         in_=WA[mb0:mb0 + nmb].rearrange("mb p c -> p mb c")),
             [], [dst.t], dma=True)

    def load_WB(dst, WB, nk, c0, cw):
        P.op("sp", lambda e: e.dma_start(out=dst.ap.rearrange("p (kc c) -> p kc c", kc=nk),
                                         in_=WB[:, c0:c0 + cw].rearrange("(kc p) c -> p kc c", p=128)),
             [], [dst.t], dma=True)

    def rms_rstd(ss_ap, n, rstd_buf, reads):
        P.op("act", lambda e: e.activation(out=rstd_buf.ap, in_=ss_ap, func=AF.Sqrt, bias=epsb.ap[:, 0:1], scale=1.0 / n),
             reads + [epsb.t], [rstd_buf.t])
        P.op("dve", lambda e: e.reciprocal(out=rstd_buf.ap, in_=rstd_buf.ap), [rstd_buf.t], [rstd_buf.t])

    def act_recip(dst, src_ap, reads):
        P.op("dve", lambda e: e.reciprocal(out=dst.ap, in_=src_ap), reads, [dst.t])

    def norm_transpose(xt, hT, s, xn, ssb, junk):
        P.op("act", lambda e: e.activation(out=junk.ap, in_=xt.ap, func=AF.Square, accum_out=ssb.ap[:, 0:1]), [xt.t], [junk.t, ssb.t])
        P.op("act", lambda e: e.activation(out=ssb.ap[:, 1:2], in_=ssb.ap[:, 0:1], func=AF.Sqrt, bias=epsb.ap[:, 0:1], scale=1.0 / D),
             [ssb.t, epsb.t], [ssb.t])
        P.op("dve", lambda e: e.reciprocal(out=ssb.ap[:, 2:3], in_=ssb.ap[:, 1:2]), [ssb.t], [ssb.t])
        P.op("dve", lambda e: e.tensor_scalar(xn.ap, xt.ap, ssb.ap[:, 2:3], None, OP.mult), [xt.t, ssb.t], [xn.t])
        pb = nextbank()
        pv = pb.ap.bitcast(BF)
        for kc in range(8):
            P.op("pe", lambda e, kc=kc: e.transpose(pv[:, kc * 128:(kc + 1) * 128], xn.ap[:, kc * 128:(kc + 1) * 128], ident),
                 [xn.t, cbf.t], [pb.t])
        hv = hT.ap.rearrange("p (kc t) -> p kc t", kc=8)
        P.op("act", lambda e: e.copy(out=hv[:, :, s * 128:(s + 1) * 128], in_=pv.rearrange("p (kc t) -> p kc t", kc=8)), [pb.t], [hT.t])

    def proj_fm(W, mbi, hT, nk=8):
        pb = nextbank()
        hv = hT.ap.rearrange("p (kc t) -> p kc t", kc=nk)
        for kc in range(nk):
            P.op("pe", lambda e, kc=kc: e.matmul(pb.ap, W.ap[:, mbi * 1024 + kc * 128:mbi * 1024 + (kc + 1) * 128], hv[:, kc, :],
                                                start=(kc == 0), stop=(kc == nk - 1)),
                 [W.t, hT.t], [pb.t])
        return pb

    def qknorm_rope(pb, gcol, dst_ap, dst_tok, cs, sn, wk, ones_blk, rope=True, n=64):
        sq, rstd, ybf, t1, t2 = wk
        P.op("act", lambda e: e.activation(out=sq.ap, in_=pb.ap, func=AF.Square), [pb.t], [sq.t])
        p2 = nextbank()
        P.op("pe", lambda e: e.matmul(p2.ap, ones_blk, sq.ap, start=True, stop=True), [sq.t, cbf.t], [p2.t])
        rms_rstd(p2.ap, n, rstd, [p2.t])
        P.op("dve", lambda e: e.tensor_scalar(ybf.ap, pb.ap, sm[:, gcol:gcol + 1], None, OP.mult), [pb.t, smalls.t], [ybf.t])
        if not rope:
            P.op("dve", lambda e: e.tensor_tensor(out=dst_ap, in0=ybf.ap, in1=rstd.ap, op=OP.mult), [ybf.t, rstd.t], [dst_tok])
            return
        p3 = nextbank()
        P.op("pe", lambda e: e.matmul(p3.ap, rotm, ybf.ap, start=True, stop=True), [ybf.t, cbf.t], [p3.t])
        P.op("pool", lambda e: e.tensor_tensor(out=t1.ap, in0=ybf.ap, in1=cs.ap, op=OP.mult), [ybf.t, cs.t], [t1.t])
        P.op("dve", lambda e: e.tensor_tensor(out=t2.ap, in0=p3.ap, in1=sn.ap, op=OP.mult), [p3.t, sn.t], [t2.t])
        P.op("dve", lambda e: e.tensor_tensor(out=t1.ap, in0=t1.ap, in1=t2.ap, op=OP.add), [t2.t, t1.t], [t1.t])
        P.op("dve", lambda e: e.tensor_tensor(out=dst_ap, in0=t1.ap, in1=rstd.ap, op=OP.mult), [t1.t, rstd.t], [dst_tok])

    def headnorm2(pbs, gcols, dsts, wk2, n=256):
        sqA, sqB, rstd = wk2
        N = dsts[0][0].shape[1]
        P.op("act", lambda e: e.activation(out=sqA.ap[:, 0:N], in_=pbs[0].ap[:, 0:N], func=AF.Square), [pbs[0].t], [sqA.t])
        P.op("act", lambda e: e.activation(out=sqB.ap[:, 0:N], in_=pbs[1].ap[:, 0:N], func=AF.Square), [pbs[1].t], [sqB.t])
        p2 = nextbank()
        P.op("pe", lambda e: e.matmul(p2.ap[:, 0:N], ones128, sqA.ap[:, 0:N], start=True, stop=False), [sqA.t, cbf.t], [p2.t])
        P.op("pe", lambda e: e.matmul(p2.ap[:, 0:N], ones128, sqB.ap[:, 0:N], start=False, stop=True), [sqB.t, cbf.t], [p2.t])
        P.op("act", lambda e: e.activation(out=rstd.ap[:, 0:N], in_=p2.ap[:, 0:N], func=AF.Sqrt, bias=epsb.ap[:, 0:1], scale=1.0 / n),
             [p2.t, epsb.t], [rstd.t])
        P.op("dve", lambda e: e.reciprocal(out=rstd.ap[:, 0:N], in_=rstd.ap[:, 0:N]), [rstd.t], [rstd.t])
        for c in range(2):
            P.op("dve", lambda e, c=c: e.scalar_tensor_tensor(out=dsts[c][0], in0=pbs[c].ap[:, 0:N], scalar=sm[:, gcols[c]:gcols[c] + 1],
                                                             in1=rstd.ap[:, 0:N], op0=OP.mult, op1=OP.mult),
                 [pbs[c].t, rstd.t, smalls.t], [dsts[c][1]])

    def do_seq(si, L, tok0):
        NT = L // 512
        NKT = L // 128
        astate["off"] = work_mark
        Wuk = abf(16 * 1024)
        P.op("sp", lambda e: e.dma_start(out=Wuk.ap[:, 0:8192].rearrange("p (mb c) -> p mb c", mb=8),
                                         in_=WAin[0:8].rearrange("mb p c -> p mb c")), [], [Wuk.t], dma=True)
        P.op("sp", lambda e: e.dma_start(out=Wuk.ap[:, 8192:16384].rearrange("p (mb c) -> p mb c", mb=8),
                                         in_=WAin[16:24].rearrange("mb p c -> p mb c")), [], [Wuk.t], dma=True)
        Wvh = [abf(8 * 512) for _ in range(2)]
        for hf in range(2):
            load_WB(Wvh[hf], WBv, 8, hf * 512, 512)
        xts = [alloc(1024) for _ in range(2)]
        xn = abf(1024)
        ssb = alloc(4)
        hTs = [abf(8 * 512) for _ in range(2)]
        css = [alloc(512) for _ in range(2)]
        sns = [alloc(512) for _ in range(2)]
        wk = (abf(512), alloc(512), abf(512), alloc(512), alloc(512))
        outb = [abf(512) for _ in range(4)]
        vtb = [abf(1024) for _ in range(2)]
        oi = 0
        for it in range(NT):
            hT = hTs[it % 2]
            cs, sn = css[it % 2], sns[it % 2]
            P.op("sp", lambda e, cs=cs, it=it: e.dma_start(out=cs.ap, in_=ropec_d[:, it * 512:(it + 1) * 512]), [], [cs.t], dma=True)
            P.op("sp", lambda e, sn=sn, it=it: e.dma_start(out=sn.ap, in_=ropes_d[:, it * 512:(it + 1) * 512]), [], [sn.t], dma=True)
            for s in range(4):
                xt = xts[s % 2]
                r0 = tok0 + it * 512 + s * 128
                P.op("sp", lambda e, xt=xt, r0=r0: e.dma_start(out=xt.ap, in_=x_d[r0:r0 + 128, :]), [], [xt.t], dma=True)
                norm_transpose(xt, hT, s, xn, ssb, xn)
            P.op("sp", lambda e, hT=hT, it=it: e.dma_start(out=hT_s[:, it * 512:(it + 1) * 512].rearrange("(kc p) t -> p kc t", p=128),
                                                          in_=hT.ap.rearrange("p (kc t) -> p kc t", kc=8)), [hT.t], [], dma=True)
            for m in range(8):
                pb = proj_fm(Wuk, m, hT)
                ob = outb[oi % 4]; oi += 1
                P.op("act", lambda e, ob=ob, pb=pb: e.copy(out=ob.ap, in_=pb.ap), [pb.t], [ob.t])
                P.op("sp", lambda e, ob=ob, m=m, it=it: e.dma_start(out=uT_s[m * 128:(m + 1) * 128, it * 512:(it + 1) * 512], in_=ob.ap),
                     [ob.t], [], dma=True)
            for m in range(8):
                pb = proj_fm(Wuk, 8 + m, hT)
                ob = outb[oi % 4]; oi += 1
                qknorm_rope(pb, KG, ob.ap, ob.t, cs, sn, wk, ones64)
                P.op("sp", lambda e, ob=ob, m=m, it=it: e.dma_start(out=KT_s[m * 128:(m + 1) * 128, it * 512:(it + 1) * 512], in_=ob.ap),
                     [ob.t], [], dma=True)
            hv = hT.ap.rearrange("p (kc t) -> p kc t", kc=8)
            for s in range(4):
                vt = vtb[s % 2]
                for hf in range(2):
                    pb = nextbank()
                    wv = Wvh[hf].ap.rearrange("p (kc c) -> p kc c", kc=8)
                    for kc in range(8):
                        P.op("pe", lambda e, pb=pb, kc=kc, s=s, wv=wv, hv=hv: e.matmul(pb.ap, hv[:, kc, s * 128:(s + 1) * 128], wv[:, kc, :],
                                                                                     start=(kc == 0), stop=(kc == 7)),
                             [hT.t, Wvh[hf].t], [pb.t])
                    if hf == 0:
                        P.op("act", lambda e, vt=vt, pb=pb: e.copy(out=vt.ap[:, 0:512], in_=pb.ap), [pb.t], [vt.t])
                    else:
                        P.op("dve", lambda e, vt=vt, pb=pb: e.tensor_copy(out=vt.ap[:, 512:1024], in_=pb.ap), [pb.t], [vt.t])
                P.op("sp", lambda e, vt=vt, it=it, s=s: e.dma_start(out=V_s[it * 512 + s * 128:it * 512 + (s + 1) * 128, :], in_=vt.ap),
                     [vt.t], [], dma=True)
        P.barrier()
        if STOP_AFTER == 1:
            return

        astate["off"] = work_mark
        S5Wl = alloc(S5WORDS)
        P.op("sp", lambda e, S5Wl=S5Wl: e.dma_start(out=S5Wl.ap, in_=s5w_s[:, :]), [], [S5Wl.t], dma=True)
        PWl, CPl, W1l, PAl = s5_views(S5Wl)
        wt = S5Wl.t
        T8 = 8
        Lc = L // T8
        NLV = int(math.log2(Lc))
        SAd = [[[abf(L) for _ in range(2)] for _ in range(2)] for _ in range(2)]
        Kb = [[[[alloc(Lc + 2) for _ in range(2)] for _ in range(2)] for _ in range(2)] for _ in range(2)]
        for st_ in range(2):
            for d in range(2):
                for pp in range(2):
                    for ri in range(2):
                        P.op("dve", lambda e, kb=Kb[st_][d][pp][ri]: e.memset(kb.ap, 0.0), [], [Kb[st_][d][pp][ri].t])
        uTj = [abf(L) for _ in range(2)]
        zj = abf(L)
        yj = abf(L)
        tz = [alloc(512) for _ in range(3)]
        ctmp = [alloc(Lc) for _ in range(4)]
        cti = [0]
        GC = 2.0 * math.sqrt(2.0 / math.pi)
        tpn = 512 // Lc if Lc < 512 else 1

        def stt(out, in0, scal, in1, reads, writes):
            P.op("dve", lambda e: e.scalar_tensor_tensor(out=out, in0=in0, scalar=scal, in1=in1, op0=OP.mult, op1=OP.add), reads, writes)

        def Xv(q, d):
            bufs = SAd[q % 2][d]
            return [bufs[ri].ap.rearrange("p (t c) -> p t c", t=T8) for ri in range(2)], [bufs[0].t, bufs[1].t]

        def s_evac(q):
            j, a = q // 4, q % 4
            uj = uTj[j % 2]
            if a == 0:
                P.op("sp", lambda e: e.dma_start(out=uj.ap, in_=uT_s[j * 128:(j + 1) * 128, 0:L]), [], [uj.t], dma=True)
            for d in range(2):
                for ri in range(2):
                    buf = SAd[q % 2][d][ri]
                    sav = buf.ap.rearrange("p (t c) -> p t c", t=T8)
                    for n in range(NT):
                        pb = nextbank()
                        P.op("pe", lambda e, pb=pb, d=d, ri=ri, n=n: e.matmul(
                            pb.ap, W1l[32 * a:32 * a + 32, d, j, ri, :], uj.ap[32 * a:32 * a + 32, n * 512:(n + 1) * 512],
                            start=True, stop=True, tile_position=(32 * a, 0)), [uj.t, wt], [pb.t])
                        P.op("act", lambda e, pb=pb, sav=sav, n=n: e.copy(out=sav[:, :, n * 64:(n + 1) * 64],
                                                                         in_=pb.ap.rearrange("p (c t) -> p t c", t=T8)), [pb.t], [buf.t])

        def scan_thunks(q, d):
            dq = d * 32 + q
            th = []
            X, tX = Xv(q, d)
            kb = Kb[q % 2][d]
            a1 = PAl[:, dq, 0, :]
            order = list(range(1, T8)) if d == 0 else list(range(T8 - 2, -1, -1))
            for tau in order:
                pv = tau - 1 if d == 0 else tau + 1
                th.append(lambda tau=tau, pv=pv: stt(X[0][:, tau, :], X[0][:, pv, :], a1[:, 0:1], X[0][:, tau, :], [tX[0], wt], [tX[0]]))
                th.append(lambda tau=tau, pv=pv: stt(X[1][:, tau, :], X[0][:, pv, :], a1[:, 1:2], X[1][:, tau, :], [tX[0], tX[1], wt], [tX[1]]))
                th.append(lambda tau=tau, pv=pv: stt(X[0][:, tau, :], X[1][:, pv, :], a1[:, 2:3], X[0][:, tau, :], [tX[0], tX[1], wt], [tX[0]]))
                th.append(lambda tau=tau, pv=pv: stt(X[1][:, tau, :], X[1][:, pv, :], a1[:, 0:1], X[1][:, tau, :], [tX[1], wt], [tX[1]]))
            e_ = T8 - 1 if d == 0 else 0
            cur = [X[0][:, e_, :], X[1][:, e_, :]]
            ctk = list(tX)
            for k in range(NLV):
                s = 1 << k
                nb_ = kb[k % 2]
                nd = [nb_[0].ap[:, 1:Lc + 1], nb_[1].ap[:, 1:Lc + 1]]
                ntk = [nb_[0].t, nb_[1].t]
                pw = PWl[:, dq, k + 3, :]
                if d == 0:
                    hi, lo, keep = slice(s, Lc), slice(0, Lc - s), slice(0, s)
                else:
                    hi, lo, keep = slice(0, Lc - s), slice(s, Lc), slice(Lc - s, Lc)
                for ri in range(2):
                    th.append(lambda ri=ri, nd=nd, cur=cur, keep=keep, ctk=ctk, ntk=ntk: P.op(
                        "dve", lambda e: e.tensor_copy(out=nd[ri][:, keep], in_=cur[ri][:, keep]), [ctk[ri]], [ntk[ri]]))
                th.append(lambda nd=nd, cur=cur, hi=hi, lo=lo, pw=pw, ctk=ctk, ntk=ntk: stt(nd[0][:, hi], cur[0][:, lo], pw[:, 0:1], cur[0][:, hi], [ctk[0], wt], [ntk[0]]))
                th.append(lambda nd=nd, cur=cur, hi=hi, lo=lo, pw=pw, ctk=ctk, ntk=ntk: stt(nd[1][:, hi], cur[0][:, lo], pw[:, 1:2], cur[1][:, hi], [ctk[0], ctk[1], wt], [ntk[1]]))
                th.append(lambda nd=nd, cur=cur, hi=hi, lo=lo, pw=pw, ctk=ctk, ntk=ntk: stt(nd[0][:, hi], cur[1][:, lo], pw[:, 2:3], nd[0][:, hi], [ctk[1], ntk[0], wt], [ntk[0]]))
                th.append(lambda nd=nd, cur=cur, hi=hi, lo=lo, pw=pw, ctk=ctk, ntk=ntk: stt(nd[1][:, hi], cur[1][:, lo], pw[:, 0:1], nd[1][:, hi], [ctk[1], ntk[1], wt], [ntk[1]]))
                cur, ctk = nd, ntk
            return th

        def s_scan(q):
            th0 = scan_thunks(q, 0)
            th1 = scan_thunks(q, 1)
            for i in range(max(len(th0), len(th1))):
                if i < len(th0):
                    th0[i]()
                if i < len(th1):
                    th1[i]()

        def s_carry(q):
            for d in range(2):
                dq = d * 32 + q
                X, tX = Xv(q, d)
                fin = Kb[q % 2][d][(NLV - 1) % 2]
                sh = [fin[ri].ap[:, 0:Lc] if d == 0 else fin[ri].ap[:, 2:Lc + 2] for ri in range(2)]
                ftk = [fin[0].t, fin[1].t]
                e_ = T8 - 1 if d == 0 else 0
                for ri in range(2):
                    P.op("act", lambda e, ri=ri, X=X, fin=fin, e_=e_: e.copy(out=X[ri][:, e_, :], in_=fin[ri].ap[:, 1:Lc + 1]), [ftk[ri]], [tX[ri]])
                taus = list(range(0, T8 - 1)) if d == 0 else list(range(1, T8))
                for tau in taus:
                    m = tau if d == 0 else (T8 - 1 - tau)
                    pw = PAl[:, dq, m, :]
                    for (ri, ca, cb) in ((0, 0, 2), (1, 1, 0)):
                        c1 = ctmp[cti[0] % 4]; c2 = ctmp[(cti[0] + 1) % 4]; cti[0] += 2
                        P.op("act", lambda e, c1=c1, sh=sh, pw=pw, ca=ca: e.activation(out=c1.ap, in_=sh[0], func=AF.Copy, scale=pw[:, ca:ca + 1]), [ftk[0], wt], [c1.t])
                        P.op("act", lambda e, c2=c2, sh=sh, pw=pw, cb=cb: e.activation(out=c2.ap, in_=sh[1], func=AF.Copy, scale=pw[:, cb:cb + 1]), [ftk[1], wt], [c2.t])
                        P.op("pool", lambda e, c1=c1, X=X, ri=ri, tau=tau: e.tensor_tensor(out=c1.ap, in0=c1.ap, in1=X[ri][:, tau, :], op=OP.add), [c1.t, tX[ri]], [c1.t])
                        P.op("pool", lambda e, c1=c1, c2=c2, X=X, ri=ri, tau=tau: e.tensor_tensor(out=X[ri][:, tau, :], in0=c1.ap, in1=c2.ap, op=OP.add), [c1.t, c2.t], [tX[ri]])

        def s_outmm(q):
            a = q % 4
            qs = slice(32 * a, 32 * a + 32)
            for n in range(NT):
                pb = nextbank()
                cnt = 0
                for d in range(2):
                    for ri in range(2):
                        buf = SAd[q % 2][d][ri]
                        P.op("pe", lambda e, pb=pb, d=d, ri=ri, n=n, cnt=cnt, buf=buf: e.matmul(
                            pb.ap[qs, :], CPl[:, d * 32 + q, ri, :], buf.ap[:, n * 512:(n + 1) * 512],
                            start=(cnt == 0), stop=(cnt == 3), tile_position=(0, 32 * a)), [buf.t, wt], [pb.t])
                        cnt += 1
                P.op("act", lambda e, pb=pb, n=n: e.copy(out=yj.ap[qs, n * 512:(n + 1) * 512], in_=pb.ap[qs, :]), [pb.t], [yj.t])

        def z_chunk(j):
            uj = uTj[j % 2]
            t0, t1_, t2_ = tz
            v3 = lambda ap_: ap_.rearrange("p (t c) -> p t c", t=tpn)
            for n in range(NT):
                upv = uj.ap.rearrange("p (c t) -> p t c", t=T8)[:, n * tpn:(n + 1) * tpn, :]
                zpv = zj.ap.rearrange("p (c t) -> p t c", t=T8)[:, n * tpn:(n + 1) * tpn, :]
                P.op("dve", lambda e, n=n, upv=upv: e.scalar_tensor_tensor(
                    out=v3(t0.ap), in0=upv, scalar=sm[:, S5D + j:S5D + j + 1], in1=v3(yj.ap[:, n * 512:(n + 1) * 512]), op0=OP.mult, op1=OP.add),
                    [yj.t, uj.t, smalls.t], [t0.t])
                P.op("act", lambda e: e.activation(out=t1_.ap, in_=t0.ap, func=AF.Square), [t0.t], [t1_.t])
                P.op("dve", lambda e: e.tensor_scalar(t1_.ap, t1_.ap, 0.044715, 1.0, OP.mult, OP.add), [t1_.t], [t1_.t])
                P.op("dve", lambda e: e.tensor_tensor(out=t1_.ap, in0=t1_.ap, in1=t0.ap, op=OP.mult), [t1_.t, t0.t], [t1_.t])
                P.op("act", lambda e: e.activation(out=t2_.ap, in_=t1_.ap, func=AF.Sigmoid, scale=GC), [t1_.t], [t2_.t])
                P.op("dve", lambda e, zpv=zpv: e.tensor_tensor(out=zpv, in0=v3(t0.ap), in1=v3(t2_.ap), op=OP.mult), [t0.t, t2_.t], [zj.t])
            P.op("sp", lambda e: e.dma_start(out=zT_s[j * 128:(j + 1) * 128, 0:L], in_=zj.ap), [zj.t], [], dma=True)

        s_evac(0)
        for i in range(32):
            if i >= 1:
                s_outmm(i - 1)
            if i + 1 < 32:
                s_evac(i + 1)
            s_scan(i)
            if i >= 1 and (i - 1) % 4 == 3:
                z_chunk((i - 1) // 4)
            s_carry(i)
        s_outmm(31)
        z_chunk(7)
        P.barrier()
        astate["off"] = work_mark
        Wg = abf(8 * 1024)
        load_WA(Wg, WAglu, 0, 8)
        zts = [abf(8 * 512) for _ in range(2)]
        sig = alloc(512)
        outb = [abf(512) for _ in range(4)]
        for it in range(NT):
            zt = zts[it % 2]
            P.op("sp", lambda e, zt=zt, it=it: e.dma_start(out=zt.ap.rearrange("p (kc t) -> p kc t", kc=8),
                                                          in_=zT_s[:, it * 512:(it + 1) * 512].rearrange("(kc p) t -> p kc t", p=128)),
                 [], [zt.t], dma=True)
            zv = zt.ap.rearrange("p (kc t) -> p kc t", kc=8)
            for m in range(8):
                pb = proj_fm(Wg, m, zt)
                P.op("act", lambda e, pb=pb: e.activation(out=sig.ap, in_=pb.ap, func=AF.Sigmoid), [pb.t], [sig.t])
                ob = outb[oi % 4]; oi += 1
                P.op("dve", lambda e, ob=ob, m=m, zv=zv: e.tensor_tensor(out=ob.ap, in0=zv[:, m, :], in1=sig.ap, op=OP.mult), [sig.t, zt.t], [ob.t])
                P.op("sp", lambda e, ob=ob, m=m, it=it: e.dma_start(out=s5T_s[m * 128:(m + 1) * 128, it * 512:(it + 1) * 512], in_=ob.ap),
                     [ob.t], [], dma=True)
        P.barrier()
        if STOP_AFTER == 2:
            return

        astate["off"] = work_mark
        hTs = [abf(8 * 512) for _ in range(2)]
        css = alloc(512); sns = alloc(512)
        wk = (abf(512), alloc(512), abf(512), alloc(512), alloc(512))
        wk2 = (abf(512), abf(512), alloc(512))
        NWB = 6
        wblk = [abf(1024) for _ in range(NWB)]
        wbi = [0]

        def load_block(WA, idx):
            b = wblk[wbi[0] % NWB]
            wbi[0] += 1
            P.op("sp", lambda e, b=b: e.dma_start(out=b.ap, in_=WA[idx]), [], [b.t], dma=True)
            return b

        qTb = [abf(512) for _ in range(2)]
        KTh = [abf(L) for _ in range(2)]
        Vh = [abf(L) for _ in range(2)]
        PT = [abf(512) for _ in range(4)]
        pts2 = [abf(512) for _ in range(2)]
        fsq = abf(512)
        fo = wk2[2]
        brT = [abf(8 * 512) for _ in range(3)]
        MkT = abf(8 * 256)
        MV = abf(2 * 1024)
        merged = abf(8 * 512)
        gate = wk[1]; macc = wk2[2]; mtmp = wk[4]
        WBp = [abf(8 * 512) for _ in range(2)]
        x1s = [alloc(1024) for _ in range(2)]
        xn = abf(1024)
        ssb = alloc(4)
        actT = abf(22 * 512)
        ybuf = [wk[3], wk[4]]
        memx = x1s
        brv = [b.ap.rearrange("p (kc t) -> p kc t", kc=8) for b in brT]
        mgv = merged.ap.rearrange("p (kc t) -> p kc t", kc=8)
        MkTv = MkT.ap.rearrange("p (c m) -> p c m", c=8)
        MVv = MV.ap.rearrange("p (s c) -> p s c", s=2)
        actv = actT.ap.rearrange("p (kc t) -> p kc t", kc=22)

        mhT = hTs[0]
        mhv = mhT.ap.rearrange("p (kc t) -> p kc t", kc=8)
        for s in range(2):
            xt = memx[s]
            r0 = si * NMEM + s * 128
            P.op("sp", lambda e, xt=xt, r0=r0: e.dma_start(out=xt.ap, in_=mem_d[r0:r0 + 128, :]), [], [xt.t], dma=True)
            norm_transpose(xt, mhT, s, xn, ssb, xn)
        for hm in range(4):
            pbs = []
            for c in range(2):
                wb = load_block(WAkv, 2 * hm + c)
                pb = nextbank()
                for kc in range(8):
                    P.op("pe", lambda e, pb=pb, wb=wb, kc=kc: e.matmul(pb.ap[:, 0:256], wb.ap[:, kc * 128:(kc + 1) * 128], mhv[:, kc, 0:256],
                                                                      start=(kc == 0), stop=(kc == 7)), [wb.t, mhT.t], [pb.t])
                pbs.append(pb)
            headnorm2(pbs, (MKG, MKG + 1), [(MkTv[:, 2 * hm + c, :], MkT.t) for c in range(2)], wk2)
        for hf in range(2):
            wp = WBp[hf]
            load_WB(wp, WBkv, 8, hf * 512, 512)
            wpv = wp.ap.rearrange("p (kc c) -> p kc c", kc=8)
            for s in range(2):
                pb = nextbank()
                for kc in range(8):
                    P.op("pe", lambda e, pb=pb, kc=kc, s=s, wpv=wpv: e.matmul(pb.ap, mhv[:, kc, s * 128:(s + 1) * 128], wpv[:, kc, :],
                                                                             start=(kc == 0), stop=(kc == 7)), [wp.t, mhT.t], [pb.t])
                P.op("act", lambda e, pb=pb, s=s, hf=hf: e.copy(out=MVv[:, s, hf * 512:(hf + 1) * 512], in_=pb.ap), [pb.t], [MV.t])

        def load_kv(h):
            kth, vh = KTh[h % 2], Vh[h % 2]
            P.op("sp", lambda e: e.dma_start(out=kth.ap, in_=KT_s[h * 128:(h + 1) * 128, 0:L]), [], [kth.t], dma=True)
            P.op("sp", lambda e: e.dma_start(out=vh.ap.rearrange("p (kt c) -> p kt c", kt=NKT),
                                             in_=V_s[0:L, h * 128:(h + 1) * 128].rearrange("(kt p) c -> p kt c", p=128)),
                 [], [vh.t], dma=True)

        def tile_loads(it):
            ts = slice(it * 512, (it + 1) * 512)
            hT = hTs[it % 2]
            P.op("sp", lambda e: e.dma_start(out=hT.ap.rearrange("p (kc t) -> p kc t", kc=8),
                                             in_=hT_s[:, ts].rearrange("(kc p) t -> p kc t", p=128)), [], [hT.t], dma=True)
            P.op("sp", lambda e: e.dma_start(out=css.ap, in_=ropec_d[:, ts]), [], [css.t], dma=True)
            P.op("sp", lambda e: e.dma_start(out=sns.ap, in_=ropes_d[:, ts]), [], [sns.t], dma=True)
            P.op("sp", lambda e: e.dma_start(out=brv[0], in_=s5T_s[:, ts].rearrange("(kc p) t -> p kc t", p=128)), [], [brT[0].t], dma=True)
            load_kv(0)

        tile_loads(0)
        for it in range(NT):
            ts = slice(it * 512, (it + 1) * 512)
            hT = hTs[it % 2]
            h2T = hTs[(it + 1) % 2]
            sq, rstd, ybf, t1, t2 = wk

            def qA(h):
                wbq = load_block(WAin, 8 + h)
                pbq = proj_fm(wbq, 0, hT)
                P.op("act", lambda e: e.activation(out=sq.ap, in_=pbq.ap, func=AF.Square), [pbq.t], [sq.t])
                P.op("dve", lambda e: e.tensor_scalar(ybf.ap, pbq.ap, sm[:, QG:QG + 1], None, OP.mult), [pbq.t, smalls.t, sq.t], [ybf.t])

            def qB(h):
                dst = qTb[h % 2]
                p2 = nextbank()
                P.op("pe", lambda e: e.matmul(p2.ap, ones64, sq.ap, start=True, stop=True), [sq.t, cbf.t], [p2.t])
                p3 = nextbank()
                P.op("pe", lambda e: e.matmul(p3.ap, rotm, ybf.ap, start=True, stop=True), [ybf.t, cbf.t], [p3.t])
                rms_rstd(p2.ap, 64, rstd, [p2.t])
                P.op("pool", lambda e: e.tensor_tensor(out=t1.ap, in0=ybf.ap, in1=css.ap, op=OP.mult), [ybf.t, css.t], [t1.t])
                P.op("dve", lambda e: e.tensor_tensor(out=t2.ap, in0=p3.ap, in1=sns.ap, op=OP.mult), [p3.t, sns.t], [t2.t])
                P.op("dve", lambda e: e.tensor_tensor(out=t1.ap, in0=t1.ap, in1=t2.ap, op=OP.add), [t2.t, t1.t], [t1.t])
                P.op("dve", lambda e: e.tensor_tensor(out=dst.ap, in0=t1.ap, in1=rstd.ap, op=OP.mult), [t1.t, rstd.t], [dst.t])

            def F1(h):
                O1, O2, Z1, Z2 = psb[4], psb[5], psb[6], psb[7]
                P.op("act", lambda e: e.copy(out=t1.ap, in_=Z1.ap), [Z1.t], [t1.t])
                P.op("act", lambda e: e.copy(out=t2.ap, in_=Z2.ap), [Z2.t], [t2.t])
                P.op("act", lambda e: e.copy(out=fo.ap, in_=O1.ap), [O1.t], [fo.t])
                P.op("act", lambda e: e.copy(out=rstd.ap, in_=O2.ap), [O2.t], [rstd.t])
                P.op("dve", lambda e: e.reciprocal(out=t1.ap, in_=t1.ap), [t1.t], [t1.t])
                P.op("dve", lambda e: e.tensor_tensor(out=t1.ap, in0=fo.ap, in1=t1.ap, op=OP.mult), [fo.t, t1.t], [t1.t])
                P.op("dve", lambda e: e.reciprocal(out=t2.ap, in_=t2.ap), [t2.t], [t2.t])
                P.op("dve", lambda e: e.tensor_tensor(out=t2.ap, in0=rstd.ap, in1=t2.ap, op=OP.mult), [rstd.t, t2.t], [t2.t])
                P.op("dve", lambda e: e.scalar_tensor_tensor(out=fo.ap, in0=t2.ap, scalar=NEGLAM, in1=t1.ap, op0=OP.mult, op1=OP.add),
                     [t1.t, t2.t, lamb.t], [fo.t])
                P.op("act", lambda e: e.activation(out=fsq.ap, in_=fo.ap, func=AF.Square), [fo.t], [fsq.t])

            def F2(h):
                p2b = nextbank()
                P.op("pe", lambda e: e.matmul(p2b.ap, ones128, fsq.ap, start=True, stop=True), [fsq.t, cbf.t], [p2b.t])
                rms_rstd(p2b.ap, 128, rstd, [p2b.t])
                P.op("dve", lambda e: e.scalar_tensor_tensor(out=t1.ap, in0=fo.ap, scalar=sm[:, SUBG:SUBG + 1], in1=rstd.ap, op0=OP.mult, op1=OP.mult),
                     [fo.t, rstd.t, smalls.t], [t1.t])
                P.op("act", lambda e: e.activation(out=brv[1][:, h, :], in_=t1.ap, func=AF.Copy, scale=1.0 - LAMBDA_INIT), [t1.t], [brT[1].t])

            qA(0)
            qB(0)
            k1, k3 = min(1, NKT - 1), min(3, NKT - 1)
            for h in range(8):
                kth, vh, qT = KTh[h % 2], Vh[h % 2], qTb[h % 2]
                if h > 0:
                    load_kv(h)
                vhv = vh.ap.rearrange("p (kt c) -> p kt c", kt=NKT)
                O1, O2, Z1, Z2 = psb[4], psb[5], psb[6], psb[7]
                stages = {}
                if h + 1 < 8:
                    stages.setdefault(k1, []).append(lambda h=h: qA(h + 1))
                    stages.setdefault(k3 if SPLIT_Q else k1, []).append(lambda h=h: qB(h + 1))
                if h >= 1 and SPLIT_F:
                    stages.setdefault(k3, []).append(lambda h=h: F2(h - 1))

                def emit_scores(kt):
                    ks = slice(kt * 128, (kt + 1) * 128)
                    pS1 = nextbank(); pS2 = nextbank()
                    P.op("pe", lambda e, pS1=pS1, ks=ks, kth=kth, qT=qT: e.matmul(pS1.ap, kth.ap[0:64, ks], qT.ap[0:64, :], start=True, stop=True,
                                                                               tile_position=(0, 0)), [kth.t, qT.t], [pS1.t])
                    P.op("pe", lambda e, pS2=pS2, ks=ks, kth=kth, qT=qT: e.matmul(pS2.ap, kth.ap[64:128, ks], qT.ap[64:128, :], start=True, stop=True,
                                                                               tile_position=(64, 0)), [kth.t, qT.t], [pS2.t])
                    return pS1, pS2

                nxt_sc = emit_scores(0)
                for kt in range(NKT):
                    pS1, pS2 = nxt_sc
                    ins_here = stages.get(kt)
                    if kt + 1 < NKT and not ins_here:
                        nxt_sc = emit_scores(kt + 1)
                    p1, p2 = PT[(2 * kt) % 4], PT[(2 * kt + 1) % 4]
                    P.op("act", lambda e, pS1=pS1, p1=p1: e.activation(out=p1.ap, in_=pS1.ap, func=AF.Exp, scale=0.125), [pS1.t], [p1.t])
                    P.op("act", lambda e, pS2=pS2, p2=p2: e.activation(out=p2.ap, in_=pS2.ap, func=AF.Exp, scale=0.125), [pS2.t], [p2.t])
                    st, sp_ = (kt == 0), (kt == NKT - 1)
                    P.op("pe", lambda e, p1=p1, kt=kt, st=st, sp_=sp_, vhv=vhv: e.matmul(O1.ap, vhv[:, kt, :], p1.ap, start=st, stop=sp_), [vh.t, p1.t], [O1.t])
                    P.op("pe", lambda e, p1=p1, st=st, sp_=sp_: e.matmul(Z1.ap, ones128, p1.ap, start=st, stop=sp_), [cbf.t, p1.t], [Z1.t])
                    P.op("pe", lambda e, p2=p2, kt=kt, st=st, sp_=sp_, vhv=vhv: e.matmul(O2.ap, vhv[:, kt, :], p2.ap, start=st, stop=sp_), [vh.t, p2.t], [O2.t])
                    P.op("pe", lambda e, p2=p2, st=st, sp_=sp_: e.matmul(Z2.ap, ones128, p2.ap, start=st, stop=sp_), [cbf.t, p2.t], [Z2.t])
                    if ins_here:
                        for f in ins_here:
                            f()
                        if kt + 1 < NKT:
                            nxt_sc = emit_scores(kt + 1)
                F1(h)
                if not SPLIT_F:
                    F2(h)
            if SPLIT_F:
                F2(7)
            def mem_A(hm):
                pbs = []
                for c in range(2):
                    wb = load_block(WAin, 32 + 2 * hm + c)
                    pbs.append(proj_fm(wb, 0, hT))
                qa, qb = PT[2 * (hm % 2)], PT[2 * (hm % 2) + 1]
                headnorm2(pbs, (MQG, MQG + 1), [(qa.ap, qa.t), (qb.ap, qb.t)], wk2)

            def mem_B(hm):
                qa, qb = PT[2 * (hm % 2)], PT[2 * (hm % 2) + 1]
                pts = pts2
                for mt in range(2):
                    pS = nextbank()
                    ms = slice(mt * 128, (mt + 1) * 128)
                    P.op("pe", lambda e, pS=pS, ms=ms: e.matmul(pS.ap, MkTv[:, 2 * hm, ms], qa.ap, start=True, stop=False), [MkT.t, qa.t], [pS.t])
                    P.op("pe", lambda e, pS=pS, ms=ms: e.matmul(pS.ap, MkTv[:, 2 * hm + 1, ms], qb.ap, start=False, stop=True), [MkT.t, qb.t], [pS.t])
                    P.op("act", lambda e, pS=pS, mt=mt: e.activation(out=pts[mt].ap, in_=pS.ap, func=AF.Exp, scale=1.0 / 16.0), [pS.t], [pts[mt].t])
                pZ = nextbank()
                P.op("pe", lambda e: e.matmul(pZ.ap, ones128, pts[0].ap, start=True, stop=False), [cbf.t, pts[0].t], [pZ.t])
                P.op("pe", lambda e: e.matmul(pZ.ap, ones128, pts[1].ap, start=False, stop=True), [cbf.t, pts[1].t], [pZ.t])
                rz = wk[3]
                act_recip(rz, pZ.ap, [pZ.t])
                for ec in range(2):
                    pO = nextbank()
                    es = slice((2 * hm + ec) * 128, (2 * hm + ec + 1) * 128)
                    P.op("pe", lambda e, pO=pO, es=es: e.matmul(pO.ap, MVv[:, 0, es], pts[0].ap, start=True, stop=False), [MV.t, pts[0].t], [pO.t])
                    P.op("pe", lambda e, pO=pO, es=es: e.matmul(pO.ap, MVv[:, 1, es], pts[1].ap, start=False, stop=True), [MV.t, pts[1].t], [pO.t])
                    P.op("dve", lambda e, pO=pO, ec=ec: e.tensor_tensor(out=brv[2][:, 2 * hm + ec, :], in0=pO.ap, in1=rz.ap, op=OP.mult),
                         [pO.t, rz.t], [brT[2].t])

            mem_A(0)
            for hm in range(4):
                if hm + 1 < 4:
                    mem_A(hm + 1)
                mem_B(hm)
            for m in range(8):
                for nb in range(3):
                    wg = load_block(WAin, 40 + 8 * nb + m)
                    pG = proj_fm(wg, 0, hT)
                    P.op("act", lambda e, pG=pG, nb=nb, m=m: e.activation(out=gate.ap, in_=pG.ap, func=AF.Sigmoid,
                                                                        bias=sm[:, B_GATE + 8 * nb + m:B_GATE + 8 * nb + m + 1]),
                         [pG.t, smalls.t], [gate.t])
                    wbb = load_block(WAbr, 8 * nb + m)
                    pB = proj_fm(wbb, 0, brT[nb])
                    if nb == 0:
                        P.op("dve", lambda e, pB=pB: e.tensor_tensor(out=macc.ap, in0=pB.ap, in1=gate.ap, op=OP.mult), [pB.t, gate.t], [macc.t])
                    else:
                        P.op("dve", lambda e, pB=pB: e.tensor_tensor(out=mtmp.ap, in0=pB.ap, in1=gate.ap, op=OP.mult), [pB.t, gate.t], [mtmp.t])
                        if nb == 1:
                            P.op("pool", lambda e: e.tensor_tensor(out=macc.ap, in0=macc.ap, in1=mtmp.ap, op=OP.add), [macc.t, mtmp.t], [macc.t])
                        else:
                            P.op("pool", lambda e, m=m: e.tensor_tensor(out=mgv[:, m, :], in0=macc.ap, in1=mtmp.ap, op=OP.add), [macc.t, mtmp.t], [merged.t])
            for hf in range(2):
                load_WB(WBp[hf], WBout, 8, hf * 512, 512)
            ytoks = [Tok() for _ in range(4)]

            def op_mm(s):
                x1 = x1s[s % 2]
                r0 = tok0 + it * 512 + s * 128
                P.op("sp", lambda e: e.dma_start(out=x1.ap, in_=x_d[r0:r0 + 128, :]), [], [x1.t], dma=True)
                for hf in range(2):
                    pb = nextbank()
                    wpv = WBp[hf].ap.rearrange("p (kc c) -> p kc c", kc=8)
                    for kc in range(8):
                        P.op("pe", lambda e, pb=pb, kc=kc, wpv=wpv: e.matmul(pb.ap, mgv[:, kc, s * 128:(s + 1) * 128], wpv[:, kc, :],
                                                                          start=(kc == 0), stop=(kc == 7)), [merged.t, WBp[hf].t], [pb.t])
                    P.op("dve", lambda e, pb=pb, hf=hf: e.tensor_tensor(out=x1.ap[:, hf * 512:(hf + 1) * 512], in0=pb.ap,
                                                                      in1=x1.ap[:, hf * 512:(hf + 1) * 512], op=OP.add), [pb.t, x1.t], [x1.t])

            def op_post(s):
                x1 = x1s[s % 2]
                r0 = tok0 + it * 512 + s * 128
                norm_transpose(x1, h2T, s, xn, ssb, xn)
                P.op("sp", lambda e: e.dma_start(out=y_d[r0:r0 + 128, :], in_=x1.ap), [x1.t], [ytoks[s]], dma=True)

            op_mm(0)
            for s in range(4):
                if s + 1 < 4:
                    op_mm(s + 1)
                op_post(s)
            for m in range(22):
                wg = load_block(WAgu, m)
                pg = proj_fm(wg, 0, h2T)
                wu = load_block(WAgu, 22 + m)
                pu = proj_fm(wu, 0, h2T)
                P.op("act", lambda e, pg=pg: e.activation(out=gate.ap, in_=pg.ap, func=AF.Silu), [pg.t], [gate.t])
                P.op("dve", lambda e, pu=pu, m=m: e.tensor_tensor(out=actv[:, m, :], in0=pu.ap, in1=gate.ap, op=OP.mult), [pu.t, gate.t], [actT.t])
            if it + 1 < NT:
                tile_loads(it + 1)
            pi = 0
            for hf in range(2):
                groups = [(0, 8), (8, 16), (16, 22)]
                for (g0, g1) in groups:
                    wp = WBp[pi % 2]; pi += 1
                    nk = g1 - g0
                    P.op("sp", lambda e, wp=wp, g0=g0, g1=g1, nk=nk, hf=hf: e.dma_start(
                        out=wp.ap[:, 0:nk * 512].rearrange("p (kc c) -> p kc c", kc=nk),
                        in_=WBdown[g0 * 128:g1 * 128, hf * 512:(hf + 1) * 512].rearrange("(kc p) c -> p kc c", p=128)), [], [wp.t], dma=True)
                    wpv = wp.ap[:, 0:nk * 512].rearrange("p (kc c) -> p kc c", kc=nk)
                    for s in range(4):
                        acc = psb[4 + s] if hf == 0 else psb[s]
                        for kc in range(g0, g1):
                            P.op("pe", lambda e, acc=acc, kc=kc, g0=g0, s=s, wpv=wpv: e.matmul(acc.ap, actv[:, kc, s * 128:(s + 1) * 128], wpv[:, kc - g0, :],
                                                                                             start=(kc == 0), stop=(kc == 21)), [actT.t, wp.t], [acc.t])
                for s in range(4):
                    yb = ybuf[s % 2]
                    r0 = tok0 + it * 512 + s * 128
                    cs_ = slice(hf * 512, (hf + 1) * 512)
                    acc = psb[4 + s] if hf == 0 else psb[s]
                    P.op("sp", lambda e, yb=yb, r0=r0, cs_=cs_: e.dma_start(out=yb.ap, in_=y_d[r0:r0 + 128, cs_]), [ytoks[s]], [yb.t], dma=True)
                    P.op("dve", lambda e, yb=yb, acc=acc: e.tensor_tensor(out=yb.ap, in0=acc.ap, in1=yb.ap, op=OP.add), [acc.t, yb.t], [yb.t])
                    P.op("sp", lambda e, yb=yb, r0=r0, cs_=cs_: e.dma_start(out=y_d[r0:r0 + 128, cs_], in_=yb.ap), [yb.t], [ytoks[s]], dma=True)
        P.barrier()


    tok0 = 0
    for si, L in enumerate(seqLs):
        do_seq(si, L, tok0)
        tok0 += L

    P.prepare(nc)
    with nc.Block() as block:
        P.emit(nc, block)
    return nc


def _host_consts():
    bf = ml_dtypes.bfloat16
    cb = np.zeros((128, 512), np.float32)
    cb[:, 0:128] = np.eye(128)
    rot = np.zeros((128, 128), np.float32)
    for b in (0, 64):
        for d in range(32):
            rot[b + d + 32, b + d] = -1.0
            rot[b + d, b + d + 32] = 1.0
    cb[:, 128:256] = rot
    o64 = np.zeros((128, 128), np.float32)
    o64[0:64, 0:64] = 1.0
    o64[64:128, 64:128] = 1.0
    cb[:, 256:384] = o64
    cb[:, 384:512] = 1.0
    half = 32
    inv = (np.float32(10000.0) ** (-(np.arange(half, dtype=np.float32) / np.float32(half)))).astype(np.float32)
    ang = (np.arange(4096, dtype=np.float32)[:, None] * inv[None, :]).astype(np.float32)
    cos = np.cos(ang).astype(np.float32)
    sin = np.sin(ang).astype(np.float32)
    idx = np.arange(128) % 32
    ropec = np.ascontiguousarray(cos[:, idx].T)
    ropes = np.ascontiguousarray(sin[:, idx].T)
    return cb.astype(bf), ropec, ropes


def _host_params(inp):
    f = lambda k: np.asarray(inp[k], np.float32)
    sm = np.zeros((128, 320), np.float32)
    col = lambda v, n: np.ascontiguousarray(v.reshape(n, 128).T)
    sm[:, 0:8] = col(f("norm_mix_g")[0], 8)
    sm[:, 8:16] = col(f("ffn_norm_g")[0], 8)
    sm[:, 16:24] = col(f("mem_norm_g")[0], 8)
    sm[:, 24:48] = col(f("b_gate")[0], 24)
    sm[:, 48:56] = col(f("s5_d")[0], 8)
    p = np.arange(128)
    sm[:, 56] = f("diff_q_g")[0][p % 64]
    sm[:, 57] = f("diff_k_g")[0][p % 64]
    sm[:, 58] = f("diff_sub_g")[0]
    sm[:, 59:61] = col(f("mem_q_g")[0], 2)
    sm[:, 61:63] = col(f("mem_k_g")[0], 2)
    for i, k in enumerate(("diff_lq1", "diff_lk1", "diff_lq2", "diff_lk2")):
        sm[:, 63 + 64 * i:63 + 64 * (i + 1)] = f(k)[0][None, :]
    def ps(a):
        a = a.reshape((2, 32, 2, 64) + a.shape[3:])
        a = np.moveaxis(a, (2, 3), (0, 1))
        return np.ascontiguousarray(a.reshape((128, 64) + a.shape[4:]))
    lre = ps(f("s5_lam_re")[0])
    lim = ps(f("s5_lam_im")[0])
    ldt = ps(np.broadcast_to(f("s5_log_dt")[0][:, :, None], (2, 64, 64)).copy())
    bre = ps(f("s5_b_re")[0])
    bim = ps(f("s5_b_im")[0])
    cre = ps(np.swapaxes(f("s5_c_re")[0], 2, 3).copy())
    cim = ps(np.swapaxes(f("s5_c_im")[0], 2, 3).copy())
    Bst = np.stack([bre, bim], axis=2).reshape(128, 2048)
    Cst = np.stack([cre, cim], axis=2).reshape(128, 2048)
    s5p = np.concatenate([lre, lim, ldt, Bst, Cst], axis=1).astype(np.float32)
    return sm, np.ascontiguousarray(s5p)


_NC_CACHE = {}


def _get_nc(seqLs, **kw):
    key = (tuple(seqLs), tuple(sorted(kw.items())))
    if key not in _NC_CACHE:
        _NC_CACHE[key] = build(list(seqLs), **kw)
    return _NC_CACHE[key]


def _weights_map(inp):
    f = lambda k: np.ascontiguousarray(np.asarray(inp[k], np.float32))
    sm, s5p = _host_params(inp)
    cb, ropec, ropes = _host_consts()
    return {
        "w_in": f("w_in")[0], "w_glu": f("s5_w_glu")[0], "w_br": f("w_branch")[0].reshape(3072, 1024),
        "w_out": f("w_out")[0], "w_gu": f("w_gate_up")[0], "w_down": f("w_down")[0], "w_kv": f("w_mem_kv")[0],
        "smalls": sm, "s5p": s5p, "cbf": cb, "ropec": ropec, "ropes": ropes,
    }


def kernel(**inp):
    xp = np.asarray(inp["x_prompt"], np.float32)
    xs = np.asarray(inp["x_sample"], np.float32)
    mp = np.asarray(inp["mem_prompt"], np.float32)
    ms = np.asarray(inp["mem_sample"], np.float32)
    seqLs = [xp.shape[1]] * 2 + [xs.shape[1]] * 2
    nc = _get_nc(seqLs)
    wm = _weights_map(inp)
    in_maps = []
    for c in range(8):
        x = np.concatenate([xp[2 * c], xp[2 * c + 1], xs[2 * c], xs[2 * c + 1]], axis=0)
        mem = np.concatenate([mp[2 * c], mp[2 * c + 1], ms[2 * c], ms[2 * c + 1]], axis=0)
        m = dict(wm)
        m["x"] = np.ascontiguousarray(x)
        m["mem"] = np.ascontiguousarray(mem)
        in_maps.append(m)
    res = run_bass_kernel_spmd(nc, in_maps, core_ids=list(range(8)))
    yp = np.empty_like(xp)
    ys = np.empty_like(xs)
    Lp, Ls = xp.shape[1], xs.shape[1]
    for c in range(8):
        y = np.asarray(res.results[c]["y"], np.float32)
        yp[2 * c] = y[0:Lp]
        yp[2 * c + 1] = y[Lp:2 * Lp]
        ys[2 * c] = y[2 * Lp:2 * Lp + Ls]
        ys[2 * c + 1] = y[2 * Lp + Ls:2 * Lp + 2 * Ls]
    return (yp, ys)
```

```python
import math
import numpy as np
import ml_dtypes
import concourse.bass as bass
import concourse.mybir as mybir
from concourse.bass_utils import run_bass_kernel_spmd

F32 = mybir.dt.float32
BF = mybir.dt.bfloat16
I32 = mybir.dt.int32
AF = mybir.ActivationFunctionType
OP = mybir.AluOpType

D = 1024
NMEM = 256
EPS = 1e-6
LAMBDA_INIT = 0.8 - 0.6 * math.exp(-0.3 * 0)
TWO_PI = 2.0 * math.pi
KDMA = 8

SPLIT_Q = True
SPLIT_F = True


class Tok:
    __slots__ = ("w", "r", "dr")

    def __init__(self):
        self.w = None
        self.r = {}
        self.dr = []


class Prog:
    def __init__(self):
        self.ops = []
        self.last = {}
        self.dma_since = []

    def op(self, eng, fn, reads=(), writes=(), dma=False):
        deps = set()
        for b in reads:
            if b.w is not None:
                deps.add(b.w)
        for b in writes:
            if b.w is not None:
                deps.add(b.w)
            deps.update(b.r.values())
            deps.update(b.dr)
        i = len(self.ops)
        self.ops.append((eng, fn, deps, dma))
        for b in reads:
            if dma:
                b.dr.append(i)
            else:
                b.r[eng] = i
        for b in writes:
            b.w = i
            b.r = {}
            b.dr = []
        self.last[eng] = i
        if dma:
            self.dma_since.append(i)
        return i

    def barrier(self):
        deps = set(self.last.values()) | set(self.dma_since)
        for eng in ("pe", "act", "dve", "pool", "sp"):
            self.ops.append((eng, None, set(deps), False))
        self.dma_since = []

    def prepare(self, nc):
        self.csem = {e: nc.semaphore("cs_" + e).__enter__() for e in ("pe", "act", "dve", "pool")}
        self.dsem = [nc.semaphore("ds%d" % i).__enter__() for i in range(KDMA)]

    def emit(self, nc, block):
        ops = self.ops
        n = len(ops)
        needed = [False] * n
        for j in range(n):
            ej = ops[j][0]
            for k in ops[j][2]:
                if ops[k][0] == "pe" and ej == "pe" and not ops[k][3]:
                    continue
                needed[k] = True
        csem, dsem = self.csem, self.dsem
        semof = [None] * n
        cnt = {e: 0 for e in csem}
        dcount = 0
        dma_prev = {}
        for j in range(n):
            eng, fn, deps, dma = ops[j]
            if fn is None:
                continue
            if dma:
                s = dsem[dcount % KDMA]
                semof[j] = (s, 16 * (dcount // KDMA + 1), dcount)
                dcount += 1
            elif needed[j]:
                cnt[eng] += 1
                semof[j] = (csem[eng], cnt[eng], None)
        streams = {e: [] for e in ("pe", "act", "dve", "pool", "sp")}
        for j in range(n):
            streams[ops[j][0]].append(j)
        final_dma = {}
        for j in range(n):
            if ops[j][3] and ops[j][1] is not None:
                s, v, _ = semof[j]
                final_dma[id(s)] = (s, v)

        def run(engname, e):
            waited = {}
            for j in streams[engname]:
                eng, fn, deps, dma = ops[j]
                need = {}
                for k in deps:
                    if ops[k][1] is None:
                        continue
                    if ops[k][0] == "pe" and eng == "pe" and not ops[k][3]:
                        continue
                    sk = semof[k]
                    if sk is None:
                        continue
                    s, v, _ = sk
                    if need.get(id(s), (None, 0))[1] < v:
                        need[id(s)] = (s, v)
                if dma and fn is not None:
                    s, v, idx = semof[j]
                    if idx >= KDMA:
                        pv = v - 16
                        if need.get(id(s), (None, 0))[1] < pv:
                            need[id(s)] = (s, pv)
                for sid, (s, v) in need.items():
                    if waited.get(sid, 0) < v:
                        e.wait_ge(s, v)
                        waited[sid] = v
                if fn is None:
                    continue
                inst = fn(e)
                if semof[j] is not None:
                    inst.then_inc(semof[j][0], 16 if dma else 1)
            if engname == "sp":
                for sid, (s, v) in final_dma.items():
                    if waited.get(sid, 0) < v:
                        e.wait_ge(s, v)

        @block.tensor
        def _(e):
            run("pe", e)

        @block.scalar
        def _(e):
            run("act", e)

        @block.vector
        def _(e):
            run("dve", e)

        @block.gpsimd
        def _(e):
            run("pool", e)

        @block.sync
        def _(e):
            run("sp", e)


class Buf:
    def __init__(self, ap):
        self.ap = ap
        self.t = Tok()


def build(seqLs, STOP_AFTER=99, DEBUG=False):
    NS = len(seqLs)
    NTOK = sum(seqLs)
    LMAX = max(seqLs)
    nc = bass.Bass("TRN2", target_bir_lowering=False)

    def dram(name, shape, dtype, kind):
        return nc.dram_tensor(name, shape, dtype, kind=kind).ap()

    x_d = dram("x", [NTOK, D], F32, "ExternalInput")
    mem_d = dram("mem", [NS * NMEM, D], F32, "ExternalInput")
    y_d = dram("y", [NTOK, D], F32, "ExternalOutput")
    w_in_d = dram("w_in", [D, 8192], F32, "ExternalInput")
    w_glu_d = dram("w_glu", [D, D], F32, "ExternalInput")
    w_br_d = dram("w_br", [3 * D, D], F32, "ExternalInput")
    w_out_d = dram("w_out", [D, D], F32, "ExternalInput")
    w_gu_d = dram("w_gu", [D, 5632], F32, "ExternalInput")
    w_down_d = dram("w_down", [2816, D], F32, "ExternalInput")
    w_kv_d = dram("w_kv", [D, 2048], F32, "ExternalInput")
    smalls_d = dram("smalls", [128, 320], F32, "ExternalInput")
    s5p_d = dram("s5p", [128, 64 * 67], F32, "ExternalInput")
    cbf_d = dram("cbf", [128, 512], BF, "ExternalInput")
    ropec_d = dram("ropec", [128, 4096], F32, "ExternalInput")
    ropes_d = dram("ropes", [128, 4096], F32, "ExternalInput")

    WAin = dram("WAin", [64, 128, 1024], BF, "Internal")
    WAglu = dram("WAglu", [8, 128, 1024], BF, "Internal")
    WAbr = dram("WAbr", [24, 128, 1024], BF, "Internal")
    WAgu = dram("WAgu", [44, 128, 1024], BF, "Internal")
    WAkv = dram("WAkv", [8, 128, 1024], BF, "Internal")
    WBv = dram("WBv", [D, D], BF, "Internal")
    WBout = dram("WBout", [D, D], BF, "Internal")
    WBdown = dram("WBdown", [2816, D], BF, "Internal")
    WBkv = dram("WBkv", [D, D], BF, "Internal")
    hT_s = dram("hT_s", [D, LMAX], BF, "ExternalOutput" if DEBUG else "Internal")
    uT_s = dram("uT_s", [D, LMAX], BF, "ExternalOutput" if DEBUG else "Internal")
    zT_s = dram("zT_s", [D, LMAX], BF, "ExternalOutput" if DEBUG else "Internal")
    s5T_s = dram("s5T_s", [D, LMAX], BF, "ExternalOutput" if DEBUG else "Internal")
    KT_s = dram("KT_s", [D, LMAX], BF, "ExternalOutput" if DEBUG else "Internal")
    V_s = dram("V_s", [LMAX, D], BF, "ExternalOutput" if DEBUG else "Internal")
    s5w_s = dram("s5w_s", [128, 7936], F32, "ExternalOutput" if DEBUG else "Internal")

    P = Prog()
    ARENA = 45000
    arena_cm = nc.sbuf_tensor("arena", [128, ARENA], F32)
    arena = arena_cm.__enter__()
    ps_cms = [nc.psum_tensor("ps%d" % i, [128, 512], F32) for i in range(8)]
    psb = [Buf(c.__enter__()[:, :]) for c in ps_cms]
    astate = {"off": 0}

    def alloc(words, dtype=F32, shape=None):
        o = astate["off"]
        assert o + words <= ARENA, ("arena overflow", o, words)
        astate["off"] = o + words
        ap = arena[:, o:o + words]
        if dtype != F32:
            ap = ap.bitcast(dtype)
        return Buf(ap)

    def abf(cols):
        return alloc((cols + 1) // 2, BF)

    smalls = alloc(320)
    cbf = abf(512)
    P.op("sp", lambda e: e.dma_start(out=smalls.ap, in_=smalls_d[:, :]), [], [smalls.t], dma=True)
    P.op("sp", lambda e: e.dma_start(out=cbf.ap, in_=cbf_d[:, :]), [], [cbf.t], dma=True)
    ident = cbf.ap[:, 0:128]
    rotm = cbf.ap[:, 128:256]
    ones64 = cbf.ap[:, 256:384]
    ones128 = cbf.ap[:, 384:512]
    G_MIX, G_FFN, G_MEM, B_GATE, S5D = 0, 8, 16, 24, 48
    QG, KG, SUBG, MQG, MKG, LQ = 56, 57, 58, 59, 61, 63
    sm = smalls.ap
    epsb = alloc(2)
    P.op("dve", lambda e: e.memset(epsb.ap[:, 0:1], EPS), [], [epsb.t])
    lamb = alloc(8)

    def emit_lambda():
        a = lamb.ap
        P.op("dve", lambda e: e.memset(a[:, 0:8], 0.0), [], [lamb.t])
        tmp = alloc(64)
        for i in range(2):
            P.op("dve", lambda e, i=i: e.tensor_tensor(out=tmp.ap, in0=sm[:, LQ + 128 * i:LQ + 128 * i + 64],
                                                    in1=sm[:, LQ + 128 * i + 64:LQ + 128 * i + 128], op=OP.mult),
                 [smalls.t, lamb.t], [tmp.t])
            P.op("dve", lambda e, i=i: e.tensor_reduce(out=a[:, 1 + i:2 + i], in_=tmp.ap, axis=mybir.AxisListType.X, op=OP.add),
                 [tmp.t], [lamb.t])
        P.op("act", lambda e: e.activation(out=a[:, 3:5], in_=a[:, 1:3], func=AF.Exp), [lamb.t], [lamb.t])
        P.op("dve", lambda e: e.tensor_tensor(out=a[:, 5:6], in0=a[:, 4:5], in1=a[:, 3:4], op=OP.subtract), [lamb.t], [lamb.t])
        P.op("dve", lambda e: e.tensor_scalar(a[:, 0:1], a[:, 5:6], -LAMBDA_INIT, None, OP.add), [lamb.t], [lamb.t])

    emit_lambda()
    NEGLAM = lamb.ap[:, 0:1]

    persist_mark = astate["off"]

    stg = [alloc(2048) for _ in range(2)]
    stb = [abf(2048) for _ in range(2)]
    cast_i = [0]

    def cast_weight(src, nk, ncols, dstA=None, dstB=None, gcol=None, colsA=None, colsB=None):
        for kc in range(nk):
            for c0 in range(0, ncols, 2048):
                cw = min(2048, ncols - c0)
                i = cast_i[0]
                cast_i[0] += 1
                sg, sb = stg[i % 2], stb[i % 2]
                P.op("sp", lambda e, sg=sg, kc=kc, c0=c0, cw=cw: e.dma_start(out=sg.ap[:, 0:cw], in_=src[kc * 128:(kc + 1) * 128, c0:c0 + cw]),
                     [], [sg.t], dma=True)
                if gcol is not None:
                    gap = sm[:, gcol + kc:gcol + kc + 1]
                    if i % 2 == 0:
                        P.op("act", lambda e, sg=sg, sb=sb, cw=cw, gap=gap: e.activation(out=sb.ap[:, 0:cw], in_=sg.ap[:, 0:cw], func=AF.Copy, scale=gap),
                             [sg.t, smalls.t], [sb.t])
                    else:
                        P.op("dve", lambda e, sg=sg, sb=sb, cw=cw, gap=gap: e.tensor_scalar(sb.ap[:, 0:cw], sg.ap[:, 0:cw], gap, None, OP.mult),
                             [sg.t, smalls.t], [sb.t])
                else:
                    if i % 2 == 0:
                        P.op("act", lambda e, sg=sg, sb=sb, cw=cw: e.copy(out=sb.ap[:, 0:cw], in_=sg.ap[:, 0:cw]), [sg.t], [sb.t])
                    else:
                        P.op("dve", lambda e, sg=sg, sb=sb, cw=cw: e.tensor_copy(out=sb.ap[:, 0:cw], in_=sg.ap[:, 0:cw]), [sg.t], [sb.t])
                for (lo, hi, dA, mb0) in (colsA or []):
                    a0, a1 = max(lo, c0), min(hi, c0 + cw)
                    if a0 >= a1:
                        continue
                    dview = dA.rearrange("mb p (kc m) -> p mb kc m", kc=8)[:, mb0 + (a0 - lo) // 128:mb0 + (a1 - lo) // 128, kc, :]
                    sview = sb.ap[:, a0 - c0:a1 - c0].rearrange("p (mb m) -> p mb m", m=128)
                    P.op("sp", lambda e, dview=dview, sview=sview: e.dma_start(out=dview, in_=sview), [sb.t], [], dma=True)
                for (lo, hi, dB) in (colsB or []):
                    a0, a1 = max(lo, c0), min(hi, c0 + cw)
                    if a0 >= a1:
                        continue
                    P.op("sp", lambda e, dB=dB, kc=kc, a0=a0, a1=a1, lo=lo, sb=sb, c0=c0: e.dma_start(
                        out=dB[kc * 128:(kc + 1) * 128, a0 - lo:a1 - lo], in_=sb.ap[:, a0 - c0:a1 - c0]), [sb.t], [], dma=True)

    cast_weight(w_in_d, 8, 8192, gcol=G_MIX, colsA=[(0, 8192, WAin, 0)], colsB=[(3072, 4096, WBv)])
    cast_weight(w_glu_d, 8, 1024, colsA=[(0, 1024, WAglu, 0)])
    for nb in range(3):
        cast_weight(w_br_d[nb * 1024:(nb + 1) * 1024, :], 8, 1024, colsA=[(0, 1024, WAbr, 8 * nb)])
    cast_weight(w_out_d, 8, 1024, colsB=[(0, 1024, WBout)])
    cast_weight(w_gu_d, 8, 5632, gcol=G_FFN, colsA=[(0, 5632, WAgu, 0)])
    cast_weight(w_down_d, 22, 1024, colsB=[(0, 1024, WBdown)])
    cast_weight(w_kv_d, 8, 2048, gcol=G_MEM, colsA=[(0, 1024, WAkv, 0)], colsB=[(1024, 2048, WBkv)])

    s5raw = alloc(64 * 67)
    P.op("sp", lambda e: e.dma_start(out=s5raw.ap, in_=s5p_d[:, :]), [], [s5raw.t], dma=True)
    raw = s5raw.ap
    LRE = raw[:, 0:64]
    LIM = raw[:, 64:128]
    LDT = raw[:, 128:192]
    Braw = raw[:, 192:192 + 2048].rearrange("p (q r h) -> p q r h", q=64, r=2)
    Craw = raw[:, 2240:2240 + 2048].rearrange("p (q r h) -> p q r h", q=64, r=2)
    S5WORDS = 2304 + 2048 + 2048 + 1536

    def s5_views(buf):
        ap = buf.ap
        pwv = ap[:, 0:2304].rearrange("p (q k c) -> p q k c", q=64, k=12)
        cpv = ap[:, 2304:4352].bitcast(BF).rearrange("p (q r c) -> p q r c", q=64, r=2)
        w1v = ap[:, 4352:6400].bitcast(BF).rearrange("p (d j r m) -> p d j r m", d=2, j=8, r=2)
        pav = ap[:, 6400:7936].rearrange("p (q k c) -> p q k c", q=64, k=8)
        return pwv, cpv, w1v, pav

    S5W = alloc(S5WORDS)
    PWv, CPv, W1v, PAv = s5_views(S5W)
    CPflat = S5W.ap[:, 2304:4352].bitcast(BF)
    BPb = abf(64 * 2 * 32)
    BPv = BPb.ap.rearrange("p (q r c) -> p q r c", q=64, r=2)
    s5_mark = astate["off"]
    tl = [alloc(64) for _ in range(16)]
    tint = alloc(64, I32)
    T = [t.ap for t in tl]
    s5t = Tok()

    def dv(fn):
        P.op("dve", fn, [s5raw.t, s5t], [s5t])

    def ac(fn):
        P.op("act", fn, [s5raw.t, s5t], [s5t])

    ac(lambda e: e.activation(out=T[0], in_=LDT, func=AF.Exp))
    dv(lambda e: e.tensor_tensor(out=T[1], in0=LRE, in1=T[0], op=OP.mult))
    ac(lambda e: e.activation(out=T[1], in_=T[1], func=AF.Exp))
    dv(lambda e: e.tensor_tensor(out=T[2], in0=LIM, in1=T[0], op=OP.mult))

    def sin_of(dst, src, shift):
        dv(lambda e: e.tensor_scalar(T[10], src, shift, 1.0 / TWO_PI, OP.add, OP.mult))
        dv(lambda e: e.tensor_copy(out=tint.ap, in_=T[10]))
        dv(lambda e: e.tensor_copy(out=T[10], in_=tint.ap))
        dv(lambda e: e.tensor_scalar(T[11], src, shift, None, OP.add))
        dv(lambda e: e.scalar_tensor_tensor(out=T[11], in0=T[10], scalar=-TWO_PI, in1=T[11], op0=OP.mult, op1=OP.add))
        dv(lambda e: e.tensor_scalar(T[12], T[11], math.pi, -TWO_PI, OP.is_gt, OP.mult))
        dv(lambda e: e.tensor_tensor(out=T[11], in0=T[11], in1=T[12], op=OP.add))
        dv(lambda e: e.tensor_scalar(T[12], T[11], -math.pi, TWO_PI, OP.is_lt, OP.mult))
        dv(lambda e: e.tensor_tensor(out=T[11], in0=T[11], in1=T[12], op=OP.add))
        dv(lambda e: e.tensor_scalar(T[11], T[11], math.pi, -math.pi, OP.min, OP.max))
        ac(lambda e: e.activation(out=dst, in_=T[11], func=AF.Sin))

    sin_of(T[3], T[2], 0.0)
    sin_of(T[4], T[2], math.pi / 2)
    dv(lambda e: e.tensor_tensor(out=T[5], in0=T[1], in1=T[4], op=OP.mult))
    dv(lambda e: e.tensor_tensor(out=T[6], in0=T[1], in1=T[3], op=OP.mult))
    dv(lambda e: e.tensor_copy(out=PWv[:, :, 0, 0], in_=T[5]))
    dv(lambda e: e.tensor_copy(out=PWv[:, :, 0, 1], in_=T[6]))
    for k in range(11):
        dv(lambda e, k=k: e.tensor_tensor(out=T[10], in0=PWv[:, :, k, 0], in1=PWv[:, :, k, 0], op=OP.mult))
        dv(lambda e, k=k: e.tensor_tensor(out=T[11], in0=PWv[:, :, k, 1], in1=PWv[:, :, k, 1], op=OP.mult))
        dv(lambda e, k=k: e.tensor_tensor(out=PWv[:, :, k + 1, 0], in0=T[10], in1=T[11], op=OP.subtract))
        dv(lambda e, k=k: e.tensor_tensor(out=T[10], in0=PWv[:, :, k, 0], in1=PWv[:, :, k, 1], op=OP.mult))
        dv(lambda e, k=k: e.tensor_scalar(PWv[:, :, k + 1, 1], T[10], 2.0, None, OP.mult))
    dv(lambda e: e.tensor_scalar(PWv[:, :, :, 2], PWv[:, :, :, 1], -1.0, None, OP.mult))
    dv(lambda e: e.tensor_copy(out=PAv[:, :, 0, 0], in_=T[5]))
    dv(lambda e: e.tensor_copy(out=PAv[:, :, 0, 1], in_=T[6]))
    for m in range(1, 8):
        dv(lambda e, m=m: e.tensor_tensor(out=T[10], in0=PAv[:, :, m - 1, 0], in1=T[5], op=OP.mult))
        dv(lambda e, m=m: e.tensor_tensor(out=T[11], in0=PAv[:, :, m - 1, 1], in1=T[6], op=OP.mult))
        dv(lambda e, m=m: e.tensor_tensor(out=PAv[:, :, m, 0], in0=T[10], in1=T[11], op=OP.subtract))
        dv(lambda e, m=m: e.tensor_tensor(out=T[10], in0=PAv[:, :, m - 1, 0], in1=T[6], op=OP.mult))
        dv(lambda e, m=m: e.tensor_tensor(out=T[11], in0=PAv[:, :, m - 1, 1], in1=T[5], op=OP.mult))
        dv(lambda e, m=m: e.tensor_tensor(out=PAv[:, :, m, 1], in0=T[10], in1=T[11], op=OP.add))
    dv(lambda e: e.tensor_scalar(PAv[:, :, :, 2], PAv[:, :, :, 1], -1.0, None, OP.mult))
    dv(lambda e: e.tensor_scalar(T[7], T[5], -1.0, None, OP.add))
    dv(lambda e: e.tensor_tensor(out=T[8], in0=LRE, in1=LRE, op=OP.mult))
    dv(lambda e: e.tensor_tensor(out=T[9], in0=LIM, in1=LIM, op=OP.mult))
    dv(lambda e: e.tensor_tensor(out=T[8], in0=T[8], in1=T[9], op=OP.add))
    dv(lambda e: e.reciprocal(out=T[8], in_=T[8]))
    dv(lambda e: e.tensor_tensor(out=T[9], in0=T[7], in1=LRE, op=OP.mult))
    dv(lambda e: e.tensor_tensor(out=T[13], in0=T[6], in1=LIM, op=OP.mult))
    dv(lambda e: e.tensor_tensor(out=T[9], in0=T[9], in1=T[13], op=OP.add))
    dv(lambda e: e.tensor_tensor(out=T[9], in0=T[9], in1=T[8], op=OP.mult))
    dv(lambda e: e.tensor_tensor(out=T[13], in0=T[6], in1=LRE, op=OP.mult))
    dv(lambda e: e.tensor_tensor(out=T[14], in0=T[7], in1=LIM, op=OP.mult))
    dv(lambda e: e.tensor_tensor(out=T[13], in0=T[13], in1=T[14], op=OP.subtract))
    dv(lambda e: e.tensor_tensor(out=T[13], in0=T[13], in1=T[8], op=OP.mult))
    bb = alloc(64 * 16 * 3)
    bbv = bb.ap.rearrange("p (c q h) -> p c q h", c=3, q=64)
    crb = T[9].unsqueeze(2).to_broadcast([128, 64, 16])
    cib = T[13].unsqueeze(2).to_broadcast([128, 64, 16])
    P.op("dve", lambda e: e.memset(BPb.ap, 0.0), [], [s5t])
    P.op("dve", lambda e: e.memset(CPflat, 0.0), [], [s5t])
    dv(lambda e: e.tensor_tensor(out=bbv[:, 0], in0=Braw[:, :, 0, :], in1=crb, op=OP.mult))
    dv(lambda e: e.tensor_tensor(out=bbv[:, 2], in0=Braw[:, :, 1, :], in1=cib, op=OP.mult))
    dv(lambda e: e.tensor_tensor(out=bbv[:, 0], in0=bbv[:, 0], in1=bbv[:, 2], op=OP.subtract))
    dv(lambda e: e.tensor_tensor(out=bbv[:, 1], in0=Braw[:, :, 1, :], in1=crb, op=OP.mult))
    dv(lambda e: e.tensor_tensor(out=bbv[:, 2], in0=Braw[:, :, 0, :], in1=cib, op=OP.mult))
    dv(lambda e: e.tensor_tensor(out=bbv[:, 1], in0=bbv[:, 1], in1=bbv[:, 2], op=OP.add))
    for ri in range(2):
        dv(lambda e, ri=ri: e.tensor_copy(out=BPv[0:64, :, ri, 0:16], in_=bbv[0:64, ri]))
        dv(lambda e, ri=ri: e.tensor_copy(out=BPv[64:128, :, ri, 16:32], in_=bbv[64:128, ri]))
    dv(lambda e: e.tensor_copy(out=CPv[0:64, :, 0, 0:16], in_=Craw[0:64, :, 0, :]))
    dv(lambda e: e.tensor_copy(out=CPv[64:128, :, 0, 16:32], in_=Craw[64:128, :, 0, :]))
    dv(lambda e: e.tensor_scalar(CPv[0:64, :, 1, 0:16], Craw[0:64, :, 1, :], -1.0, None, OP.mult))
    dv(lambda e: e.tensor_scalar(CPv[64:128, :, 1, 16:32], Craw[64:128, :, 1, :], -1.0, None, OP.mult))
    for d in range(2):
        for j in range(8):
            for ri in range(2):
                pb = psb[(d * 16 + j * 2 + ri) % 4]
                for a in range(4):
                    dq = d * 32 + j * 4 + a
                    P.op("pe", lambda e, pb=pb, a=a, dq=dq, ri=ri: e.matmul(pb.ap[32 * a:32 * a + 32, 0:128], BPv[:, dq, ri, :], ident,
                                                                              start=True, stop=True, tile_position=(0, 32 * a)),
                         [s5t, cbf.t], [pb.t])
                P.op("dve", lambda e, pb=pb, d=d, j=j, ri=ri: e.tensor_copy(out=W1v[:, d, j, ri, :], in_=pb.ap[:, 0:128]), [pb.t], [s5t])
    P.op("sp", lambda e: e.dma_start(out=s5w_s[:, :], in_=S5W.ap), [s5t], [], dma=True)
    astate["off"] = persist_mark
    P.barrier()
    work_mark = astate["off"]

    bank_rr = [0]

    def nextbank(lo=0, hi=4):
        b = psb[lo + bank_rr[0] % (hi - lo)]
        bank_rr[0] += 1
        return b

    def load_WA(dst, WA, mb0, nmb):
        P.op("sp", lambda e: e.dma_start(out=dst.ap.rearrange("p (mb c) -> p mb c", mb=nmb),
                                         in_=WA[mb0:mb0 + nmb].rearrange("mb p c -> p mb c")),
             [], [dst.t], dma=True)

    def load_WB(dst, WB, nk, c0, cw):
        P.op("sp", lambda e: e.dma_start(out=dst.ap.rearrange("p (kc c) -> p kc c", kc=nk),
                                         in_=WB[:, c0:c0 + cw].rearrange("(kc p) c -> p kc c", p=128)),
             [], [dst.t], dma=True)

    def rms_rstd(ss_ap, n, rstd_buf, reads):
        P.op("act", lambda e: e.activation(out=rstd_buf.ap, in_=ss_ap, func=AF.Sqrt, bias=epsb.ap[:, 0:1], scale=1.0 / n),
             reads + [epsb.t], [rstd_buf.t])
        P.op("dve", lambda e: e.reciprocal(out=rstd_buf.ap, in_=rstd_buf.ap), [rstd_buf.t], [rstd_buf.t])

    def act_recip(dst, src_ap, reads):
        P.op("dve", lambda e: e.reciprocal(out=dst.ap, in_=src_ap), reads, [dst.t])

    def norm_transpose(xt, hT, s, xn, ssb, junk):
        P.op("act", lambda e: e.activation(out=junk.ap, in_=xt.ap, func=AF.Square, accum_out=ssb.ap[:, 0:1]), [xt.t], [junk.t, ssb.t])
        P.op("act", lambda e: e.activation(out=ssb.ap[:, 1:2], in_=ssb.ap[:, 0:1], func=AF.Sqrt, bias=epsb.ap[:, 0:1], scale=1.0 / D),
             [ssb.t, epsb.t], [ssb.t])
        P.op("dve", lambda e: e.reciprocal(out=ssb.ap[:, 2:3], in_=ssb.ap[:, 1:2]), [ssb.t], [ssb.t])
        P.op("dve", lambda e: e.tensor_scalar(xn.ap, xt.ap, ssb.ap[:, 2:3], None, OP.mult), [xt.t, ssb.t], [xn.t])
        pb = nextbank()
        pv = pb.ap.bitcast(BF)
        for kc in range(8):
            P.op("pe", lambda e, kc=kc: e.transpose(pv[:, kc * 128:(kc + 1) * 128], xn.ap[:, kc * 128:(kc + 1) * 128], ident),
                 [xn.t, cbf.t], [pb.t])
        hv = hT.ap.rearrange("p (kc t) -> p kc t", kc=8)
        P.op("act", lambda e: e.copy(out=hv[:, :, s * 128:(s + 1) * 128], in_=pv.rearrange("p (kc t) -> p kc t", kc=8)), [pb.t], [hT.t])

    def proj_fm(W, mbi, hT, nk=8):
        pb = nextbank()
        hv = hT.ap.rearrange("p (kc t) -> p kc t", kc=nk)
        for kc in range(nk):
            P.op("pe", lambda e, kc=kc: e.matmul(pb.ap, W.ap[:, mbi * 1024 + kc * 128:mbi * 1024 + (kc + 1) * 128], hv[:, kc, :],
                                                start=(kc == 0), stop=(kc == nk - 1)),
                 [W.t, hT.t], [pb.t])
        return pb

    def qknorm_rope(pb, gcol, dst_ap, dst_tok, cs, sn, wk, ones_blk, rope=True, n=64):
        sq, rstd, ybf, t1, t2 = wk
        P.op("act", lambda e: e.activation(out=sq.ap, in_=pb.ap, func=AF.Square), [pb.t], [sq.t])
        p2 = nextbank()
        P.op("pe", lambda e: e.matmul(p2.ap, ones_blk, sq.ap, start=True, stop=True), [sq.t, cbf.t], [p2.t])
        P.op("dve", lambda e: e.tensor_scalar(ybf.ap, pb.ap, sm[:, gcol:gcol + 1], None, OP.mult), [pb.t, smalls.t, sq.t], [ybf.t])
        rms_rstd(p2.ap, n, rstd, [p2.t])
        if not rope:
            P.op("dve", lambda e: e.tensor_tensor(out=dst_ap, in0=ybf.ap, in1=rstd.ap, op=OP.mult), [ybf.t, rstd.t], [dst_tok])
            return
        p3 = nextbank()
        P.op("pe", lambda e: e.matmul(p3.ap, rotm, ybf.ap, start=True, stop=True), [ybf.t, cbf.t], [p3.t])
        P.op("pool", lambda e: e.tensor_tensor(out=t1.ap, in0=ybf.ap, in1=cs.ap, op=OP.mult), [ybf.t, cs.t], [t1.t])
        P.op("dve", lambda e: e.tensor_tensor(out=t2.ap, in0=p3.ap, in1=sn.ap, op=OP.mult), [p3.t, sn.t], [t2.t])
        P.op("dve", lambda e: e.tensor_tensor(out=t1.ap, in0=t1.ap, in1=t2.ap, op=OP.add), [t2.t, t1.t], [t1.t])
        P.op("dve", lambda e: e.tensor_tensor(out=dst_ap, in0=t1.ap, in1=rstd.ap, op=OP.mult), [t1.t, rstd.t], [dst_tok])

    def headnorm2(pbs, gcols, dsts, wk2, n=256):
        sqA, sqB, rstd = wk2
        N = dsts[0][0].shape[1]
        P.op("act", lambda e: e.activation(out=sqA.ap[:, 0:N], in_=pbs[0].ap[:, 0:N], func=AF.Square), [pbs[0].t], [sqA.t])
        P.op("act", lambda e: e.activation(out=sqB.ap[:, 0:N], in_=pbs[1].ap[:, 0:N], func=AF.Square), [pbs[1].t], [sqB.t])
        p2 = nextbank()
        P.op("pe", lambda e: e.matmul(p2.ap[:, 0:N], ones128, sqA.ap[:, 0:N], start=True, stop=False), [sqA.t, cbf.t], [p2.t])
        P.op("pe", lambda e: e.matmul(p2.ap[:, 0:N], ones128, sqB.ap[:, 0:N], start=False, stop=True), [sqB.t, cbf.t], [p2.t])
        P.op("act", lambda e: e.activation(out=rstd.ap[:, 0:N], in_=p2.ap[:, 0:N], func=AF.Sqrt, bias=epsb.ap[:, 0:1], scale=1.0 / n),
             [p2.t, epsb.t], [rstd.t])
        P.op("dve", lambda e: e.reciprocal(out=rstd.ap[:, 0:N], in_=rstd.ap[:, 0:N]), [rstd.t], [rstd.t])
        for c in range(2):
            P.op("dve", lambda e, c=c: e.scalar_tensor_tensor(out=dsts[c][0], in0=pbs[c].ap[:, 0:N], scalar=sm[:, gcols[c]:gcols[c] + 1],
                                                             in1=rstd.ap[:, 0:N], op0=OP.mult, op1=OP.mult),
                 [pbs[c].t, rstd.t, smalls.t], [dsts[c][1]])

    def do_seq(si, L, tok0):
        NT = L // 512
        NKT = L // 128
        astate["off"] = work_mark
        Wuk = abf(16 * 1024)
        P.op("sp", lambda e: e.dma_start(out=Wuk.ap[:, 0:8192].rearrange("p (mb c) -> p mb c", mb=8),
                                         in_=WAin[0:8].rearrange("mb p c -> p mb c")), [], [Wuk.t], dma=True)
        P.op("sp", lambda e: e.dma_start(out=Wuk.ap[:, 8192:16384].rearrange("p (mb c) -> p mb c", mb=8),
                                         in_=WAin[16:24].rearrange("mb p c -> p mb c")), [], [Wuk.t], dma=True)
        Wvh = [abf(8 * 512) for _ in range(2)]
        for hf in range(2):
            load_WB(Wvh[hf], WBv, 8, hf * 512, 512)
        xts = [alloc(1024) for _ in range(2)]
        xn = abf(1024)
        ssb = alloc(4)
        hTs = [abf(8 * 512) for _ in range(2)]
        css = [alloc(512) for _ in range(2)]
        sns = [alloc(512) for _ in range(2)]
        wk = (abf(512), alloc(512), abf(512), alloc(512), alloc(512))
        outb = [abf(512) for _ in range(4)]
        vtb = [abf(1024) for _ in range(2)]
        oi = 0
        for it in range(NT):
            hT = hTs[it % 2]
            cs, sn = css[it % 2], sns[it % 2]
            P.op("sp", lambda e, cs=cs, it=it: e.dma_start(out=cs.ap, in_=ropec_d[:, it * 512:(it + 1) * 512]), [], [cs.t], dma=True)
            P.op("sp", lambda e, sn=sn, it=it: e.dma_start(out=sn.ap, in_=ropes_d[:, it * 512:(it + 1) * 512]), [], [sn.t], dma=True)
            for s in range(4):
                xt = xts[s % 2]
                r0 = tok0 + it * 512 + s * 128
                P.op("sp", lambda e, xt=xt, r0=r0: e.dma_start(out=xt.ap, in_=x_d[r0:r0 + 128, :]), [], [xt.t], dma=True)
                norm_transpose(xt, hT, s, xn, ssb, xn)
            P.op("sp", lambda e, hT=hT, it=it: e.dma_start(out=hT_s[:, it * 512:(it + 1) * 512].rearrange("(kc p) t -> p kc t", p=128),
                                                          in_=hT.ap.rearrange("p (kc t) -> p kc t", kc=8)), [hT.t], [], dma=True)
            for m in range(8):
                pb = proj_fm(Wuk, m, hT)
                ob = outb[oi % 4]; oi += 1
                P.op("act", lambda e, ob=ob, pb=pb: e.copy(out=ob.ap, in_=pb.ap), [pb.t], [ob.t])
                P.op("sp", lambda e, ob=ob, m=m, it=it: e.dma_start(out=uT_s[m * 128:(m + 1) * 128, it * 512:(it + 1) * 512], in_=ob.ap),
                     [ob.t], [], dma=True)
            for m in range(8):
                pb = proj_fm(Wuk, 8 + m, hT)
                ob = outb[oi % 4]; oi += 1
                qknorm_rope(pb, KG, ob.ap, ob.t, cs, sn, wk, ones64)
                P.op("sp", lambda e, ob=ob, m=m, it=it: e.dma_start(out=KT_s[m * 128:(m + 1) * 128, it * 512:(it + 1) * 512], in_=ob.ap),
                     [ob.t], [], dma=True)
            hv = hT.ap.rearrange("p (kc t) -> p kc t", kc=8)
            for s in range(4):
                vt = vtb[s % 2]
                for hf in range(2):
                    pb = nextbank()
                    wv = Wvh[hf].ap.rearrange("p (kc c) -> p kc c", kc=8)
                    for kc in range(8):
                        P.op("pe", lambda e, pb=pb, kc=kc, s=s, wv=wv, hv=hv: e.matmul(pb.ap, hv[:, kc, s * 128:(s + 1) * 128], wv[:, kc, :],
                                                                                     start=(kc == 0), stop=(kc == 7)),
                             [hT.t, Wvh[hf].t], [pb.t])
                    if hf == 0:
                        P.op("act", lambda e, vt=vt, pb=pb: e.copy(out=vt.ap[:, 0:512], in_=pb.ap), [pb.t], [vt.t])
                    else:
                        P.op("dve", lambda e, vt=vt, pb=pb: e.tensor_copy(out=vt.ap[:, 512:1024], in_=pb.ap), [pb.t], [vt.t])
                P.op("sp", lambda e, vt=vt, it=it, s=s: e.dma_start(out=V_s[it * 512 + s * 128:it * 512 + (s + 1) * 128, :], in_=vt.ap),
                     [vt.t], [], dma=True)
        P.barrier()
        if STOP_AFTER == 1:
            return

        astate["off"] = work_mark
        S5Wl = alloc(S5WORDS)
        P.op("sp", lambda e, S5Wl=S5Wl: e.dma_start(out=S5Wl.ap, in_=s5w_s[:, :]), [], [S5Wl.t], dma=True)
        PWl, CPl, W1l, PAl = s5_views(S5Wl)
        wt = S5Wl.t
        T8 = 8
        Lc = L // T8
        NLV = int(math.log2(Lc))
        SAd = [[[abf(L) for _ in range(2)] for _ in range(2)] for _ in range(2)]
        Kb = [[[[alloc(Lc + 2) for _ in range(2)] for _ in range(2)] for _ in range(2)] for _ in range(2)]
        for st_ in range(2):
            for d in range(2):
                for pp in range(2):
                    for ri in range(2):
                        P.op("dve", lambda e, kb=Kb[st_][d][pp][ri]: e.memset(kb.ap, 0.0), [], [Kb[st_][d][pp][ri].t])
        uTj = [abf(L) for _ in range(2)]
        zj = abf(L)
        yj = abf(L)
        tz = [alloc(512) for _ in range(3)]
        ctmp = [alloc(Lc) for _ in range(4)]
        cti = [0]
        GC = 2.0 * math.sqrt(2.0 / math.pi)
        tpn = 512 // Lc if Lc < 512 else 1

        def stt(out, in0, scal, in1, reads, writes):
            P.op("dve", lambda e: e.scalar_tensor_tensor(out=out, in0=in0, scalar=scal, in1=in1, op0=OP.mult, op1=OP.add), reads, writes)

        def Xv(q, d):
            bufs = SAd[q % 2][d]
            return [bufs[ri].ap.rearrange("p (t c) -> p t c", t=T8) for ri in range(2)], [bufs[0].t, bufs[1].t]

        def s_evac(q):
            j, a = q // 4, q % 4
            uj = uTj[j % 2]
            if a == 0:
                P.op("sp", lambda e: e.dma_start(out=uj.ap, in_=uT_s[j * 128:(j + 1) * 128, 0:L]), [], [uj.t], dma=True)
            for d in range(2):
                for ri in range(2):
                    buf = SAd[q % 2][d][ri]
                    sav = buf.ap.rearrange("p (t c) -> p t c", t=T8)
                    for n in range(NT):
                        pb = nextbank()
                        P.op("pe", lambda e, pb=pb, d=d, ri=ri, n=n: e.matmul(
                            pb.ap, W1l[32 * a:32 * a + 32, d, j, ri, :], uj.ap[32 * a:32 * a + 32, n * 512:(n + 1) * 512],
                            start=True, stop=True, tile_position=(32 * a, 0)), [uj.t, wt], [pb.t])
                        P.op("act", lambda e, pb=pb, sav=sav, n=n: e.copy(out=sav[:, :, n * 64:(n + 1) * 64],
                                                                         in_=pb.ap.rearrange("p (c t) -> p t c", t=T8)), [pb.t], [buf.t])

        def scan_thunks(q, d):
            dq = d * 32 + q
            th = []
            X, tX = Xv(q, d)
            kb = Kb[q % 2][d]
            a1 = PAl[:, dq, 0, :]
            order = list(range(1, T8)) if d == 0 else list(range(T8 - 2, -1, -1))
            for tau in order:
                pv = tau - 1 if d == 0 else tau + 1
                th.append(lambda tau=tau, pv=pv: stt(X[0][:, tau, :], X[0][:, pv, :], a1[:, 0:1], X[0][:, tau, :], [tX[0], wt], [tX[0]]))
                th.append(lambda tau=tau, pv=pv: stt(X[1][:, tau, :], X[0][:, pv, :], a1[:, 1:2], X[1][:, tau, :], [tX[0], tX[1], wt], [tX[1]]))
                th.append(lambda tau=tau, pv=pv: stt(X[0][:, tau, :], X[1][:, pv, :], a1[:, 2:3], X[0][:, tau, :], [tX[0], tX[1], wt], [tX[0]]))
                th.append(lambda tau=tau, pv=pv: stt(X[1][:, tau, :], X[1][:, pv, :], a1[:, 0:1], X[1][:, tau, :], [tX[1], wt], [tX[1]]))
            e_ = T8 - 1 if d == 0 else 0
            cur = [X[0][:, e_, :], X[1][:, e_, :]]
            ctk = list(tX)
            for k in range(NLV):
                s = 1 << k
                nb_ = kb[k % 2]
                nd = [nb_[0].ap[:, 1:Lc + 1], nb_[1].ap[:, 1:Lc + 1]]
                ntk = [nb_[0].t, nb_[1].t]
                pw = PWl[:, dq, k + 3, :]
                if d == 0:
                    hi, lo, keep = slice(s, Lc), slice(0, Lc - s), slice(0, s)
                else:
                    hi, lo, keep = slice(0, Lc - s), slice(s, Lc), slice(Lc - s, Lc)
                for ri in range(2):
                    th.append(lambda ri=ri, nd=nd, cur=cur, keep=keep, ctk=ctk, ntk=ntk: P.op(
                        "dve", lambda e: e.tensor_copy(out=nd[ri][:, keep], in_=cur[ri][:, keep]), [ctk[ri]], [ntk[ri]]))
                th.append(lambda nd=nd, cur=cur, hi=hi, lo=lo, pw=pw, ctk=ctk, ntk=ntk: stt(nd[0][:, hi], cur[0][:, lo], pw[:, 0:1], cur[0][:, hi], [ctk[0], wt], [ntk[0]]))
                th.append(lambda nd=nd, cur=cur, hi=hi, lo=lo, pw=pw, ctk=ctk, ntk=ntk: stt(nd[1][:, hi], cur[0][:, lo], pw[:, 1:2], cur[1][:, hi], [ctk[0], ctk[1], wt], [ntk[1]]))
                th.append(lambda nd=nd, cur=cur, hi=hi, lo=lo, pw=pw, ctk=ctk, ntk=ntk: stt(nd[0][:, hi], cur[1][:, lo], pw[:, 2:3], nd[0][:, hi], [ctk[1], ntk[0], wt], [ntk[0]]))
                th.append(lambda nd=nd, cur=cur, hi=hi, lo=lo, pw=pw, ctk=ctk, ntk=ntk: stt(nd[1][:, hi], cur[1][:, lo], pw[:, 0:1], nd[1][:, hi], [ctk[1], ntk[1], wt], [ntk[1]]))
                cur, ctk = nd, ntk
            return th

        def s_scan(q):
            th0 = scan_thunks(q, 0)
            th1 = scan_thunks(q, 1)
            for i in range(max(len(th0), len(th1))):
                if i < len(th0):
                    th0[i]()
                if i < len(th1):
                    th1[i]()

        def s_carry(q):
            for d in range(2):
                dq = d * 32 + q
                X, tX = Xv(q, d)
                fin = Kb[q % 2][d][(NLV - 1) % 2]
                sh = [fin[ri].ap[:, 0:Lc] if d == 0 else fin[ri].ap[:, 2:Lc + 2] for ri in range(2)]
                ftk = [fin[0].t, fin[1].t]
                e_ = T8 - 1 if d == 0 else 0
                for ri in range(2):
                    P.op("act", lambda e, ri=ri, X=X, fin=fin, e_=e_: e.copy(out=X[ri][:, e_, :], in_=fin[ri].ap[:, 1:Lc + 1]), [ftk[ri]], [tX[ri]])
                taus = list(range(0, T8 - 1)) if d == 0 else list(range(1, T8))
                for tau in taus:
                    m = tau if d == 0 else (T8 - 1 - tau)
                    pw = PAl[:, dq, m, :]
                    for (ri, ca, cb) in ((0, 0, 2), (1, 1, 0)):
                        c1 = ctmp[cti[0] % 4]; c2 = ctmp[(cti[0] + 1) % 4]; cti[0] += 2
                        P.op("act", lambda e, c1=c1, sh=sh, pw=pw, ca=ca: e.activation(out=c1.ap, in_=sh[0], func=AF.Copy, scale=pw[:, ca:ca + 1]), [ftk[0], wt], [c1.t])
                        P.op("act", lambda e, c2=c2, sh=sh, pw=pw, cb=cb: e.activation(out=c2.ap, in_=sh[1], func=AF.Copy, scale=pw[:, cb:cb + 1]), [ftk[1], wt], [c2.t])
                        P.op("pool", lambda e, c1=c1, X=X, ri=ri, tau=tau: e.tensor_tensor(out=c1.ap, in0=c1.ap, in1=X[ri][:, tau, :], op=OP.add), [c1.t, tX[ri]], [c1.t])
                        P.op("pool", lambda e, c1=c1, c2=c2, X=X, ri=ri, tau=tau: e.tensor_tensor(out=X[ri][:, tau, :], in0=c1.ap, in1=c2.ap, op=OP.add), [c1.t, c2.t], [tX[ri]])

        def s_outmm(q):
            a = q % 4
            qs = slice(32 * a, 32 * a + 32)
            for n in range(NT):
                pb = nextbank()
                cnt = 0
                for d in range(2):
                    for ri in range(2):
                        buf = SAd[q % 2][d][ri]
                        P.op("pe", lambda e, pb=pb, d=d, ri=ri, n=n, cnt=cnt, buf=buf: e.matmul(
                            pb.ap[qs, :], CPl[:, d * 32 + q, ri, :], buf.ap[:, n * 512:(n + 1) * 512],
                            start=(cnt == 0), stop=(cnt == 3), tile_position=(0, 32 * a)), [buf.t, wt], [pb.t])
                        cnt += 1
                P.op("act", lambda e, pb=pb, n=n: e.copy(out=yj.ap[qs, n * 512:(n + 1) * 512], in_=pb.ap[qs, :]), [pb.t], [yj.t])

        def z_chunk(j):
            uj = uTj[j % 2]
            t0, t1_, t2_ = tz
            v3 = lambda ap_: ap_.rearrange("p (t c) -> p t c", t=tpn)
            for n in range(NT):
                upv = uj.ap.rearrange("p (c t) -> p t c", t=T8)[:, n * tpn:(n + 1) * tpn, :]
                zpv = zj.ap.rearrange("p (c t) -> p t c", t=T8)[:, n * tpn:(n + 1) * tpn, :]
                P.op("dve", lambda e, n=n, upv=upv: e.scalar_tensor_tensor(
                    out=v3(t0.ap), in0=upv, scalar=sm[:, S5D + j:S5D + j + 1], in1=v3(yj.ap[:, n * 512:(n + 1) * 512]), op0=OP.mult, op1=OP.add),
                    [yj.t, uj.t, smalls.t], [t0.t])
                P.op("act", lambda e: e.activation(out=t1_.ap, in_=t0.ap, func=AF.Square), [t0.t], [t1_.t])
                P.op("dve", lambda e: e.tensor_scalar(t1_.ap, t1_.ap, 0.044715, 1.0, OP.mult, OP.add), [t1_.t], [t1_.t])
                P.op("dve", lambda e: e.tensor_tensor(out=t1_.ap, in0=t1_.ap, in1=t0.ap, op=OP.mult), [t1_.t, t0.t], [t1_.t])
                P.op("act", lambda e: e.activation(out=t2_.ap, in_=t1_.ap, func=AF.Sigmoid, scale=GC), [t1_.t], [t2_.t])
                P.op("dve", lambda e, zpv=zpv: e.tensor_tensor(out=zpv, in0=v3(t0.ap), in1=v3(t2_.ap), op=OP.mult), [t0.t, t2_.t], [zj.t])
            P.op("sp", lambda e: e.dma_start(out=zT_s[j * 128:(j + 1) * 128, 0:L], in_=zj.ap), [zj.t], [], dma=True)

        s_evac(0)
        for i in range(32):
            if i >= 1:
                s_outmm(i - 1)
            if i + 1 < 32:
                s_evac(i + 1)
            s_scan(i)
            if i >= 1 and (i - 1) % 4 == 3:
                z_chunk((i - 1) // 4)
            s_carry(i)
        s_outmm(31)
        z_chunk(7)
        P.barrier()
        astate["off"] = work_mark
        Wg = abf(8 * 1024)
        load_WA(Wg, WAglu, 0, 8)
        zts = [abf(8 * 512) for _ in range(2)]
        sig = alloc(512)
        outb = [abf(512) for _ in range(4)]
        for it in range(NT):
            zt = zts[it % 2]
            P.op("sp", lambda e, zt=zt, it=it: e.dma_start(out=zt.ap.rearrange("p (kc t) -> p kc t", kc=8),
                                                          in_=zT_s[:, it * 512:(it + 1) * 512].rearrange("(kc p) t -> p kc t", p=128)),
                 [], [zt.t], dma=True)
            zv = zt.ap.rearrange("p (kc t) -> p kc t", kc=8)
            for m in range(8):
                pb = proj_fm(Wg, m, zt)
                P.op("act", lambda e, pb=pb: e.activation(out=sig.ap, in_=pb.ap, func=AF.Sigmoid), [pb.t], [sig.t])
                ob = outb[oi % 4]; oi += 1
                P.op("dve", lambda e, ob=ob, m=m, zv=zv: e.tensor_tensor(out=ob.ap, in0=zv[:, m, :], in1=sig.ap, op=OP.mult), [sig.t, zt.t], [ob.t])
                P.op("sp", lambda e, ob=ob, m=m, it=it: e.dma_start(out=s5T_s[m * 128:(m + 1) * 128, it * 512:(it + 1) * 512], in_=ob.ap),
                     [ob.t], [], dma=True)
        P.barrier()
        if STOP_AFTER == 2:
            return

        astate["off"] = work_mark
        hTs = [abf(8 * 512) for _ in range(2)]
        css = alloc(512); sns = alloc(512)
        wk = (abf(512), alloc(512), abf(512), alloc(512), alloc(512))
        wk2 = (abf(512), abf(512), alloc(512))
        NWB = 6
        wblk = [abf(1024) for _ in range(NWB)]
        wbi = [0]

        def load_block(WA, idx):
            b = wblk[wbi[0] % NWB]
            wbi[0] += 1
            P.op("sp", lambda e, b=b: e.dma_start(out=b.ap, in_=WA[idx]), [], [b.t], dma=True)
            return b

        qTb = [abf(512) for _ in range(2)]
        KTh = [abf(L) for _ in range(2)]
        Vh = [abf(L) for _ in range(2)]
        PT = [abf(512) for _ in range(4)]
        pts2 = [abf(512) for _ in range(2)]
        fsq = abf(512)
        fo = wk2[2]
        brT = [abf(8 * 512) for _ in range(3)]
        MkT = abf(8 * 256)
        MV = abf(2 * 1024)
        merged = abf(8 * 512)
        gate = wk[1]; macc = wk2[2]; mtmp = wk[4]
        WBp = [abf(8 * 512) for _ in range(2)]
        x1s = [alloc(1024) for _ in range(2)]
        xn = abf(1024)
        ssb = alloc(4)
        actT = abf(22 * 512)
        ybuf = [wk[3], wk[4]]
        memx = x1s
        brv = [b.ap.rearrange("p (kc t) -> p kc t", kc=8) for b in brT]
        mgv = merged.ap.rearrange("p (kc t) -> p kc t", kc=8)
        MkTv = MkT.ap.rearrange("p (c m) -> p c m", c=8)
        MVv = MV.ap.rearrange("p (s c) -> p s c", s=2)
        actv = actT.ap.rearrange("p (kc t) -> p kc t", kc=22)

        mhT = hTs[0]
        mhv = mhT.ap.rearrange("p (kc t) -> p kc t", kc=8)
        for s in range(2):
            xt = memx[s]
            r0 = si * NMEM + s * 128
            P.op("sp", lambda e, xt=xt, r0=r0: e.dma_start(out=xt.ap, in_=mem_d[r0:r0 + 128, :]), [], [xt.t], dma=True)
            norm_transpose(xt, mhT, s, xn, ssb, xn)
        for hm in range(4):
            pbs = []
            for c in range(2):
                wb = load_block(WAkv, 2 * hm + c)
                pb = nextbank()
                for kc in range(8):
                    P.op("pe", lambda e, pb=pb, wb=wb, kc=kc: e.matmul(pb.ap[:, 0:256], wb.ap[:, kc * 128:(kc + 1) * 128], mhv[:, kc, 0:256],
                                                                      start=(kc == 0), stop=(kc == 7)), [wb.t, mhT.t], [pb.t])
                pbs.append(pb)
            headnorm2(pbs, (MKG, MKG + 1), [(MkTv[:, 2 * hm + c, :], MkT.t) for c in range(2)], wk2)
        for hf in range(2):
            wp = WBp[hf]
            load_WB(wp, WBkv, 8, hf * 512, 512)
            wpv = wp.ap.rearrange("p (kc c) -> p kc c", kc=8)
            for s in range(2):
                pb = nextbank()
                for kc in range(8):
                    P.op("pe", lambda e, pb=pb, kc=kc, s=s, wpv=wpv: e.matmul(pb.ap, mhv[:, kc, s * 128:(s + 1) * 128], wpv[:, kc, :],
                                                                             start=(kc == 0), stop=(kc == 7)), [wp.t, mhT.t], [pb.t])
                P.op("act", lambda e, pb=pb, s=s, hf=hf: e.copy(out=MVv[:, s, hf * 512:(hf + 1) * 512], in_=pb.ap), [pb.t], [MV.t])

        def load_kv(h):
            kth, vh = KTh[h % 2], Vh[h % 2]
            P.op("sp", lambda e: e.dma_start(out=kth.ap, in_=KT_s[h * 128:(h + 1) * 128, 0:L]), [], [kth.t], dma=True)
            P.op("sp", lambda e: e.dma_start(out=vh.ap.rearrange("p (kt c) -> p kt c", kt=NKT),
                                             in_=V_s[0:L, h * 128:(h + 1) * 128].rearrange("(kt p) c -> p kt c", p=128)),
                 [], [vh.t], dma=True)

        def tile_loads(it):
            ts = slice(it * 512, (it + 1) * 512)
            hT = hTs[it % 2]
            P.op("sp", lambda e: e.dma_start(out=hT.ap.rearrange("p (kc t) -> p kc t", kc=8),
                                             in_=hT_s[:, ts].rearrange("(kc p) t -> p kc t", p=128)), [], [hT.t], dma=True)
            P.op("sp", lambda e: e.dma_start(out=css.ap, in_=ropec_d[:, ts]), [], [css.t], dma=True)
            P.op("sp", lambda e: e.dma_start(out=sns.ap, in_=ropes_d[:, ts]), [], [sns.t], dma=True)
            P.op("sp", lambda e: e.dma_start(out=brv[0], in_=s5T_s[:, ts].rearrange("(kc p) t -> p kc t", p=128)), [], [brT[0].t], dma=True)
            load_kv(0)

        tile_loads(0)
        for it in range(NT):
            ts = slice(it * 512, (it + 1) * 512)
            hT = hTs[it % 2]
            h2T = hTs[(it + 1) % 2]
            sq, rstd, ybf, t1, t2 = wk

            def qA(h):
                wbq = load_block(WAin, 8 + h)
                pbq = proj_fm(wbq, 0, hT)
                P.op("act", lambda e: e.activation(out=sq.ap, in_=pbq.ap, func=AF.Square), [pbq.t], [sq.t])
                P.op("dve", lambda e: e.tensor_scalar(ybf.ap, pbq.ap, sm[:, QG:QG + 1], None, OP.mult), [pbq.t, smalls.t, sq.t], [ybf.t])

            def qB(h):
                dst = qTb[h % 2]
                p2 = nextbank()
                P.op("pe", lambda e: e.matmul(p2.ap, ones64, sq.ap, start=True, stop=True), [sq.t, cbf.t], [p2.t])
                p3 = nextbank()
                P.op("pe", lambda e: e.matmul(p3.ap, rotm, ybf.ap, start=True, stop=True), [ybf.t, cbf.t], [p3.t])
                rms_rstd(p2.ap, 64, rstd, [p2.t])
                P.op("pool", lambda e: e.tensor_tensor(out=t1.ap, in0=ybf.ap, in1=css.ap, op=OP.mult), [ybf.t, css.t], [t1.t])
                P.op("dve", lambda e: e.tensor_tensor(out=t2.ap, in0=p3.ap, in1=sns.ap, op=OP.mult), [p3.t, sns.t], [t2.t])
                P.op("dve", lambda e: e.tensor_tensor(out=t1.ap, in0=t1.ap, in1=t2.ap, op=OP.add), [t2.t, t1.t], [t1.t])
                P.op("dve", lambda e: e.tensor_tensor(out=dst.ap, in0=t1.ap, in1=rstd.ap, op=OP.mult), [t1.t, rstd.t], [dst.t])

            def F1(h):
                O1, O2, Z1, Z2 = psb[4], psb[5], psb[6], psb[7]
                P.op("act", lambda e: e.copy(out=t1.ap, in_=Z1.ap), [Z1.t], [t1.t])
                P.op("act", lambda e: e.copy(out=t2.ap, in_=Z2.ap), [Z2.t], [t2.t])
                P.op("act", lambda e: e.copy(out=fo.ap, in_=O1.ap), [O1.t], [fo.t])
                P.op("act", lambda e: e.copy(out=rstd.ap, in_=O2.ap), [O2.t], [rstd.t])
                P.op("dve", lambda e: e.reciprocal(out=t1.ap, in_=t1.ap), [t1.t], [t1.t])
                P.op("dve", lambda e: e.tensor_tensor(out=t1.ap, in0=fo.ap, in1=t1.ap, op=OP.mult), [fo.t, t1.t], [t1.t])
                P.op("dve", lambda e: e.reciprocal(out=t2.ap, in_=t2.ap), [t2.t], [t2.t])
                P.op("dve", lambda e: e.tensor_tensor(out=t2.ap, in0=rstd.ap, in1=t2.ap, op=OP.mult), [rstd.t, t2.t], [t2.t])
                P.op("dve", lambda e: e.scalar_tensor_tensor(out=fo.ap, in0=t2.ap, scalar=NEGLAM, in1=t1.ap, op0=OP.mult, op1=OP.add),
                     [t1.t, t2.t, lamb.t], [fo.t])
                P.op("pool", lambda e: e.tensor_tensor(out=fsq.ap, in0=fo.ap, in1=fo.ap, op=OP.mult), [fo.t], [fsq.t])

            def F2(h):
                p2b = nextbank()
                P.op("pe", lambda e: e.matmul(p2b.ap, ones128, fsq.ap, start=True, stop=True), [fsq.t, cbf.t], [p2b.t])
                rms_rstd(p2b.ap, 128, rstd, [p2b.t])
                P.op("dve", lambda e: e.scalar_tensor_tensor(out=t1.ap, in0=fo.ap, scalar=sm[:, SUBG:SUBG + 1], in1=rstd.ap, op0=OP.mult, op1=OP.mult),
                     [fo.t, rstd.t, smalls.t], [t1.t])
                P.op("pool", lambda e: e.tensor_scalar(brv[1][:, h, :], t1.ap, 1.0 - LAMBDA_INIT, None, OP.mult), [t1.t], [brT[1].t])

            qA(0)
            qB(0)
            k1, k3 = min(1, NKT - 1), min(3, NKT - 1)
            for h in range(8):
                kth, vh, qT = KTh[h % 2], Vh[h % 2], qTb[h % 2]
                if h > 0:
                    load_kv(h)
                vhv = vh.ap.rearrange("p (kt c) -> p kt c", kt=NKT)
                O1, O2, Z1, Z2 = psb[4], psb[5], psb[6], psb[7]
                stages = {}
                if h + 1 < 8:
                    stages.setdefault(k1, []).append(lambda h=h: qA(h + 1))
                    stages.setdefault(k3 if SPLIT_Q else k1, []).append(lambda h=h: qB(h + 1))
                if h >= 1 and SPLIT_F:
                    stages.setdefault(k3, []).append(lambda h=h: F2(h - 1))

                def emit_scores(kt):
                    ks = slice(kt * 128, (kt + 1) * 128)
                    pS1 = nextbank(); pS2 = nextbank()
                    P.op("pe", lambda e, pS1=pS1, ks=ks, kth=kth, qT=qT: e.matmul(pS1.ap, kth.ap[0:64, ks], qT.ap[0:64, :], start=True, stop=True,
                                                                               tile_position=(0, 0)), [kth.t, qT.t], [pS1.t])
                    P.op("pe", lambda e, pS2=pS2, ks=ks, kth=kth, qT=qT: e.matmul(pS2.ap, kth.ap[64:128, ks], qT.ap[64:128, :], start=True, stop=True,
                                                                               tile_position=(64, 0)), [kth.t, qT.t], [pS2.t])
                    return pS1, pS2

                nxt_sc = emit_scores(0)
                for kt in range(NKT):
                    pS1, pS2 = nxt_sc
                    ins_here = stages.get(kt)
                    if kt + 1 < NKT and not ins_here:
                        nxt_sc = emit_scores(kt + 1)
                    p1, p2 = PT[(2 * kt) % 4], PT[(2 * kt + 1) % 4]
                    P.op("act", lambda e, pS1=pS1, p1=p1: e.activation(out=p1.ap, in_=pS1.ap, func=AF.Exp, scale=0.125), [pS1.t], [p1.t])
                    P.op("act", lambda e, pS2=pS2, p2=p2: e.activation(out=p2.ap, in_=pS2.ap, func=AF.Exp, scale=0.125), [pS2.t], [p2.t])
                    st, sp_ = (kt == 0), (kt == NKT - 1)
                    P.op("pe", lambda e, p1=p1, kt=kt, st=st, sp_=sp_, vhv=vhv: e.matmul(O1.ap, vhv[:, kt, :], p1.ap, start=st, stop=sp_), [vh.t, p1.t], [O1.t])
                    P.op("pe", lambda e, p1=p1, st=st, sp_=sp_: e.matmul(Z1.ap, ones128, p1.ap, start=st, stop=sp_), [cbf.t, p1.t], [Z1.t])
                    P.op("pe", lambda e, p2=p2, kt=kt, st=st, sp_=sp_, vhv=vhv: e.matmul(O2.ap, vhv[:, kt, :], p2.ap, start=st, stop=sp_), [vh.t, p2.t], [O2.t])
                    P.op("pe", lambda e, p2=p2, st=st, sp_=sp_: e.matmul(Z2.ap, ones128, p2.ap, start=st, stop=sp_), [cbf.t, p2.t], [Z2.t])
                    if ins_here:
                        for f in ins_here:
                            f()
                        if kt + 1 < NKT:
                            nxt_sc = emit_scores(kt + 1)
                F1(h)
                if not SPLIT_F:
                    F2(h)
            if SPLIT_F:
                F2(7)
            def mem_A(hm):
                pbs = []
                for c in range(2):
                    wb = load_block(WAin, 32 + 2 * hm + c)
                    pbs.append(proj_fm(wb, 0, hT))
                qa, qb = PT[2 * (hm % 2)], PT[2 * (hm % 2) + 1]
                headnorm2(pbs, (MQG, MQG + 1), [(qa.ap, qa.t), (qb.ap, qb.t)], wk2)

            def mem_B(hm):
                qa, qb = PT[2 * (hm % 2)], PT[2 * (hm % 2) + 1]
                pts = pts2
                for mt in range(2):
                    pS = nextbank()
                    ms = slice(mt * 128, (mt + 1) * 128)
                    P.op("pe", lambda e, pS=pS, ms=ms: e.matmul(pS.ap, MkTv[:, 2 * hm, ms], qa.ap, start=True, stop=False), [MkT.t, qa.t], [pS.t])
                    P.op("pe", lambda e, pS=pS, ms=ms: e.matmul(pS.ap, MkTv[:, 2 * hm + 1, ms], qb.ap, start=False, stop=True), [MkT.t, qb.t], [pS.t])
                    P.op("act", lambda e, pS=pS, mt=mt: e.activation(out=pts[mt].ap, in_=pS.ap, func=AF.Exp, scale=1.0 / 16.0), [pS.t], [pts[mt].t])
                pZ = nextbank()
                P.op("pe", lambda e: e.matmul(pZ.ap, ones128, pts[0].ap, start=True, stop=False), [cbf.t, pts[0].t], [pZ.t])
                P.op("pe", lambda e: e.matmul(pZ.ap, ones128, pts[1].ap, start=False, stop=True), [cbf.t, pts[1].t], [pZ.t])
                rz = wk[3]
                act_recip(rz, pZ.ap, [pZ.t])
                for ec in range(2):
                    pO = nextbank()
                    es = slice((2 * hm + ec) * 128, (2 * hm + ec + 1) * 128)
                    P.op("pe", lambda e, pO=pO, es=es: e.matmul(pO.ap, MVv[:, 0, es], pts[0].ap, start=True, stop=False), [MV.t, pts[0].t], [pO.t])
                    P.op("pe", lambda e, pO=pO, es=es: e.matmul(pO.ap, MVv[:, 1, es], pts[1].ap, start=False, stop=True), [MV.t, pts[1].t], [pO.t])
                    P.op("dve", lambda e, pO=pO, ec=ec: e.tensor_tensor(out=brv[2][:, 2 * hm + ec, :], in0=pO.ap, in1=rz.ap, op=OP.mult),
                         [pO.t, rz.t], [brT[2].t])

            mem_A(0)
            for hm in range(4):
                if hm + 1 < 4:
                    mem_A(hm + 1)
                mem_B(hm)
            for m in range(8):
                for nb in range(3):
                    wg = load_block(WAin, 40 + 8 * nb + m)
                    pG = proj_fm(wg, 0, hT)
                    P.op("act", lambda e, pG=pG, nb=nb, m=m: e.activation(out=gate.ap, in_=pG.ap, func=AF.Sigmoid,
                                                                        bias=sm[:, B_GATE + 8 * nb + m:B_GATE + 8 * nb + m + 1]),
                         [pG.t, smalls.t], [gate.t])
                    wbb = load_block(WAbr, 8 * nb + m)
                    pB = proj_fm(wbb, 0, brT[nb])
                    if nb == 0:
                        P.op("dve", lambda e, pB=pB: e.tensor_tensor(out=macc.ap, in0=pB.ap, in1=gate.ap, op=OP.mult), [pB.t, gate.t], [macc.t])
                    else:
                        P.op("dve", lambda e, pB=pB: e.tensor_tensor(out=mtmp.ap, in0=pB.ap, in1=gate.ap, op=OP.mult), [pB.t, gate.t], [mtmp.t])
                        if nb == 1:
                            P.op("pool", lambda e: e.tensor_tensor(out=macc.ap, in0=macc.ap, in1=mtmp.ap, op=OP.add), [macc.t, mtmp.t], [macc.t])
                        else:
                            P.op("pool", lambda e, m=m: e.tensor_tensor(out=mgv[:, m, :], in0=macc.ap, in1=mtmp.ap, op=OP.add), [macc.t, mtmp.t], [merged.t])
            for hf in range(2):
                load_WB(WBp[hf], WBout, 8, hf * 512, 512)
            ytoks = [Tok() for _ in range(4)]

            def op_mm(s):
                x1 = x1s[s % 2]
                r0 = tok0 + it * 512 + s * 128
                P.op("sp", lambda e: e.dma_start(out=x1.ap, in_=x_d[r0:r0 + 128, :]), [], [x1.t], dma=True)
                for hf in range(2):
                    pb = nextbank()
                    wpv = WBp[hf].ap.rearrange("p (kc c) -> p kc c", kc=8)
                    for kc in range(8):
                        P.op("pe", lambda e, pb=pb, kc=kc, wpv=wpv: e.matmul(pb.ap, mgv[:, kc, s * 128:(s + 1) * 128], wpv[:, kc, :],
                                                                          start=(kc == 0), stop=(kc == 7)), [merged.t, WBp[hf].t], [pb.t])
                    P.op("dve", lambda e, pb=pb, hf=hf: e.tensor_tensor(out=x1.ap[:, hf * 512:(hf + 1) * 512], in0=pb.ap,
                                                                      in1=x1.ap[:, hf * 512:(hf + 1) * 512], op=OP.add), [pb.t, x1.t], [x1.t])

            def op_post(s):
                x1 = x1s[s % 2]
                r0 = tok0 + it * 512 + s * 128
                norm_transpose(x1, h2T, s, xn, ssb, xn)
                P.op("sp", lambda e: e.dma_start(out=y_d[r0:r0 + 128, :], in_=x1.ap), [x1.t], [ytoks[s]], dma=True)

            op_mm(0)
            for s in range(4):
                if s + 1 < 4:
                    op_mm(s + 1)
                op_post(s)
            for m in range(22):
                wg = load_block(WAgu, m)
                pg = proj_fm(wg, 0, h2T)
                wu = load_block(WAgu, 22 + m)
                pu = proj_fm(wu, 0, h2T)
                P.op("act", lambda e, pg=pg: e.activation(out=gate.ap, in_=pg.ap, func=AF.Silu), [pg.t], [gate.t])
                P.op("dve", lambda e, pu=pu, m=m: e.tensor_tensor(out=actv[:, m, :], in0=pu.ap, in1=gate.ap, op=OP.mult), [pu.t, gate.t], [actT.t])
            pi = 0
            for hf in range(2):
                groups = [(0, 8), (8, 16), (16, 22)]
                for (g0, g1) in groups:
                    wp = WBp[pi % 2]; pi += 1
                    nk = g1 - g0
                    P.op("sp", lambda e, wp=wp, g0=g0, g1=g1, nk=nk, hf=hf: e.dma_start(
                        out=wp.ap[:, 0:nk * 512].rearrange("p (kc c) -> p kc c", kc=nk),
                        in_=WBdown[g0 * 128:g1 * 128, hf * 512:(hf + 1) * 512].rearrange("(kc p) c -> p kc c", p=128)), [], [wp.t], dma=True)
                    if pi == 4 and it + 1 < NT:
                        tile_loads(it + 1)
                    wpv = wp.ap[:, 0:nk * 512].rearrange("p (kc c) -> p kc c", kc=nk)
                    for s in range(4):
                        acc = psb[4 + s] if hf == 0 else psb[s]
                        for kc in range(g0, g1):
                            P.op("pe", lambda e, acc=acc, kc=kc, g0=g0, s=s, wpv=wpv: e.matmul(acc.ap, actv[:, kc, s * 128:(s + 1) * 128], wpv[:, kc - g0, :],
                                                                                             start=(kc == 0), stop=(kc == 21)), [actT.t, wp.t], [acc.t])
            for hf in range(2):
                cs_ = slice(hf * 512, (hf + 1) * 512)

                def y_reload(s, hf=hf, cs_=cs_):
                    yb = ybuf[s % 2]
                    r0 = tok0 + it * 512 + s * 128
                    P.op("sp", lambda e: e.dma_start(out=yb.ap, in_=y_d[r0:r0 + 128, cs_]), [ytoks[s]], [yb.t], dma=True)

                def y_finish(s, hf=hf, cs_=cs_):
                    yb = ybuf[s % 2]
                    r0 = tok0 + it * 512 + s * 128
                    acc = psb[4 + s] if hf == 0 else psb[s]
                    P.op("dve", lambda e: e.tensor_tensor(out=yb.ap, in0=acc.ap, in1=yb.ap, op=OP.add), [acc.t, yb.t], [yb.t])
                    P.op("sp", lambda e: e.dma_start(out=y_d[r0:r0 + 128, cs_], in_=yb.ap), [yb.t], [ytoks[s]], dma=True)

                y_reload(0)
                y_reload(1)
                y_finish(0)
                y_reload(2)
                y_finish(1)
                y_reload(3)
                y_finish(2)
                y_finish(3)
        P.barrier()


    tok0 = 0
    for si, L in enumerate(seqLs):
        do_seq(si, L, tok0)
        tok0 += L

    P.prepare(nc)
    with nc.Block() as block:
        P.emit(nc, block)
    return nc


def _host_consts():
    bf = ml_dtypes.bfloat16
    cb = np.zeros((128, 512), np.float32)
    cb[:, 0:128] = np.eye(128)
    rot = np.zeros((128, 128), np.float32)
    for b in (0, 64):
        for d in range(32):
            rot[b + d + 32, b + d] = -1.0
            rot[b + d, b + d + 32] = 1.0
    cb[:, 128:256] = rot
    o64 = np.zeros((128, 128), np.float32)
    o64[0:64, 0:64] = 1.0
    o64[64:128, 64:128] = 1.0
    cb[:, 256:384] = o64
    cb[:, 384:512] = 1.0
    half = 32
    inv = (np.float32(10000.0) ** (-(np.arange(half, dtype=np.float32) / np.float32(half)))).astype(np.float32)
    ang = (np.arange(4096, dtype=np.float32)[:, None] * inv[None, :]).astype(np.float32)
    cos = np.cos(ang).astype(np.float32)
    sin = np.sin(ang).astype(np.float32)
    idx = np.arange(128) % 32
    ropec = np.ascontiguousarray(cos[:, idx].T)
    ropes = np.ascontiguousarray(sin[:, idx].T)
    return cb.astype(bf), ropec, ropes


def _host_params(inp):
    f = lambda k: np.asarray(inp[k], np.float32)
    sm = np.zeros((128, 320), np.float32)
    col = lambda v, n: np.ascontiguousarray(v.reshape(n, 128).T)
    sm[:, 0:8] = col(f("norm_mix_g")[0], 8)
    sm[:, 8:16] = col(f("ffn_norm_g")[0], 8)
    sm[:, 16:24] = col(f("mem_norm_g")[0], 8)
    sm[:, 24:48] = col(f("b_gate")[0], 24)
    sm[:, 48:56] = col(f("s5_d")[0], 8)
    p = np.arange(128)
    sm[:, 56] = f("diff_q_g")[0][p % 64]
    sm[:, 57] = f("diff_k_g")[0][p % 64]
    sm[:, 58] = f("diff_sub_g")[0]
    sm[:, 59:61] = col(f("mem_q_g")[0], 2)
    sm[:, 61:63] = col(f("mem_k_g")[0], 2)
    for i, k in enumerate(("diff_lq1", "diff_lk1", "diff_lq2", "diff_lk2")):
        sm[:, 63 + 64 * i:63 + 64 * (i + 1)] = f(k)[0][None, :]
    def ps(a):
        a = a.reshape((2, 32, 2, 64) + a.shape[3:])
        a = np.moveaxis(a, (2, 3), (0, 1))
        return np.ascontiguousarray(a.reshape((128, 64) + a.shape[4:]))
    lre = ps(f("s5_lam_re")[0])
    lim = ps(f("s5_lam_im")[0])
    ldt = ps(np.broadcast_to(f("s5_log_dt")[0][:, :, None], (2, 64, 64)).copy())
    bre = ps(f("s5_b_re")[0])
    bim = ps(f("s5_b_im")[0])
    cre = ps(np.swapaxes(f("s5_c_re")[0], 2, 3).copy())
    cim = ps(np.swapaxes(f("s5_c_im")[0], 2, 3).copy())
    Bst = np.stack([bre, bim], axis=2).reshape(128, 2048)
    Cst = np.stack([cre, cim], axis=2).reshape(128, 2048)
    s5p = np.concatenate([lre, lim, ldt, Bst, Cst], axis=1).astype(np.float32)
    return sm, np.ascontiguousarray(s5p)


_NC_CACHE = {}


def _get_nc(seqLs, **kw):
    key = (tuple(seqLs), tuple(sorted(kw.items())))
    if key not in _NC_CACHE:
        _NC_CACHE[key] = build(list(seqLs), **kw)
    return _NC_CACHE[key]


def _weights_map(inp):
    f = lambda k: np.ascontiguousarray(np.asarray(inp[k], np.float32))
    sm, s5p = _host_params(inp)
    cb, ropec, ropes = _host_consts()
    return {
        "w_in": f("w_in")[0], "w_glu": f("s5_w_glu")[0], "w_br": f("w_branch")[0].reshape(3072, 1024),
        "w_out": f("w_out")[0], "w_gu": f("w_gate_up")[0], "w_down": f("w_down")[0], "w_kv": f("w_mem_kv")[0],
        "smalls": sm, "s5p": s5p, "cbf": cb, "ropec": ropec, "ropes": ropes,
    }


def kernel(**inp):
    xp = np.asarray(inp["x_prompt"], np.float32)
    xs = np.asarray(inp["x_sample"], np.float32)
    mp = np.asarray(inp["mem_prompt"], np.float32)
    ms = np.asarray(inp["mem_sample"], np.float32)
    seqLs = [xp.shape[1]] * 2 + [xs.shape[1]] * 2
    nc = _get_nc(seqLs)
    wm = _weights_map(inp)
    in_maps = []
    for c in range(8):
        x = np.concatenate([xp[2 * c], xp[2 * c + 1], xs[2 * c], xs[2 * c + 1]], axis=0)
        mem = np.concatenate([mp[2 * c], mp[2 * c + 1], ms[2 * c], ms[2 * c + 1]], axis=0)
        m = dict(wm)
        m["x"] = np.ascontiguousarray(x)
        m["mem"] = np.ascontiguousarray(mem)
        in_maps.append(m)
    res = run_bass_kernel_spmd(nc, in_maps, core_ids=list(range(8)))
    yp = np.empty_like(xp)
    ys = np.empty_like(xs)
    Lp, Ls = xp.shape[1], xs.shape[1]
    for c in range(8):
        y = np.asarray(res.results[c]["y"], np.float32)
        yp[2 * c] = y[0:Lp]
        yp[2 * c + 1] = y[Lp:2 * Lp]
        ys[2 * c] = y[2 * Lp:2 * Lp + Ls]
        ys[2 * c + 1] = y[2 * Lp + Ls:2 * Lp + 2 * Ls]
    return (yp, ys)
```
